# Optimizing a Trainium2 kernel written in Bass

```python
import jax, jax.numpy as jnp
from jax import lax
import numpy as np

D_MODEL = 1024
BATCH = 4
SEQ = 4096
DEPTH = 4

CHUNK = 64
N_A_LAYERS = DEPTH // 2
N_B_LAYERS = DEPTH - N_A_LAYERS
GMLP_BLOCK = 128
GMLP_WIDTH = D_MODEL
GMLP_GROUPS = 8
GMLP_GROUP_DIM = GMLP_WIDTH // GMLP_GROUPS
ATTN_HEADS = 16
HEAD_DIM = D_MODEL // ATTN_HEADS
LEFT_CHUNKS = 8
BAND_CHUNKS = LEFT_CHUNKS + 1
BAND = BAND_CHUNKS * CHUNK
REL_CLIP = 256
FFN_HIDDEN = 4 * D_MODEL
RMS_EPS = 1e-6
LN_EPS = 1e-5
NEG_INF = -1e30

kernel_name = "yoco_gmlp_chunk_relbias_attn_trunk"


def rms_norm(x, g):
    xf = x.astype(jnp.float32)
    y = xf * lax.rsqrt(jnp.mean(xf * xf, axis=-1, keepdims=True) + RMS_EPS)
    return (y * g.astype(jnp.float32)).astype(x.dtype)


def layer_norm(x, g, b):
    xf = x.astype(jnp.float32)
    mu = jnp.mean(xf, axis=-1, keepdims=True)
    xc = xf - mu
    var = jnp.mean(xc * xc, axis=-1, keepdims=True)
    y = xc * lax.rsqrt(var + LN_EPS) * g.astype(jnp.float32) + b.astype(jnp.float32)
    return y.astype(x.dtype)


def gmlp_spatial_gating(h, w_in, ln_g, ln_b, w_s, b_s, w_out):
    bsz, seq, _ = h.shape
    nblk = seq // GMLP_BLOCK
    uv = jax.nn.gelu(h @ w_in)
    u, v = jnp.split(uv, 2, axis=-1)
    v = layer_norm(v, ln_g, ln_b)
    v = v.reshape(bsz, nblk, GMLP_BLOCK, GMLP_GROUPS, GMLP_GROUP_DIM)
    chunk_id = jnp.arange(GMLP_BLOCK) // CHUNK
    mask = chunk_id[:, None] >= chunk_id[None, :]
    w_masked = jnp.where(mask[None], w_s, jnp.zeros((), w_s.dtype))
    sv = jnp.einsum('gij,bnjgd->bnigd', w_masked, v)
    sv = sv + jnp.transpose(b_s)[None, None, :, :, None]
    gate = sv.reshape(bsz, seq, GMLP_WIDTH)
    return (u * gate) @ w_out


def shared_kv_band(h, kv_norm_g, w_k, w_v):
    bsz, seq, _ = h.shape
    nc = seq // CHUNK
    hn = rms_norm(h, kv_norm_g)
    k = (hn @ w_k).reshape(bsz, nc, CHUNK, ATTN_HEADS, HEAD_DIM)
    v = (hn @ w_v).reshape(bsz, nc, CHUNK, ATTN_HEADS, HEAD_DIM)

    def band(t):
        tp = jnp.pad(t, ((0, 0), (LEFT_CHUNKS, 0), (0, 0), (0, 0), (0, 0)))
        tb = jnp.stack([tp[:, o:o + nc] for o in range(BAND_CHUNKS)], axis=2)
        return tb.reshape(bsz, nc, BAND, ATTN_HEADS, HEAD_DIM)

    key_chunk = jnp.arange(nc)[:, None] + jnp.arange(BAND_CHUNKS)[None, :] - LEFT_CHUNKS
    valid = jnp.repeat(key_chunk >= 0, CHUNK, axis=1)
    return band(k), band(v), valid


def relative_bias(table):
    q_pos = jnp.arange(CHUNK)[:, None] + LEFT_CHUNKS * CHUNK
    k_pos = jnp.arange(BAND)[None, :]
    idx = jnp.clip(q_pos - k_pos, -REL_CLIP, REL_CLIP) + REL_CLIP
    return table[:, idx]


def chunk_band_attention(h, w_q, rel_table, w_o, k_band, v_band, valid):
    bsz, seq, _ = h.shape
    nc = seq // CHUNK
    q = (h @ w_q).reshape(bsz, nc, CHUNK, ATTN_HEADS, HEAD_DIM)
    s = jnp.einsum('bcqhd,bckhd->bchqk', q, k_band).astype(jnp.float32)
    s = s * (HEAD_DIM ** -0.5) + relative_bias(rel_table).astype(jnp.float32)[None, None]
    s = jnp.where(valid[None, :, None, None, :], s, NEG_INF)
    p = jax.nn.softmax(s, axis=-1).astype(h.dtype)
    o = jnp.einsum('bchqk,bckhd->bcqhd', p, v_band).reshape(bsz, seq, D_MODEL)
    return o @ w_o


def sq_relu_mlp(h, w_up, w_down):
    return jnp.square(jax.nn.relu(h @ w_up)) @ w_down


def setup_inputs(seed: int = 0) -> dict:
    key = jax.random.key(seed)
    ks = jax.random.split(key, 16)
    nrm = jax.random.normal
    D, W, G, H = D_MODEL, GMLP_WIDTH, GMLP_GROUPS, ATTN_HEADS
    return {
        "x": nrm(ks[0], (BATCH, SEQ, D), jnp.float32),
        "norm_g": 1.0 + 0.1 * nrm(ks[1], (DEPTH, 4, D), jnp.float32),
        "a_w_in": nrm(ks[2], (N_A_LAYERS, D, 2 * W), jnp.float32) * D ** -0.5,
        "a_ln_g": 1.0 + 0.1 * nrm(ks[3], (N_A_LAYERS, W), jnp.float32),
        "a_ln_b": 0.02 * nrm(ks[4], (N_A_LAYERS, W), jnp.float32),
        "a_w_s": nrm(ks[5], (N_A_LAYERS, G, GMLP_BLOCK, GMLP_BLOCK), jnp.float32) * GMLP_BLOCK ** -0.5,
        "a_b_s": 1.0 + 0.1 * nrm(ks[6], (N_A_LAYERS, G, GMLP_BLOCK), jnp.float32),
        "a_w_out": nrm(ks[7], (N_A_LAYERS, W, D), jnp.float32) * W ** -0.5,
        "kv_norm_g": 1.0 + 0.1 * nrm(ks[8], (D,), jnp.float32),
        "w_k": nrm(ks[9], (D, D), jnp.float32) * D ** -0.5,
        "w_v": nrm(ks[10], (D, D), jnp.float32) * D ** -0.5,
        "b_w_q": nrm(ks[11], (N_B_LAYERS, D, D), jnp.float32) * D ** -0.5,
        "b_rel_bias": 0.5 * nrm(ks[12], (N_B_LAYERS, H, 2 * REL_CLIP + 1), jnp.float32),
        "b_w_o": nrm(ks[13], (N_B_LAYERS, D, D), jnp.float32) * D ** -0.5,
        "w_up": nrm(ks[14], (DEPTH, D, FFN_HIDDEN), jnp.float32) * D ** -0.5,
        "w_down": nrm(ks[15], (DEPTH, FFN_HIDDEN, D), jnp.float32) * FFN_HIDDEN ** -0.5,
    }


def reference(x, norm_g, a_w_in, a_ln_g, a_ln_b, a_w_s, a_b_s, a_w_out,
              kv_norm_g, w_k, w_v, b_w_q, b_rel_bias, b_w_o, w_up, w_down):
    h = x
    k_band = v_band = valid = None
    for layer in range(DEPTH):
        g = norm_g[layer]
        if layer < N_A_LAYERS:
            m = gmlp_spatial_gating(rms_norm(h, g[0]), a_w_in[layer], a_ln_g[layer],
                                    a_ln_b[layer], a_w_s[layer], a_b_s[layer], a_w_out[layer])
        else:
            if layer == N_A_LAYERS:
                k_band, v_band, valid = shared_kv_band(h, kv_norm_g, w_k, w_v)
            j = layer - N_A_LAYERS
            m = chunk_band_attention(rms_norm(h, g[0]), b_w_q[j], b_rel_bias[j], b_w_o[j],
                                     k_band, v_band, valid)
        h = h + rms_norm(m, g[1])
        f = sq_relu_mlp(rms_norm(h, g[2]), w_up[layer], w_down[layer])
        h = h + rms_norm(f, g[3])
    return h
```

```python
from contextlib import ExitStack

import os
import numpy as np
import concourse.bass as bass
import concourse.mybir as mybir
from concourse.bass_utils import run_bass_kernel_spmd

F32 = mybir.dt.float32
BF16 = mybir.dt.bfloat16
I32 = mybir.dt.int32
AF = mybir.ActivationFunctionType
ALU = mybir.AluOpType

COMPUTE = ("pe", "act", "dve", "pool")
QUEUES = ("pe", "act", "dve", "pool", "sp")
NGRP = 5
RING = int(os.environ.get('K_RING', '5'))
NHS = int(os.environ.get('K_NHS', '4'))
INTERLEAVE = os.environ.get('K_INTER', '0') == '1'
POOL_ADD = os.environ.get('K_POOLADD', '1') == '1'
BIAS_BCAST = os.environ.get('K_BIASB', '1') == '1'
RMS_EPS = 1e-6
LN_EPS = 1e-5


class Chan:
    def __init__(self, name):
        self.name = name
        self.count = 0
        self.sem = None


class Sched:
    def __init__(self):
        self.ins = {e: [] for e in QUEUES}
        self.last_w = {}
        self.readers = {}
        self.seen = {e: {} for e in QUEUES}
        self.chans = []
        self.waited = {e: set() for e in COMPUTE}

    def chan(self, name):
        c = Chan(name)
        self.chans.append(c)
        return c

    def _collect(self, eng, reads, writes):
        need = {}

        def add(tok, is_raw):
            if tok[0] == "eng":
                _, e, idx = tok
                if e == eng and eng in ("pe", "sp"):
                    return
                key = ("eng", e)
                if need.get(key, -1) < idx:
                    need[key] = idx
            else:
                _, ch, cnt = tok
                key = ("dma", ch)
                if need.get(key, -1) < cnt:
                    need[key] = cnt

        for k in reads:
            t = self.last_w.get(k)
            if t is not None:
                add(t, True)
        for k in writes:
            t = self.last_w.get(k)
            if t is not None:
                add(t, True)
            for r in self.readers.get(k, ()):
                add(r, False)
        deps = []
        seen = self.seen[eng]
        for key, val in need.items():
            if seen.get(key, -1) >= val:
                continue
            seen[key] = val
            deps.append((key, val))
            if key[0] == "eng":
                self.waited[key[1]].add(val)
        return deps

    def _record(self, tok, reads, writes):
        for k in writes:
            self.last_w[k] = tok
            self.readers[k] = []
        for k in reads:
            lst = self.readers.setdefault(k, [])
            src = tok[:2]
            lst[:] = [r for r in lst if r[:2] != src]
            lst.append(tok)

    def op(self, eng, fn, reads=(), writes=()):
        idx = len(self.ins[eng])
        deps = self._collect(eng, reads, writes)
        self.ins[eng].append(dict(fn=fn, deps=deps, chan=None))
        self._record(("eng", eng, idx), reads, writes)
        return idx

    def dma(self, queue, fn, chan, reads=(), writes=()):
        deps = self._collect(queue, reads, writes)
        chan.count += 16
        self.ins[queue].append(dict(fn=fn, deps=deps, chan=chan))
        self._record(("dma", chan, chan.count), reads, writes)

    def wait_only(self, queue, reads=(), writes=()):
        deps = self._collect(queue, reads, writes)
        self.ins[queue].append(dict(fn=None, deps=deps, chan=None))

    def emit(self, nc):
        engobj = {"pe": "tensor", "act": "scalar", "dve": "vector", "pool": "gpsimd", "sp": "sync"}
        with ExitStack() as es:
            esem = {e: es.enter_context(nc.semaphore("prog_" + e)) for e in COMPUTE}
            for c in self.chans:
                c.sem = es.enter_context(nc.semaphore("ch_" + c.name))
            rank = {}
            for e in COMPUTE:
                rank[e] = {idx: i + 1 for i, idx in enumerate(sorted(self.waited[e]))}
            block = es.enter_context(nc.Block())

            def run(eng, name):
                mine = rank.get(name, {})
                for idx, r in enumerate(self.ins[name]):
                    for key, val in r["deps"]:
                        if key[0] == "eng":
                            eng.wait_ge(esem[key[1]], rank[key[1]][val])
                        else:
                            eng.wait_ge(key[1].sem, val)
                    if r["fn"] is None:
                        continue
                    ins = r["fn"](eng)
                    if r["chan"] is not None:
                        ins.then_inc(r["chan"].sem, 16)
                    elif idx in mine:
                        ins.then_inc(esem[name], 1)

            for name in QUEUES:
                if self.ins[name]:
                    getattr(block, engobj[name])(lambda eng, name=name: run(eng, name))
        return {e: len(self.ins[e]) for e in QUEUES}


def build_program(stop_after=None, ngrp=NGRP, noffn=False, dump=False):
    order = ["l0", "l1", "kv", "l2", "l3"]
    last = order.index(stop_after) if stop_after else len(order) - 1
    do = {s: (order.index(s) <= last) for s in order}

    nc = bass.Bass("TRN2", target_bir_lowering=False)
    S = Sched()
    marks = []

    def mark(label):
        marks.append((len(S.ins["pe"]), label))

    def dram(name, shape, kind="ExternalInput", dt=F32):
        return nc.dram_tensor(name, list(shape), dt, kind=kind).ap()

    x_d = dram("x", [2560, 1024])
    out_d = dram("out", [2048, 1024], kind="ExternalOutput")
    hv_d = dram("hv", [128, 1])
    gT_d = dram("gT", [128, 136])
    ng_d = dram("norm_g", [16, 1024])
    win_d = dram("a_w_in", [2, 1024, 2048])
    lng_d = dram("a_ln_g", [2, 1024])
    lnb_d = dram("a_ln_b", [2, 1024])
    wsT_d = dram("wsT", [2, 128, 1024])
    bs_d = dram("bs", [2, 1024])
    wout_d = dram("a_w_out", [2, 1024, 1024])
    wk_d = dram("w_k", [1024, 1024])
    wv_d = dram("w_v", [1024, 1024])
    wq_d = dram("b_w_q", [2, 1024, 1024])
    wo_d = dram("b_w_o", [2, 1024, 1024])
    rb_d = dram("rb", [2, 128, 8192])
    ch_d = dram("ch", [128, 32])
    wup_d = dram("w_up", [4, 1024, 4096])
    wdn_d = dram("w_down", [4, 4096, 1024])
    id_d = dram("ident", [128, 128])
    if dump:
        dbg = {
            "hnT": dram("dbg_hnT", [128, 4096], kind="ExternalOutput", dt=BF16),
            "uT": dram("dbg_uT", [128, 4096], kind="ExternalOutput", dt=BF16),
            "vF": dram("dbg_vF", [128, 4096], kind="ExternalOutput", dt=F32),
            "vn": dram("dbg_vn", [128, 4096], kind="ExternalOutput", dt=BF16),
            "gu": dram("dbg_gu", [128, 4096], kind="ExternalOutput", dt=BF16),
        }

    def sb(name, shape, dt):
        return nc.alloc_sbuf_tensor(name, list(shape), dt)

    hcur = sb("hcur", [128, 4, 1024], F32)
    KT = [sb(f"KT{i}", [128, 8, 512], BF16) for i in range(2)]
    VA = [sb(f"VA{i}", [128, 4, 16, 65], BF16) for i in range(2)]
    hnT = sb("hnT", [128, 8, 512], BF16)
    actT = sb("actT", [128, 8, 512], BF16)
    X = sb("X", [128, 16384], BF16)
    qT = [sb("qTA", [128, 8, 512], BF16), sb("qTB", [128, 8, 512], BF16)]
    hs = [sb(f"hs{i}", [128, 1024], BF16) for i in range(NHS)]
    pnt = [sb(f"pnt{i}", [128, 1024], F32) for i in range(2)]
    rtmp = [sb(f"rtmp{i}", [128, 512], BF16) for i in range(2)]
    PT = [sb(f"PT{i}", [128, 640], BF16) for i in range(3)]
    stmp = [sb(f"stmp{i}", [128, 512], F32) for i in range(2)]
    ob = [sb(f"ob{i}", [128, 1024], BF16) for i in range(2)]
    junk = ob[0]
    stat = sb("stat", [128, 256], F32)
    gT = sb("gTs", [128, 136], F32)
    gb = [sb(f"gb{i}", [128, 1024], F32) for i in range(2)]
    lnt = [sb(f"lnt{i}", [128, 1024], F32) for i in range(2)]
    wsT = sb("wsTs", [128, 2, 8, 128], BF16)
    bsrow = sb("bsrow", [1, 1024], BF16)
    ch = sb("chs", [128, 32], F32)
    ident = sb("identb", [128, 128], BF16)
    ones_row = sb("ones_row", [1, 128], BF16)
    hv = sb("hvs", [128, 1], F32)
    lnstat = [sb(f"lnstat{i}", [128, 12], F32) for i in range(2)]
    slots = [sb(f"slot{i}", [128, 4096], BF16) for i in range(RING)]

    hidT = X[:, :].rearrange("p (j t) -> p j t", j=32)
    uT = X[:, 0:4096].rearrange("p (c t) -> p c t", c=8)
    vF = X[:, 4096:12288].bitcast(F32).rearrange("p (t d) -> p t d", t=4)
    vn = X[:, 12288:16384].rearrange("p (t d) -> p t d", t=4)
    Rb = X[:, :].bitcast(F32).rearrange("p (h x) -> p h x", h=16)
    def XK(i):
        return ("X", i)

    RB_KEYS = [("X", i) for i in range(16)]

    PS = nc.alloc_psum_tensor("PS", [128, 4096], F32)
    pairs = [PS[:, i * 1024:(i + 1) * 1024] for i in range(4)]

    def bank(b):
        return pairs[b // 2][:, (b % 2) * 512:(b % 2) * 512 + 512]

    def bkey(b):
        return ("ps", b)

    psT = bank(7).bitcast(BF16).rearrange("p (c t) -> p c t", c=8)
    psT6 = bank(6).bitcast(BF16).rearrange("p (c t) -> p c t", c=8)
    psT0 = bank(0).bitcast(BF16).rearrange("p (c t) -> p c t", c=8)

    c_stage = [S.chan("stg0"), S.chan("stg1")]
    c_x = [S.chan(f"x{t}") for t in range(4)]
    c_out = [S.chan(f"o{t}") for t in range(4)]
    c_gb = [S.chan("gb0"), S.chan("gb1")]
    c_ln = [S.chan("ln0"), S.chan("ln1")]
    c_rb = S.chan("rb")
    c_bs = [S.chan("bs0"), S.chan("bs1")]
    c_dbg = S.chan("dbg")
    c_slot = [S.chan(f"slot{i}") for i in range(RING)]

    def panel(w2d, s):
        return ("panel", w2d.rearrange("(kc p) n -> p kc n", p=128)[:, :, s * 512:(s + 1) * 512])

    def dslab(w2d, s):
        return ("down", w2d.rearrange("(j p) n -> p j n", p=128)[:, 4 * s:4 * s + 4, :])

    plan = []
    for g in range(ngrp):
        for l in (0, 1):
            if not do[f"l{l}"]:
                continue
            plan += [panel(win_d[l], s) for s in (2, 3, 0, 1)]
            plan += [panel(wout_d[l], s) for s in range(2)]
            if not noffn:
                plan += [panel(wup_d[l], s) for s in range(8)]
                plan += [dslab(wdn_d[l], s) for s in range(8)]
        if do["kv"]:
            plan += [panel(wk_d, s) for s in range(2)]
            plan += [panel(wv_d, s) for s in range(2)]
        if g >= 1:
            for l in (2, 3):
                if not do[f"l{l}"]:
                    continue
                plan += [panel(wq_d[l - 2], s) for s in range(2)]
                plan += [panel(wo_d[l - 2], s) for s in range(2)]
                plan += [panel(wup_d[l], s) for s in range(8)]
                plan += [dslab(wdn_d[l], s) for s in range(8)]

    ring_state = dict(issued=0, taken=0)

    def ring_issue():
        n = ring_state["issued"]
        if n >= len(plan):
            return
        ring_state["issued"] = n + 1
        kind, src = plan[n]
        si = n % RING
        if kind == "panel":
            dst = slots[si][:, :].rearrange("p (a b) -> p a b", a=8)
        else:
            dst = slots[si][:, :].rearrange("p (a b) -> p a b", a=4)
        S.dma("pool", lambda e, dst=dst, src=src: e.dma_start(out=dst, in_=src), c_slot[si],
              writes=[("w", si)])

    def ring_next(kind):
        n = ring_state["taken"]
        ring_state["taken"] = n + 1
        assert plan[n][0] == kind, (n, plan[n][0], kind)
        si = n % RING
        if kind == "panel":
            v = slots[si][:, :].rearrange("p (a b) -> p a b", a=8)
        else:
            v = slots[si][:, :].rearrange("p (a b) -> p a b", a=4)
        return v, ("w", si)

    def ring_release():
        ring_issue()

    stat_i = [0]

    def newcols():
        i = stat_i[0] % 8
        stat_i[0] += 1
        return stat[:, i * 32:(i + 1) * 32], ("stat", i)

    def emit_rsqrt(xin, xin_key, pre_scale, eps, out, out_key, n=1):
        c, k = newcols()
        a, y, p, q = c[:, 0:n], c[:, 8:8 + n], c[:, 12:12 + n], c[:, 16:16 + n]
        S.op("dve", lambda e: e.tensor_scalar(out=a, in0=xin, scalar1=pre_scale, scalar2=eps,
                                              op0=ALU.mult, op1=ALU.add), reads=[xin_key], writes=[k])
        S.op("dve", lambda e: e.tensor_scalar(out=y.bitcast(I32), in0=a.bitcast(I32), scalar1=-0.5,
                                              scalar2=1597463007.0, op0=ALU.mult, op1=ALU.add),
             reads=[k], writes=[k])
        for it in range(2):
            S.op("dve", lambda e: e.tensor_tensor(out=p, in0=y, in1=y, op=ALU.mult), reads=[k], writes=[k])
            S.op("dve", lambda e: e.scalar_tensor_tensor(out=q, in0=p, scalar=-0.5, in1=a, op0=ALU.mult,
                                                         op1=ALU.mult), reads=[k], writes=[k])
            dst, dk = (y, k) if it == 0 else (out, out_key)
            S.op("dve", lambda e, dst=dst: e.scalar_tensor_tensor(out=dst, in0=q, scalar=1.5, in1=y, op0=ALU.add,
                                                                  op1=ALU.mult), reads=[k], writes=[dk])

    rot = dict(fm=0, tm=0, pair=0, hs=0, pnt=0, rt=0, lns=0, tb=0)

    def nxt(name, n):
        v = rot[name] % n
        rot[name] += 1
        return v

    def fm_bank():
        return 4 + nxt("fm", 3)

    def tm_bank():
        return nxt("tm", 4)

    def tm_pair():
        return nxt("pair", 2)

    def tile_cols(t):
        return slice(t * 128, (t + 1) * 128)

    def norm_part2(t, hidx, gcol, extra_gcol=None):
        tb = 7 - nxt("tb", 2)
        pT = psT if tb == 7 else psT6
        for fc in range(8):
            S.op("pe", lambda e, fc=fc: e.transpose(out=pT[:, fc, :], in_=hs[hidx][:, fc * 128:(fc + 1) * 128],
                                                    identity=ident[:, :]),
                 reads=[("hs", hidx), "ident"], writes=[bkey(tb)])
        gbc = gT[:, gcol:gcol + 8].unsqueeze(2).to_broadcast([128, 8, 128])
        S.op("dve", lambda e: e.tensor_tensor(out=hnT[:, :, tile_cols(t)], in0=pT, in1=gbc, op=ALU.mult),
             reads=[bkey(tb), "gT"], writes=[("hnT", t)])
        if extra_gcol is not None:
            gbc2 = gT[:, extra_gcol:extra_gcol + 8].unsqueeze(2).to_broadcast([128, 8, 128])
            S.op("dve", lambda e: e.tensor_tensor(out=actT[:, :, tile_cols(t)], in0=pT, in1=gbc2, op=ALU.mult),
                 reads=[bkey(tb), "gT"], writes=[("actT", t)])

    def norm_batch1(tl, gcol=None, extra_gcol=None):
        n = len(tl)
        c, k = newcols()
        for i, t in enumerate(tl):
            S.op("act", lambda e, i=i, t=t: e.activation(out=junk[:, :], in_=hcur[:, t, :], func=AF.Square,
                                                         accum_out=c[:, i:i + 1]),
                 reads=[("h", t)], writes=[("ob", 0), k])
        emit_rsqrt(c[:, 0:n], k, 1.0 / 1024, RMS_EPS, c[:, 4:4 + n], k, n)
        res = []
        for i, t in enumerate(tl):
            hidx = nxt("hs", NHS)
            S.op("act", lambda e, i=i, t=t, hidx=hidx: e.activation(out=hs[hidx][:, :], in_=hcur[:, t, :], func=AF.Copy,
                                                                    scale=c[:, 4 + i:5 + i]),
                 reads=[("h", t), k], writes=[("hs", hidx)])
            if gcol is not None and res and INTERLEAVE:
                norm_part2(res[-1][0], res[-1][1], gcol, extra_gcol)
            res.append((t, hidx))
        if gcol is not None:
            if INTERLEAVE:
                norm_part2(res[-1][0], res[-1][1], gcol, extra_gcol)
            else:
                for t, hidx in res:
                    norm_part2(t, hidx, gcol, extra_gcol)
        return res

    def norm_batch(tl, gcol, extra_gcol=None):
        norm_batch1(tl, gcol, extra_gcol)

    HN_ALL = [("hnT", t) for t in range(4)]
    ACT_ALL = [("actT", t) for t in range(4)]

    def postnorm_batch(tl, pis, gtile, gkey):
        n = len(tl)
        c, k = newcols()
        for i, (t, pi) in enumerate(zip(tl, pis)):
            S.op("act", lambda e, i=i, pi=pi: e.activation(out=junk[:, :], in_=pairs[pi][:, :], func=AF.Square,
                                                           accum_out=c[:, i:i + 1]),
                 reads=[bkey(2 * pi), bkey(2 * pi + 1)], writes=[("ob", 0), k])
        emit_rsqrt(c[:, 0:n], k, 1.0 / 1024, RMS_EPS, c[:, 4:4 + n], k, n)
        for i, (t, pi) in enumerate(zip(tl, pis)):
            pi2 = nxt("pnt", 2)
            S.op("dve", lambda e, i=i, pi=pi, pi2=pi2: e.scalar_tensor_tensor(
                out=pnt[pi2][:, :], in0=pairs[pi][:, :], scalar=c[:, 4 + i:5 + i], in1=gtile[:, :],
                op0=ALU.mult, op1=ALU.mult),
                reads=[bkey(2 * pi), bkey(2 * pi + 1), k, gkey], writes=[("pnt", pi2)])
            add_eng = "pool" if (POOL_ADD and i % 2 == 0) else "dve"
            S.op(add_eng, lambda e, t=t, pi2=pi2: e.tensor_tensor(out=hcur[:, t, :], in0=hcur[:, t, :], in1=pnt[pi2][:, :],
                                                                   op=ALU.add),
                 reads=[("h", t), ("pnt", pi2)], writes=[("h", t)])

    def load_bcast(dst, key, chan, row_ap):
        S.dma("sp", lambda e: e.dma_start(out=dst[:, :], in_=row_ap.partition_broadcast(128)), chan,
              writes=[key])

    def proj_tm_post(src_keys_fn, srcT, gtile, gkey, next_gcol):
        mark("proj")
        slabA, kA = ring_next("panel")
        slabB, kB = ring_next("panel")
        ppairs = [0, 1, 2, 0]
        pend = []
        for bi, tl in enumerate(((0, 1), (2, 3))):
            pis = []
            for t in tl:
                pi = ppairs[t]
                pis.append(pi)
                for half, (slab, ks) in enumerate(((slabA, kA), (slabB, kB))):
                    for kc in range(8):
                        S.op("pe", lambda e, kc=kc, slab=slab, half=half, pi=pi, t=t: e.matmul(
                            pairs[pi][:, half * 512:(half + 1) * 512], lhsT=srcT[:, kc, tile_cols(t)],
                            rhs=slab[:, kc, :], start=(kc == 0), stop=(kc == 7)),
                            reads=[ks, src_keys_fn(t)], writes=[bkey(2 * pi + half)])
            for f in pend:
                f()
            pend = []
            postnorm_batch(list(tl), pis, gtile, gkey)
            if next_gcol is not None:
                for t, hidx in norm_batch1(list(tl)):
                    pend.append(lambda t=t, hidx=hidx: norm_part2(t, hidx, next_gcol))
        ring_release()
        ring_release()
        for f in pend:
            f()

    def ffn(l, next_gcol, extra_gcol=None):
        if noffn:
            return
        mark("up")
        for s in range(8):
            slab, ks = ring_next("panel")
            for j4 in range(4):
                j = s * 4 + j4
                b = fm_bank()
                for kc in range(8):
                    S.op("pe", lambda e, kc=kc, slab=slab, j4=j4, b=b: e.matmul(
                        bank(b), lhsT=slab[:, kc, j4 * 128:(j4 + 1) * 128], rhs=hnT[:, kc, :],
                        start=(kc == 0), stop=(kc == 7)),
                        reads=[ks] + HN_ALL, writes=[bkey(b)])
                ri = nxt("rt", 2)
                S.op("act", lambda e, b=b, ri=ri: e.activation(out=rtmp[ri][:, :], in_=bank(b), func=AF.Relu),
                     reads=[bkey(b)], writes=[("rt", ri)])
                S.op("dve", lambda e, j=j, ri=ri: e.tensor_tensor(out=hidT[:, j, :], in0=rtmp[ri][:, :],
                                                                   in1=rtmp[ri][:, :], op=ALU.mult),
                     reads=[("rt", ri)], writes=[XK(j // 2)])
            ring_release()
        mark("down")
        dpair = [3, 2, 1, 0]
        for s in range(8):
            slab, ks = ring_next("down")
            for t in range(4):
                pi = dpair[t]
                for half in range(2):
                    for jj in range(4):
                        j = 4 * s + jj
                        S.op("pe", lambda e, slab=slab, jj=jj, j=j, half=half, pi=pi, t=t, s=s: e.matmul(
                            pairs[pi][:, half * 512:(half + 1) * 512], lhsT=hidT[:, j, tile_cols(t)],
                            rhs=slab[:, jj, half * 512:(half + 1) * 512],
                            start=(s == 0 and jj == 0), stop=(s == 7 and jj == 3)),
                            reads=[ks, XK(j // 2)], writes=[bkey(2 * pi + half)])
            ring_release()
        mark("post")
        postnorm_batch([0, 1, 2, 3], dpair, gb[1], "gb1")
        if next_gcol is not None:
            norm_batch([0, 1, 2, 3], next_gcol, extra_gcol)

    def dump_buf(name, src_ap, keys):
        S.dma("sp", lambda e: e.dma_start(out=dbg[name][:, :], in_=src_ap), c_dbg, reads=keys)

    def a_layer(g, l, next_gcol, extra_gcol=None):
        dd = dump and g == 1 and l == 0
        mark(f"A{l} g{g} v")
        if dd:
            dump_buf("hnT", hnT[:, :, :].rearrange("p a b -> p (a b)"), HN_ALL)
        load_bcast(gb[0], "gb0", c_gb[0], ng_d[l * 4 + 1:l * 4 + 2, :])
        load_bcast(gb[1], "gb1", c_gb[1], ng_d[l * 4 + 3:l * 4 + 4, :])
        load_bcast(lnt[0], "ln0", c_ln[0], lng_d[l:l + 1, :])
        load_bcast(lnt[1], "ln1", c_ln[1], lnb_d[l:l + 1, :])
        for hf in range(2):
            S.dma("sp", lambda e, hf=hf: e.dma_start(out=stmp[hf][0:1, :], in_=bs_d[l:l + 1, hf * 512:(hf + 1) * 512]),
                  c_bs[hf], writes=[("stmp", hf)])
            S.op("act", lambda e, hf=hf: e.activation(out=bsrow[0:1, hf * 512:(hf + 1) * 512], in_=stmp[hf][0:1, :],
                                                      func=AF.Copy),
                 reads=[("stmp", hf)], writes=["bsrow"])
        for s in range(2):
            slab, ks = ring_next("panel")
            for t in range(4):
                b = tm_bank()
                for kc in range(8):
                    S.op("pe", lambda e, kc=kc, slab=slab, t=t, b=b: e.matmul(
                        bank(b), lhsT=hnT[:, kc, tile_cols(t)], rhs=slab[:, kc, :],
                        start=(kc == 0), stop=(kc == 7)),
                        reads=[ks, ("hnT", t)], writes=[bkey(b)])
                S.op("act", lambda e, b=b, t=t, s=s: e.activation(out=vF[:, t, s * 512:(s + 1) * 512], in_=bank(b),
                                                                func=AF.Gelu_apprx_tanh),
                     reads=[bkey(b)], writes=[XK(4 + 2 * t + s)])
            ring_release()
        c, k = newcols()
        mv = c[:, 0:8].rearrange("p (t two) -> p t two", two=2)
        for t in range(4):
            li = nxt("lns", 2)
            S.op("dve", lambda e, t=t, li=li: e.bn_stats(out=lnstat[li][:, 0:6], in_=vF[:, t, 0:512]),
                 reads=[XK(4 + 2 * t)], writes=[("lnstat", li)])
            S.op("dve", lambda e, t=t, li=li: e.bn_stats(out=lnstat[li][:, 6:12], in_=vF[:, t, 512:1024]),
                 reads=[XK(5 + 2 * t)], writes=[("lnstat", li)])
            S.op("dve", lambda e, t=t, li=li: e.bn_aggr(out=c[:, 2 * t:2 * t + 2], in_=lnstat[li][:, 0:12]),
                 reads=[("lnstat", li)], writes=[k])
        emit_rsqrt(mv[:, :, 1], k, 1.0, LN_EPS, c[:, 8:12], k, 4)
        S.op("dve", lambda e: e.scalar_tensor_tensor(out=c[:, 12:16], in0=mv[:, :, 0], scalar=-1.0, in1=c[:, 8:12],
                                                     op0=ALU.mult, op1=ALU.mult),
             reads=[k], writes=[k])
        def ln_part_b(t):
            pi2 = nxt("pnt", 2)
            S.op("act", lambda e, t=t, pi2=pi2: e.activation(out=pnt[pi2][:, :], in_=vF[:, t, :], func=AF.Identity,
                                                              scale=c[:, 8 + t:9 + t], bias=c[:, 12 + t:13 + t]),
                 reads=[XK(4 + 2 * t), XK(5 + 2 * t), k], writes=[("pnt", pi2)])
            S.op("dve", lambda e, pi2=pi2: e.tensor_tensor(out=pnt[pi2][:, :], in0=pnt[pi2][:, :], in1=lnt[0][:, :],
                                                            op=ALU.mult),
                 reads=[("pnt", pi2), "ln0"], writes=[("pnt", pi2)])
            S.op("dve", lambda e, pi2=pi2, t=t: e.tensor_tensor(out=vn[:, t, :], in0=pnt[pi2][:, :], in1=lnt[1][:, :],
                                                                 op=ALU.add),
                 reads=[("pnt", pi2), "ln1"], writes=[XK(12 + t)])
        mark("u")
        for s in range(2):
            slab, ks = ring_next("panel")
            for u4 in range(4):
                uc = s * 4 + u4
                b = fm_bank()
                for kc in range(8):
                    S.op("pe", lambda e, kc=kc, slab=slab, u4=u4, b=b: e.matmul(
                        bank(b), lhsT=slab[:, kc, u4 * 128:(u4 + 1) * 128], rhs=hnT[:, kc, :],
                        start=(kc == 0), stop=(kc == 7)),
                        reads=[ks] + HN_ALL, writes=[bkey(b)])
                S.op("act", lambda e, b=b, uc=uc: e.activation(out=uT[:, uc, :], in_=bank(b), func=AF.Gelu_apprx_tanh),
                     reads=[bkey(b)], writes=[XK(uc // 2)])
                if s == 1:
                    ln_part_b(u4)
            ring_release()
        if dd:
            dump_buf("uT", X[:, 0:4096], [XK(i) for i in range(4)])
            dump_buf("vF", X[:, 4096:12288].bitcast(F32), [XK(i) for i in range(4, 12)])
        if dd:
            dump_buf("vn", X[:, 12288:16384], [XK(i) for i in range(12, 16)])
        mark("gate")
        slabA, kA = ring_next("panel")
        slabB, kB = ring_next("panel")
        ffn_gcol = (l * 4 + 2) * 8
        ppairs = [0, 1, 2, 0]
        gview = pairs[3].rearrange("p (g i) -> p g i", g=8)
        pend = []
        for tl in ((0, 1), (2, 3)):
            pis = []
            for t in tl:
                for half in range(2):
                    bk = 6 + half
                    S.op("pe", lambda e, half=half, bk=bk: e.matmul(
                        bank(bk), lhsT=ones_row[0:1, :], rhs=bsrow[0:1, half * 512:(half + 1) * 512],
                        start=True, stop=False),
                        reads=["ones", "bsrow"], writes=[bkey(bk)])
                    for g4 in range(4):
                        gi = half * 4 + g4
                        S.op("pe", lambda e, gi=gi, g4=g4, t=t, bk=bk: e.matmul(
                            bank(bk)[:, g4 * 128:(g4 + 1) * 128], lhsT=vn[:, t, gi * 128:(gi + 1) * 128],
                            rhs=wsT[:, l, gi, :], start=False, stop=(g4 == 3)),
                            reads=[XK(12 + t), "wsT"], writes=[bkey(bk)])
                S.op("dve", lambda e, t=t: e.tensor_tensor(out=actT[:, :, tile_cols(t)], in0=gview,
                                                           in1=uT[:, :, tile_cols(t)], op=ALU.mult),
                     reads=[bkey(6), bkey(7)] + [XK(i) for i in range(4)], writes=[("actT", t)])
                pi = ppairs[t]
                pis.append(pi)
                for half, (slab, ks) in enumerate(((slabA, kA), (slabB, kB))):
                    for kc in range(8):
                        S.op("pe", lambda e, kc=kc, slab=slab, half=half, pi=pi, t=t: e.matmul(
                            pairs[pi][:, half * 512:(half + 1) * 512], lhsT=actT[:, kc, tile_cols(t)],
                            rhs=slab[:, kc, :], start=(kc == 0), stop=(kc == 7)),
                            reads=[ks, ("actT", t)], writes=[bkey(2 * pi + half)])
            for f in pend:
                f()
            pend = []
            postnorm_batch(list(tl), pis, gb[0], "gb0")
            for t, hidx in norm_batch1(list(tl)):
                pend.append(lambda t=t, hidx=hidx: norm_part2(t, hidx, ffn_gcol))
        ring_release()
        ring_release()
        for f in pend:
            f()
        ffn(l, next_gcol, extra_gcol)

    def kv_stage(g):
        mark(f"KV g{g}")
        cur = g % 2
        for s in range(2):
            slab, ks = ring_next("panel")
            for f4 in range(4):
                fc = s * 4 + f4
                b = fm_bank()
                for kc in range(8):
                    S.op("pe", lambda e, kc=kc, slab=slab, f4=f4, b=b: e.matmul(
                        bank(b), lhsT=slab[:, kc, f4 * 128:(f4 + 1) * 128], rhs=hnT[:, kc, :],
                        start=(kc == 0), stop=(kc == 7)),
                        reads=[ks] + HN_ALL, writes=[bkey(b)])
                S.op("act", lambda e, b=b, fc=fc: e.activation(out=KT[cur][:, fc, :], in_=bank(b), func=AF.Copy),
                     reads=[bkey(b)], writes=[("KT", cur)])
            ring_release()
        for s in range(2):
            slab, ks = ring_next("panel")
            for t in range(4):
                b = tm_bank()
                for kc in range(8):
                    S.op("pe", lambda e, kc=kc, slab=slab, t=t, b=b: e.matmul(
                        bank(b), lhsT=hnT[:, kc, tile_cols(t)], rhs=slab[:, kc, :],
                        start=(kc == 0), stop=(kc == 7)),
                        reads=[ks, ("hnT", t)], writes=[bkey(b)])
                src = bank(b).rearrange("p (h d) -> p h d", h=8)
                dst = VA[cur][:, t, s * 8:(s + 1) * 8, 0:64]
                if g == 0:
                    S.op("dve", lambda e, src=src, dst=dst: e.tensor_scalar(out=dst, in0=src, scalar1=hv[:, 0:1],
                                                                            scalar2=None, op0=ALU.mult),
                         reads=[bkey(b), "hv"], writes=[("VA", cur, t)])
                else:
                    S.op("act", lambda e, src=src, dst=dst: e.activation(out=dst, in_=src, func=AF.Copy),
                         reads=[bkey(b)], writes=[("VA", cur, t)])
            ring_release()
        for t in range(4):
            dst = VA[cur][:, t, :, 64:65]
            if g == 0:
                S.op("dve", lambda e, dst=dst: e.tensor_copy(out=dst, in_=hv[:, 0:1].unsqueeze(1).to_broadcast([128, 16, 1])),
                     reads=["hv"], writes=[("VA", cur, t)])
            else:
                S.op("dve", lambda e, dst=dst: e.memset(dst, 1.0), writes=[("VA", cur, t)])

    def b_layer(g, l, next_gcol, qsrc=None):
        mark(f"B{l} g{g} q")
        j = l - 2
        prev, cur = (g - 1) % 2, g % 2
        load_bcast(gb[0], "gb0", c_gb[0], ng_d[l * 4 + 1:l * 4 + 2, :])
        load_bcast(gb[1], "gb1", c_gb[1], ng_d[l * 4 + 3:l * 4 + 4, :])
        S.dma("sp", lambda e: e.dma_start(out=X[:, :].bitcast(F32), in_=rb_d[j]), c_rb,
              writes=RB_KEYS)
        qs, qkeys = (actT, ACT_ALL) if qsrc == "actT" else (hnT, HN_ALL)
        for s in range(2):
            slab, ks = ring_next("panel")
            for f4 in range(4):
                fc = s * 4 + f4
                b = fm_bank()
                for kc in range(8):
                    S.op("pe", lambda e, kc=kc, slab=slab, f4=f4, b=b: e.matmul(
                        bank(b), lhsT=slab[:, kc, f4 * 128:(f4 + 1) * 128], rhs=qs[:, kc, :],
                        start=(kc == 0), stop=(kc == 7)),
                        reads=[ks] + qkeys, writes=[bkey(b)])
                S.op("act", lambda e, b=b, fc=fc: e.activation(out=qT[0][0:64, fc, :], in_=bank(b)[0:64, :],
                                                                func=AF.Copy, scale=0.125),
                     reads=[bkey(b)], writes=["qTA"])
                S.op("dve", lambda e, b=b, fc=fc: e.tensor_scalar(out=qT[1][64:128, fc, :], in0=bank(b)[64:128, :],
                                                                   scalar1=0.125, scalar2=None, op0=ALU.mult),
                     reads=[bkey(b)], writes=["qTB"])
            ring_release()

        mark("attn")
        items = [(ti, h) for ti in range(4) for h in range(16)]
        ssets = [(1, 2), (3, 4), (5, 6)]
        PCOL = {0: 0, 2: 1, 3: 2, 4: 3, 1: 4}
        PASSES = [(0, 7), (7, 7), (14, 2)]

        def obank(ti, ps_i):
            return (0, 7)[(ti + ps_i) % 2]

        def opass(h):
            return 0 if h < 7 else (1 if h < 14 else 2)

        def win(ti, kb):
            w = ti + kb
            return (prev if w < 4 else cur), w % 4

        def emit_S(n):
            ti, h = items[n]
            fc = h // 2
            sa, sbk = ssets[n % 3]
            q = qT[h % 2][:, fc, tile_cols(ti)]
            for kb in range(5):
                buf, wt = win(ti, kb)
                if kb == 1:
                    o = bank(sbk)[:, 0:128]
                    ok = bkey(sbk)
                else:
                    o = bank(sa)[:, PCOL[kb] * 128:(PCOL[kb] + 1) * 128]
                    ok = bkey(sa)
                S.op("pe", lambda e, o=o, buf=buf, fc=fc, wt=wt, q=q: e.matmul(
                    o, lhsT=KT[buf][:, fc, tile_cols(wt)], rhs=q, start=True, stop=True),
                    reads=[("KT", buf), "qTA" if h % 2 == 0 else "qTB"], writes=[ok])

        def emit_soft_pv(n):
            ti, h = items[n]
            sa, sbk = ssets[n % 3]
            p = n % 3
            si = n % 2
            S.op("dve", lambda e: e.tensor_tensor(out=stmp[si][:, :], in0=bank(sa), in1=Rb[:, h, :], op=ALU.add),
                 reads=[bkey(sa)] + RB_KEYS, writes=[("stmp", si)])
            S.op("act", lambda e: e.activation(out=PT[p][:, 512:640], in_=bank(sbk)[:, 0:128], func=AF.Exp,
                                               bias=ch[:, j * 16 + h:j * 16 + h + 1]),
                 reads=[bkey(sbk), "ch"], writes=[("PT", p)])
            S.op("act", lambda e: e.activation(out=PT[p][:, 0:512], in_=stmp[si][:, :], func=AF.Exp),
                 reads=[("stmp", si)], writes=[("PT", p)])
            h0, nh = PASSES[opass(h)]
            obk = obank(ti, opass(h))
            oo = (h - h0) * 65
            for kb in range(5):
                buf, wt = win(ti, kb)
                pc = PCOL[kb]
                S.op("pe", lambda e, kb=kb, buf=buf, wt=wt, pc=pc: e.matmul(
                    bank(obk)[:, oo:oo + 65], lhsT=PT[p][:, pc * 128:(pc + 1) * 128], rhs=VA[buf][:, wt, h, :],
                    start=(kb == 0), stop=(kb == 4)),
                    reads=[("PT", p), ("VA", buf, wt)], writes=[bkey(obk)])

        def emit_norm_pass(ti, oi, ps_i):
            h0, nh = PASSES[ps_i]
            obk = obank(ti, ps_i)
            c, k = newcols()
            Ob = bank(obk)[:, 0:nh * 65].rearrange("p (h d) -> p h d", h=nh)
            S.op("dve", lambda e: e.reciprocal(out=c[:, 0:nh].unsqueeze(2), in_=Ob[:, :, 64:65]),
                 reads=[bkey(obk)], writes=[k])
            dst = ob[oi][:, h0 * 64:(h0 + nh) * 64].rearrange("p (h d) -> p h d", h=nh)
            S.op("dve", lambda e: e.tensor_tensor(
                out=dst, in0=Ob[:, :, 0:64], in1=c[:, 0:nh].unsqueeze(2).to_broadcast([128, nh, 64]), op=ALU.mult),
                reads=[bkey(obk), k], writes=[("ob", oi)])

        def emit_oT(ti, oi):
            tb = 0 if ti % 2 == 0 else 7
            pT = psT0 if tb == 0 else psT
            for fc in range(8):
                S.op("pe", lambda e, fc=fc: e.transpose(out=pT[:, fc, :], in_=ob[oi][:, fc * 128:(fc + 1) * 128],
                                                        identity=ident[:, :]),
                     reads=[("ob", oi), "ident"], writes=[bkey(tb)])
            S.op("act", lambda e: e.activation(out=actT[:, :, tile_cols(ti)], in_=pT, func=AF.Copy),
                 reads=[bkey(tb)], writes=[("actT", ti)])

        emit_S(0)
        emit_S(1)
        pend_T = None
        pend_norm = []
        for n in range(len(items)):
            ti, h = items[n]
            if n + 2 < len(items):
                emit_S(n + 2)
            emit_soft_pv(n)
            while pend_norm and pend_norm[0][0] <= n:
                pend_norm.pop(0)[1]()
            if pend_T is not None and h == 3:
                pend_T()
                pend_T = None
            if h in (6, 13, 15):
                pend_norm.append((n + 2, lambda ti=ti, ps_i=opass(h): emit_norm_pass(ti, ti % 2, ps_i)))
            if h == 15:
                pend_T = (lambda ti=ti, oi=ti % 2: emit_oT(ti, oi))
        for _, f in pend_norm:
            f()
        pend_T()
        proj_tm_post(lambda t: ("actT", t), actT, gb[0], "gb0", (l * 4 + 2) * 8)
        ffn(l, next_gcol)

    S.dma("sp", lambda e: e.dma_start(out=gT[:, :], in_=gT_d[:, :]), S.chan("c_gT"), writes=["gT"])
    S.dma("sp", lambda e: e.dma_start(out=ch[:, :], in_=ch_d[:, :]), S.chan("c_ch"), writes=["ch"])
    S.dma("sp", lambda e: e.dma_start(out=hv[:, :], in_=hv_d[:, :]), S.chan("c_hv"), writes=["hv"])
    S.dma("sp", lambda e: e.dma_start(out=pnt[0][:, 0:128], in_=id_d[:, :]), c_stage[0], writes=[("pnt", 0)])
    for i in range(min(RING, len(plan))):
        ring_issue()
    S.op("act", lambda e: e.activation(out=ident[:, :], in_=pnt[0][:, 0:128], func=AF.Copy),
         reads=[("pnt", 0)], writes=["ident"])
    S.op("dve", lambda e: e.memset(ones_row[:, :], 1.0), writes=["ones"])
    S.op("dve", lambda e: e.memset(qT[0][:, :, :], 0.0), writes=["qTA"])
    S.op("dve", lambda e: e.memset(qT[1][:, :, :], 0.0), writes=["qTB"])
    for l in range(2):
        S.dma("sp", lambda e, l=l: e.dma_start(out=pnt[l][:, :], in_=wsT_d[l]), c_stage[l], writes=[("pnt", l)])
        stg = pnt[l][:, :].rearrange("p (g i) -> p g i", g=8)
        S.op("dve", lambda e, stg=stg: e.memset(stg[64:128, :, 0:64], 0.0), reads=[("pnt", l)], writes=[("pnt", l)])
        S.op("act", lambda e, stg=stg, l=l: e.activation(out=wsT[:, l, :, :], in_=stg, func=AF.Copy),
             reads=[("pnt", l)], writes=["wsT"])

    def first_stage_gcol(after):
        seq = [s for s in order if do[s]]
        return seq

    for g in range(ngrp):
        for t in range(4):
            S.dma("sp", lambda e, g=g, t=t: e.dma_start(out=hcur[:, t, :], in_=x_d[g * 512 + t * 128:g * 512 + (t + 1) * 128, :]),
                  c_x[t], writes=[("h", t)])
        stages = [s for s in order if do[s] and (g >= 1 or s in ("l0", "l1", "kv"))]
        gcol_of = {"l0": 0, "l1": 32, "kv": 128, "l2": 64, "l3": 96}
        norm_batch([0, 1, 2, 3], gcol_of[stages[0]])
        for si, st in enumerate(stages):
            nxt_st = stages[si + 1] if si + 1 < len(stages) else None
            ngc = gcol_of[nxt_st] if nxt_st else None
            nn_st = stages[si + 2] if si + 2 < len(stages) else None
            if st in ("l0", "l1"):
                a_layer(g, int(st[1]), ngc, gcol_of[nn_st] if (nxt_st == "kv" and nn_st) else None)
            elif st == "kv":
                kv_stage(g)
                if ngc is not None and si == 0:
                    norm_batch([0, 1, 2, 3], ngc)
            else:
                shared = si >= 2 and stages[si - 1] == "kv" and stages[si - 2] in ("l0", "l1")
                b_layer(g, int(st[1]), ngc, "actT" if shared else None)
        if g >= 1:
            for t in range(4):
                S.dma("sp", lambda e, g=g, t=t: e.dma_start(
                    out=out_d[(g - 1) * 512 + t * 128:(g - 1) * 512 + (t + 1) * 128, :], in_=hcur[:, t, :]),
                    c_out[t], reads=[("h", t)])
    S.wait_only("sp", writes=[("h", t) for t in range(4)])
    if dump:
        S.wait_only("sp", writes=HN_ALL + ACT_ALL + [XK(i) for i in range(16)])
    counts = S.emit(nc)
    counts["marks"] = marks
    return nc, counts


def prep_inputs(x, norm_g, a_w_in, a_ln_g, a_ln_b, a_w_s, a_b_s, a_w_out, kv_norm_g, w_k, w_v,
                b_w_q, b_rel_bias, b_w_o, w_up, w_down):
    f = lambda a: np.ascontiguousarray(np.asarray(a, dtype=np.float32))
    x = f(x)
    norm_g = f(norm_g)
    gT = np.empty((128, 136), np.float32)
    gT[:, 0:128] = norm_g.reshape(16, 8, 128).transpose(2, 0, 1).reshape(128, 128)
    gT[:, 128:136] = f(kv_norm_g).reshape(8, 128).T
    wsT = f(np.asarray(a_w_s).transpose(0, 3, 1, 2).reshape(2, 128, 1024))
    bs = f(np.asarray(a_b_s).reshape(2, 1024))
    tab = f(b_rel_bias)
    kp = np.arange(128)[:, None, None]
    kb = np.array([0, 2, 3, 4])[None, :, None]
    q = np.arange(128)[None, None, :]
    idx = np.clip(q + 512 - (128 * kb + kp), -256, 256) + 256
    rb = tab[:, :, idx]
    rb = np.ascontiguousarray(rb.transpose(0, 2, 1, 3, 4))
    rb[:, 64:128, :, 3, 0:64] = -30000.0
    rb[:, 0:64, :, 0, 64:128] = -30000.0
    rb = rb.reshape(2, 128, 8192)
    chc = np.ascontiguousarray(np.broadcast_to(tab[:, :, 512].reshape(1, 32), (128, 32)))
    shared = {
        "gT": gT, "norm_g": norm_g.reshape(16, 1024), "a_w_in": f(a_w_in), "a_ln_g": f(a_ln_g),
        "a_ln_b": f(a_ln_b), "wsT": wsT, "bs": bs, "a_w_out": f(a_w_out), "w_k": f(w_k), "w_v": f(w_v),
        "b_w_q": f(b_w_q), "b_w_o": f(b_w_o), "rb": rb, "ch": chc, "w_up": f(w_up), "w_down": f(w_down),
        "ident": np.eye(128, dtype=np.float32),
    }
    in_maps = []
    for c in range(8):
        b, half = c // 2, c % 2
        xc = np.zeros((2560, 1024), np.float32)
        if half == 0:
            xc[512:] = x[b, 0:2048]
        else:
            xc[:] = x[b, 1536:4096]
        m = dict(shared)
        m["x"] = xc
        m["hv"] = np.full((128, 1), float(half), np.float32)
        in_maps.append(m)
    return in_maps


_CACHE = {}


def run(inputs, stop_after=None, trace=False, noffn=False, dump=False, ngrp=NGRP):
    key = (stop_after, noffn, dump, ngrp)
    if key not in _CACHE:
        _CACHE[key] = build_program(stop_after, noffn=noffn, dump=dump, ngrp=ngrp)
    nc, counts = _CACHE[key]
    in_maps = prep_inputs(**inputs)
    res = run_bass_kernel_spmd(nc, in_maps, core_ids=list(range(8)), trace=trace)
    out = np.empty((4, 4096, 1024), np.float32)
    for c in range(8):
        b, half = c // 2, c % 2
        out[b, half * 2048:(half + 1) * 2048] = res.results[c]["out"]
    return out, res


def kernel(**inputs):
    out, _ = run(inputs)
    return out
```

```python
from contextlib import ExitStack

import os
import numpy as np
import concourse.bass as bass
import concourse.mybir as mybir
from concourse.bass_utils import run_bass_kernel_spmd

F32 = mybir.dt.float32
BF16 = mybir.dt.bfloat16
I32 = mybir.dt.int32
AF = mybir.ActivationFunctionType
ALU = mybir.AluOpType

COMPUTE = ("pe", "act", "dve", "pool")
QUEUES = ("pe", "act", "dve", "pool", "sp")
NGRP = 5
RING = int(os.environ.get('K_RING', '5'))
NHS = int(os.environ.get('K_NHS', '4'))
INTERLEAVE = os.environ.get('K_INTER', '0') == '1'
POOL_ADD = os.environ.get('K_POOLADD', '1') == '1'
BIAS_BCAST = os.environ.get('K_BIASB', '1') == '1'
RMS_EPS = 1e-6
LN_EPS = 1e-5


class Chan:
    def __init__(self, name):
        self.name = name
        self.count = 0
        self.sem = None


class Sched:
    def __init__(self):
        self.ins = {e: [] for e in QUEUES}
        self.last_w = {}
        self.readers = {}
        self.seen = {e: {} for e in QUEUES}
        self.chans = []
        self.waited = {e: set() for e in COMPUTE}

    def chan(self, name):
        c = Chan(name)
        self.chans.append(c)
        return c

    def _collect(self, eng, reads, writes):
        need = {}

        def add(tok, is_raw):
            if tok[0] == "eng":
                _, e, idx = tok
                if e == eng and eng in ("pe", "sp"):
                    return
                key = ("eng", e)
                if need.get(key, -1) < idx:
                    need[key] = idx
            else:
                _, ch, cnt = tok
                key = ("dma", ch)
                if need.get(key, -1) < cnt:
                    need[key] = cnt

        for k in reads:
            t = self.last_w.get(k)
            if t is not None:
                add(t, True)
        for k in writes:
            t = self.last_w.get(k)
            if t is not None:
                add(t, True)
            for r in self.readers.get(k, ()):
                add(r, False)
        deps = []
        seen = self.seen[eng]
        for key, val in need.items():
            if seen.get(key, -1) >= val:
                continue
            seen[key] = val
            deps.append((key, val))
            if key[0] == "eng":
                self.waited[key[1]].add(val)
        return deps

    def _record(self, tok, reads, writes):
        for k in writes:
            self.last_w[k] = tok
            self.readers[k] = []
        for k in reads:
            lst = self.readers.setdefault(k, [])
            src = tok[:2]
            lst[:] = [r for r in lst if r[:2] != src]
            lst.append(tok)

    def op(self, eng, fn, reads=(), writes=()):
        idx = len(self.ins[eng])
        deps = self._collect(eng, reads, writes)
        self.ins[eng].append(dict(fn=fn, deps=deps, chan=None))
        self._record(("eng", eng, idx), reads, writes)
        return idx

    def dma(self, queue, fn, chan, reads=(), writes=()):
        deps = self._collect(queue, reads, writes)
        chan.count += 16
        self.ins[queue].append(dict(fn=fn, deps=deps, chan=chan))
        self._record(("dma", chan, chan.count), reads, writes)

    def wait_only(self, queue, reads=(), writes=()):
        deps = self._collect(queue, reads, writes)
        self.ins[queue].append(dict(fn=None, deps=deps, chan=None))

    def emit(self, nc):
        engobj = {"pe": "tensor", "act": "scalar", "dve": "vector", "pool": "gpsimd", "sp": "sync"}
        with ExitStack() as es:
            esem = {e: es.enter_context(nc.semaphore("prog_" + e)) for e in COMPUTE}
            for c in self.chans:
                c.sem = es.enter_context(nc.semaphore("ch_" + c.name))
            rank = {}
            for e in COMPUTE:
                rank[e] = {idx: i + 1 for i, idx in enumerate(sorted(self.waited[e]))}
            block = es.enter_context(nc.Block())

            def run(eng, name):
                mine = rank.get(name, {})
                for idx, r in enumerate(self.ins[name]):
                    for key, val in r["deps"]:
                        if key[0] == "eng":
                            eng.wait_ge(esem[key[1]], rank[key[1]][val])
                        else:
                            eng.wait_ge(key[1].sem, val)
                    if r["fn"] is None:
                        continue
                    ins = r["fn"](eng)
                    if r["chan"] is not None:
                        ins.then_inc(r["chan"].sem, 16)
                    elif idx in mine:
                        ins.then_inc(esem[name], 1)

            for name in QUEUES:
                if self.ins[name]:
                    getattr(block, engobj[name])(lambda eng, name=name: run(eng, name))
        return {e: len(self.ins[e]) for e in QUEUES}


def build_program(stop_after=None, ngrp=NGRP, noffn=False, dump=False):
    order = ["l0", "l1", "kv", "l2", "l3"]
    last = order.index(stop_after) if stop_after else len(order) - 1
    do = {s: (order.index(s) <= last) for s in order}

    nc = bass.Bass("TRN2", target_bir_lowering=False)
    S = Sched()
    marks = []

    def mark(label):
        marks.append((len(S.ins["pe"]), label))

    def dram(name, shape, kind="ExternalInput", dt=F32):
        return nc.dram_tensor(name, list(shape), dt, kind=kind).ap()

    x_d = dram("x", [2560, 1024])
    out_d = dram("out", [2048, 1024], kind="ExternalOutput")
    hv_d = dram("hv", [128, 1])
    gT_d = dram("gT", [128, 136])
    ng_d = dram("norm_g", [16, 1024])
    win_d = dram("a_w_in", [2, 1024, 2048])
    lng_d = dram("a_ln_g", [2, 1024])
    lnb_d = dram("a_ln_b", [2, 1024])
    wsT_d = dram("wsT", [2, 128, 1024])
    bs_d = dram("bs", [2, 1024])
    wout_d = dram("a_w_out", [2, 1024, 1024])
    wk_d = dram("w_k", [1024, 1024])
    wv_d = dram("w_v", [1024, 1024])
    wq_d = dram("b_w_q", [2, 1024, 1024])
    wo_d = dram("b_w_o", [2, 1024, 1024])
    rb_d = dram("rb", [2, 128, 8192])
    ch_d = dram("ch", [128, 32])
    wup_d = dram("w_up", [4, 1024, 4096])
    wdn_d = dram("w_down", [4, 4096, 1024])
    id_d = dram("ident", [128, 128])
    if dump:
        dbg = {
            "hnT": dram("dbg_hnT", [128, 4096], kind="ExternalOutput", dt=BF16),
            "uT": dram("dbg_uT", [128, 4096], kind="ExternalOutput", dt=BF16),
            "vF": dram("dbg_vF", [128, 4096], kind="ExternalOutput", dt=F32),
            "vn": dram("dbg_vn", [128, 4096], kind="ExternalOutput", dt=BF16),
            "gu": dram("dbg_gu", [128, 4096], kind="ExternalOutput", dt=BF16),
        }

    def sb(name, shape, dt):
        return nc.alloc_sbuf_tensor(name, list(shape), dt)

    hcur = sb("hcur", [128, 4, 1024], F32)
    KT = [sb(f"KT{i}", [128, 8, 512], BF16) for i in range(2)]
    VA = [sb(f"VA{i}", [128, 4, 16, 65], BF16) for i in range(2)]
    hnT = sb("hnT", [128, 8, 512], BF16)
    actT = sb("actT", [128, 8, 512], BF16)
    X = sb("X", [128, 16384], BF16)
    qT = [sb("qTA", [128, 8, 512], BF16), sb("qTB", [128, 8, 512], BF16)]
    hs = [sb(f"hs{i}", [128, 1024], BF16) for i in range(NHS)]
    pnt = [sb(f"pnt{i}", [128, 1024], F32) for i in range(2)]
    rtmp = [sb(f"rtmp{i}", [128, 512], BF16) for i in range(2)]
    PT = [sb(f"PT{i}", [128, 640], BF16) for i in range(3)]
    stmp = [sb(f"stmp{i}", [128, 512], F32) for i in range(2)]
    ob = [sb(f"ob{i}", [128, 1024], BF16) for i in range(2)]
    junk = ob[0]
    stat = sb("stat", [128, 256], F32)
    gT = sb("gTs", [128, 136], F32)
    gb = [sb(f"gb{i}", [128, 1024], F32) for i in range(2)]
    lnt = [sb(f"lnt{i}", [128, 1024], F32) for i in range(2)]
    wsT = sb("wsTs", [128, 2, 8, 128], BF16)
    bsrow = sb("bsrow", [1, 1024], BF16)
    ch = sb("chs", [128, 32], F32)
    ident = sb("identb", [128, 128], BF16)
    ones_row = sb("ones_row", [1, 128], BF16)
    hv = sb("hvs", [128, 1], F32)
    lnstat = [sb(f"lnstat{i}", [128, 12], F32) for i in range(2)]
    slots = [sb(f"slot{i}", [128, 4096], BF16) for i in range(RING)]

    hidT = X[:, :].rearrange("p (j t) -> p j t", j=32)
    uT = X[:, 0:4096].rearrange("p (c t) -> p c t", c=8)
    vF = X[:, 4096:12288].bitcast(F32).rearrange("p (t d) -> p t d", t=4)
    vn = X[:, 12288:16384].rearrange("p (t d) -> p t d", t=4)
    Rb = X[:, :].bitcast(F32).rearrange("p (h x) -> p h x", h=16)
    def XK(i):
        return ("X", i)

    RB_KEYS = [("X", i) for i in range(16)]

    PS = nc.alloc_psum_tensor("PS", [128, 4096], F32)
    pairs = [PS[:, i * 1024:(i + 1) * 1024] for i in range(4)]

    def bank(b):
        return pairs[b // 2][:, (b % 2) * 512:(b % 2) * 512 + 512]

    def bkey(b):
        return ("ps", b)

    psT = bank(7).bitcast(BF16).rearrange("p (c t) -> p c t", c=8)
    psT6 = bank(6).bitcast(BF16).rearrange("p (c t) -> p c t", c=8)
    psT0 = bank(0).bitcast(BF16).rearrange("p (c t) -> p c t", c=8)

    c_stage = [S.chan("stg0"), S.chan("stg1")]
    c_x = [S.chan(f"x{t}") for t in range(4)]
    c_out = [S.chan(f"o{t}") for t in range(4)]
    c_gb = [S.chan("gb0"), S.chan("gb1")]
    c_ln = [S.chan("ln0"), S.chan("ln1")]
    c_rb = S.chan("rb")
    c_bs = [S.chan("bs0"), S.chan("bs1")]
    c_dbg = S.chan("dbg")
    c_slot = [S.chan(f"slot{i}") for i in range(RING)]

    def panel(w2d, s):
        return ("panel", w2d.rearrange("(kc p) n -> p kc n", p=128)[:, :, s * 512:(s + 1) * 512])

    def dslab(w2d, s):
        return ("down", w2d.rearrange("(j p) n -> p j n", p=128)[:, 4 * s:4 * s + 4, :])

    plan = []
    for g in range(ngrp):
        for l in (0, 1):
            if not do[f"l{l}"]:
                continue
            plan += [panel(win_d[l], s) for s in (2, 3, 0, 1)]
            plan += [panel(wout_d[l], s) for s in range(2)]
            if not noffn:
                plan += [panel(wup_d[l], s) for s in range(8)]
                plan += [dslab(wdn_d[l], s) for s in range(8)]
        if do["kv"]:
            plan += [panel(wk_d, s) for s in range(2)]
            plan += [panel(wv_d, s) for s in range(2)]
        if g >= 1:
            for l in (2, 3):
                if not do[f"l{l}"]:
                    continue
                plan += [panel(wq_d[l - 2], s) for s in range(2)]
                plan += [panel(wo_d[l - 2], s) for s in range(2)]
                plan += [panel(wup_d[l], s) for s in range(8)]
                plan += [dslab(wdn_d[l], s) for s in range(8)]

    ring_state = dict(issued=0, taken=0)

    def ring_issue():
        n = ring_state["issued"]
        if n >= len(plan):
            return
        ring_state["issued"] = n + 1
        kind, src = plan[n]
        si = n % RING
        if kind == "panel":
            dst = slots[si][:, :].rearrange("p (a b) -> p a b", a=8)
        else:
            dst = slots[si][:, :].rearrange("p (a b) -> p a b", a=4)
        S.dma("pool", lambda e, dst=dst, src=src: e.dma_start(out=dst, in_=src), c_slot[si],
              writes=[("w", si)])

    def ring_next(kind):
        n = ring_state["taken"]
        ring_state["taken"] = n + 1
        assert plan[n][0] == kind, (n, plan[n][0], kind)
        si = n % RING
        if kind == "panel":
            v = slots[si][:, :].rearrange("p (a b) -> p a b", a=8)
        else:
            v = slots[si][:, :].rearrange("p (a b) -> p a b", a=4)
        return v, ("w", si)

    def ring_release():
        ring_issue()

    stat_i = [0]

    def newcols():
        i = stat_i[0] % 8
        stat_i[0] += 1
        return stat[:, i * 32:(i + 1) * 32], ("stat", i)

    def emit_rsqrt(xin, xin_key, pre_scale, eps, out, out_key, n=1):
        c, k = newcols()
        a, y, p, q = c[:, 0:n], c[:, 8:8 + n], c[:, 12:12 + n], c[:, 16:16 + n]
        S.op("dve", lambda e: e.tensor_scalar(out=a, in0=xin, scalar1=pre_scale, scalar2=eps,
                                              op0=ALU.mult, op1=ALU.add), reads=[xin_key], writes=[k])
        S.op("dve", lambda e: e.tensor_scalar(out=y.bitcast(I32), in0=a.bitcast(I32), scalar1=-0.5,
                                              scalar2=1597463007.0, op0=ALU.mult, op1=ALU.add),
             reads=[k], writes=[k])
        for it in range(2):
            S.op("dve", lambda e: e.tensor_tensor(out=p, in0=y, in1=y, op=ALU.mult), reads=[k], writes=[k])
            S.op("dve", lambda e: e.scalar_tensor_tensor(out=q, in0=p, scalar=-0.5, in1=a, op0=ALU.mult,
                                                         op1=ALU.mult), reads=[k], writes=[k])
            dst, dk = (y, k) if it == 0 else (out, out_key)
            S.op("dve", lambda e, dst=dst: e.scalar_tensor_tensor(out=dst, in0=q, scalar=1.5, in1=y, op0=ALU.add,
                                                                  op1=ALU.mult), reads=[k], writes=[dk])

    rot = dict(fm=0, tm=0, pair=0, hs=0, pnt=0, rt=0, lns=0, tb=0)

    def nxt(name, n):
        v = rot[name] % n
        rot[name] += 1
        return v

    def fm_bank():
        return 4 + nxt("fm", 3)

    def tm_bank():
        return nxt("tm", 4)

    def tm_pair():
        return nxt("pair", 2)

    def tile_cols(t):
        return slice(t * 128, (t + 1) * 128)

    def norm_part2(t, hidx, gcol, extra_gcol=None):
        tb = 7 - nxt("tb", 2)
        pT = psT if tb == 7 else psT6
        for fc in range(8):
            S.op("pe", lambda e, fc=fc: e.transpose(out=pT[:, fc, :], in_=hs[hidx][:, fc * 128:(fc + 1) * 128],
                                                    identity=ident[:, :]),
                 reads=[("hs", hidx), "ident"], writes=[bkey(tb)])
        gbc = gT[:, gcol:gcol + 8].unsqueeze(2).to_broadcast([128, 8, 128])
        S.op("dve", lambda e: e.tensor_tensor(out=hnT[:, :, tile_cols(t)], in0=pT, in1=gbc, op=ALU.mult),
             reads=[bkey(tb), "gT"], writes=[("hnT", t)])
        if extra_gcol is not None:
            gbc2 = gT[:, extra_gcol:extra_gcol + 8].unsqueeze(2).to_broadcast([128, 8, 128])
            S.op("dve", lambda e: e.tensor_tensor(out=actT[:, :, tile_cols(t)], in0=pT, in1=gbc2, op=ALU.mult),
                 reads=[bkey(tb), "gT"], writes=[("actT", t)])

    def norm_batch1(tl, gcol=None, extra_gcol=None):
        n = len(tl)
        c, k = newcols()
        for i, t in enumerate(tl):
            S.op("act", lambda e, i=i, t=t: e.activation(out=junk[:, :], in_=hcur[:, t, :], func=AF.Square,
                                                         accum_out=c[:, i:i + 1]),
                 reads=[("h", t)], writes=[("ob", 0), k])
        emit_rsqrt(c[:, 0:n], k, 1.0 / 1024, RMS_EPS, c[:, 4:4 + n], k, n)
        res = []
        for i, t in enumerate(tl):
            hidx = nxt("hs", NHS)
            S.op("act", lambda e, i=i, t=t, hidx=hidx: e.activation(out=hs[hidx][:, :], in_=hcur[:, t, :], func=AF.Copy,
                                                                    scale=c[:, 4 + i:5 + i]),
                 reads=[("h", t), k], writes=[("hs", hidx)])
            if gcol is not None and res and INTERLEAVE:
                norm_part2(res[-1][0], res[-1][1], gcol, extra_gcol)
            res.append((t, hidx))
        if gcol is not None:
            if INTERLEAVE:
                norm_part2(res[-1][0], res[-1][1], gcol, extra_gcol)
            else:
                for t, hidx in res:
                    norm_part2(t, hidx, gcol, extra_gcol)
        return res

    def norm_batch(tl, gcol, extra_gcol=None):
        norm_batch1(tl, gcol, extra_gcol)

    HN_ALL = [("hnT", t) for t in range(4)]
    ACT_ALL = [("actT", t) for t in range(4)]

    def postnorm_batch(tl, pis, gtile, gkey):
        n = len(tl)
        c, k = newcols()
        for i, (t, pi) in enumerate(zip(tl, pis)):
            S.op("act", lambda e, i=i, pi=pi: e.activation(out=junk[:, :], in_=pairs[pi][:, :], func=AF.Square,
                                                           accum_out=c[:, i:i + 1]),
                 reads=[bkey(2 * pi), bkey(2 * pi + 1)], writes=[("ob", 0), k])
        emit_rsqrt(c[:, 0:n], k, 1.0 / 1024, RMS_EPS, c[:, 4:4 + n], k, n)
        for i, (t, pi) in enumerate(zip(tl, pis)):
            pi2 = nxt("pnt", 2)
            S.op("dve", lambda e, i=i, pi=pi, pi2=pi2: e.scalar_tensor_tensor(
                out=pnt[pi2][:, :], in0=pairs[pi][:, :], scalar=c[:, 4 + i:5 + i], in1=gtile[:, :],
                op0=ALU.mult, op1=ALU.mult),
                reads=[bkey(2 * pi), bkey(2 * pi + 1), k, gkey], writes=[("pnt", pi2)])
            add_eng = "pool" if (POOL_ADD and i % 2 == 0) else "dve"
            S.op(add_eng, lambda e, t=t, pi2=pi2: e.tensor_tensor(out=hcur[:, t, :], in0=hcur[:, t, :], in1=pnt[pi2][:, :],
                                                                   op=ALU.add),
                 reads=[("h", t), ("pnt", pi2)], writes=[("h", t)])

    def load_bcast(dst, key, chan, row_ap):
        S.dma("sp", lambda e: e.dma_start(out=dst[:, :], in_=row_ap.partition_broadcast(128)), chan,
              writes=[key])

    def proj_tm_post(src_keys_fn, srcT, gtile, gkey, next_gcol):
        mark("proj")
        slabA, kA = ring_next("panel")
        slabB, kB = ring_next("panel")
        ppairs = [0, 1, 2, 0]
        pend = []
        for bi, tl in enumerate(((0, 1), (2, 3))):
            pis = []
            for t in tl:
                pi = ppairs[t]
                pis.append(pi)
                for half, (slab, ks) in enumerate(((slabA, kA), (slabB, kB))):
                    for kc in range(8):
                        S.op("pe", lambda e, kc=kc, slab=slab, half=half, pi=pi, t=t: e.matmul(
                            pairs[pi][:, half * 512:(half + 1) * 512], lhsT=srcT[:, kc, tile_cols(t)],
                            rhs=slab[:, kc, :], start=(kc == 0), stop=(kc == 7)),
                            reads=[ks, src_keys_fn(t)], writes=[bkey(2 * pi + half)])
            for f in pend:
                f()
            pend = []
            postnorm_batch(list(tl), pis, gtile, gkey)
            if next_gcol is not None:
                for t, hidx in norm_batch1(list(tl)):
                    pend.append(lambda t=t, hidx=hidx: norm_part2(t, hidx, next_gcol))
        ring_release()
        ring_release()
        for f in pend:
            f()

    def ffn(l, next_gcol, extra_gcol=None):
        if noffn:
            return
        mark("up")
        for s in range(8):
            slab, ks = ring_next("panel")
            for j4 in range(4):
                j = s * 4 + j4
                b = fm_bank()
                for kc in range(8):
                    S.op("pe", lambda e, kc=kc, slab=slab, j4=j4, b=b: e.matmul(
                        bank(b), lhsT=slab[:, kc, j4 * 128:(j4 + 1) * 128], rhs=hnT[:, kc, :],
                        start=(kc == 0), stop=(kc == 7)),
                        reads=[ks] + HN_ALL, writes=[bkey(b)])
                ri = nxt("rt", 2)
                S.op("act", lambda e, b=b, ri=ri: e.activation(out=rtmp[ri][:, :], in_=bank(b), func=AF.Relu),
                     reads=[bkey(b)], writes=[("rt", ri)])
                S.op("dve", lambda e, j=j, ri=ri: e.tensor_tensor(out=hidT[:, j, :], in0=rtmp[ri][:, :],
                                                                   in1=rtmp[ri][:, :], op=ALU.mult),
                     reads=[("rt", ri)], writes=[XK(j // 2)])
            ring_release()
        mark("down")
        dpair = [3, 2, 1, 0]
        for s in range(8):
            slab, ks = ring_next("down")
            for t in range(4):
                pi = dpair[t]
                for half in range(2):
                    for jj in range(4):
                        j = 4 * s + jj
                        S.op("pe", lambda e, slab=slab, jj=jj, j=j, half=half, pi=pi, t=t, s=s: e.matmul(
                            pairs[pi][:, half * 512:(half + 1) * 512], lhsT=hidT[:, j, tile_cols(t)],
                            rhs=slab[:, jj, half * 512:(half + 1) * 512],
                            start=(s == 0 and jj == 0), stop=(s == 7 and jj == 3)),
                            reads=[ks, XK(j // 2)], writes=[bkey(2 * pi + half)])
            ring_release()
        mark("post")
        postnorm_batch([0, 1, 2, 3], dpair, gb[1], "gb1")
        if next_gcol is not None:
            norm_batch([0, 1, 2, 3], next_gcol, extra_gcol)

    def dump_buf(name, src_ap, keys):
        S.dma("sp", lambda e: e.dma_start(out=dbg[name][:, :], in_=src_ap), c_dbg, reads=keys)

    def a_layer(g, l, next_gcol, extra_gcol=None):
        dd = dump and g == 1 and l == 0
        mark(f"A{l} g{g} v")
        if dd:
            dump_buf("hnT", hnT[:, :, :].rearrange("p a b -> p (a b)"), HN_ALL)
        load_bcast(gb[0], "gb0", c_gb[0], ng_d[l * 4 + 1:l * 4 + 2, :])
        load_bcast(gb[1], "gb1", c_gb[1], ng_d[l * 4 + 3:l * 4 + 4, :])
        load_bcast(lnt[0], "ln0", c_ln[0], lng_d[l:l + 1, :])
        load_bcast(lnt[1], "ln1", c_ln[1], lnb_d[l:l + 1, :])
        for hf in range(2):
            S.dma("sp", lambda e, hf=hf: e.dma_start(out=stmp[hf][0:1, :], in_=bs_d[l:l + 1, hf * 512:(hf + 1) * 512]),
                  c_bs[hf], writes=[("stmp", hf)])
            S.op("act", lambda e, hf=hf: e.activation(out=bsrow[0:1, hf * 512:(hf + 1) * 512], in_=stmp[hf][0:1, :],
                                                      func=AF.Copy),
                 reads=[("stmp", hf)], writes=["bsrow"])
        for s in range(2):
            slab, ks = ring_next("panel")
            for t in range(4):
                b = tm_bank()
                for kc in range(8):
                    S.op("pe", lambda e, kc=kc, slab=slab, t=t, b=b: e.matmul(
                        bank(b), lhsT=hnT[:, kc, tile_cols(t)], rhs=slab[:, kc, :],
                        start=(kc == 0), stop=(kc == 7)),
                        reads=[ks, ("hnT", t)], writes=[bkey(b)])
                S.op("act", lambda e, b=b, t=t, s=s: e.activation(out=vF[:, t, s * 512:(s + 1) * 512], in_=bank(b),
                                                                func=AF.Gelu_apprx_tanh),
                     reads=[bkey(b)], writes=[XK(4 + 2 * t + s)])
            ring_release()
        c, k = newcols()
        mv = c[:, 0:8].rearrange("p (t two) -> p t two", two=2)
        for t in range(4):
            li = nxt("lns", 2)
            S.op("dve", lambda e, t=t, li=li: e.bn_stats(out=lnstat[li][:, 0:6], in_=vF[:, t, 0:512]),
                 reads=[XK(4 + 2 * t)], writes=[("lnstat", li)])
            S.op("dve", lambda e, t=t, li=li: e.bn_stats(out=lnstat[li][:, 6:12], in_=vF[:, t, 512:1024]),
                 reads=[XK(5 + 2 * t)], writes=[("lnstat", li)])
            S.op("dve", lambda e, t=t, li=li: e.bn_aggr(out=c[:, 2 * t:2 * t + 2], in_=lnstat[li][:, 0:12]),
                 reads=[("lnstat", li)], writes=[k])
        emit_rsqrt(mv[:, :, 1], k, 1.0, LN_EPS, c[:, 8:12], k, 4)
        S.op("dve", lambda e: e.scalar_tensor_tensor(out=c[:, 12:16], in0=mv[:, :, 0], scalar=-1.0, in1=c[:, 8:12],
                                                     op0=ALU.mult, op1=ALU.mult),
             reads=[k], writes=[k])
        def ln_part_b(t):
            pi2 = nxt("pnt", 2)
            S.op("act", lambda e, t=t, pi2=pi2: e.activation(out=pnt[pi2][:, :], in_=vF[:, t, :], func=AF.Identity,
                                                              scale=c[:, 8 + t:9 + t], bias=c[:, 12 + t:13 + t]),
                 reads=[XK(4 + 2 * t), XK(5 + 2 * t), k], writes=[("pnt", pi2)])
            leng = "dve" if t < 2 else "pool"
            S.op(leng, lambda e, pi2=pi2: e.tensor_tensor(out=pnt[pi2][:, :], in0=pnt[pi2][:, :], in1=lnt[0][:, :],
                                                           op=ALU.mult),
                 reads=[("pnt", pi2), "ln0"], writes=[("pnt", pi2)])
            S.op(leng, lambda e, pi2=pi2, t=t: e.tensor_tensor(out=vn[:, t, :], in0=pnt[pi2][:, :], in1=lnt[1][:, :],
                                                                op=ALU.add),
                 reads=[("pnt", pi2), "ln1"], writes=[XK(12 + t)])
        mark("u")
        for s in range(2):
            slab, ks = ring_next("panel")
            for u4 in range(4):
                uc = s * 4 + u4
                b = fm_bank()
                for kc in range(8):
                    S.op("pe", lambda e, kc=kc, slab=slab, u4=u4, b=b: e.matmul(
                        bank(b), lhsT=slab[:, kc, u4 * 128:(u4 + 1) * 128], rhs=hnT[:, kc, :],
                        start=(kc == 0), stop=(kc == 7)),
                        reads=[ks] + HN_ALL, writes=[bkey(b)])
                S.op("act", lambda e, b=b, uc=uc: e.activation(out=uT[:, uc, :], in_=bank(b), func=AF.Gelu_apprx_tanh),
                     reads=[bkey(b)], writes=[XK(uc // 2)])
                if s == 1:
                    ln_part_b(u4)
            ring_release()
        if dd:
            dump_buf("uT", X[:, 0:4096], [XK(i) for i in range(4)])
            dump_buf("vF", X[:, 4096:12288].bitcast(F32), [XK(i) for i in range(4, 12)])
        if dd:
            dump_buf("vn", X[:, 12288:16384], [XK(i) for i in range(12, 16)])
        mark("gate")
        slabA, kA = ring_next("panel")
        slabB, kB = ring_next("panel")
        ffn_gcol = (l * 4 + 2) * 8
        ppairs = [0, 1, 2, 0]
        gview = pairs[3].rearrange("p (g i) -> p g i", g=8)
        def gate_tile(t):
            for half in range(2):
                bk = 6 + half
                S.op("pe", lambda e, half=half, bk=bk: e.matmul(
                    bank(bk), lhsT=ones_row[0:1, :], rhs=bsrow[0:1, half * 512:(half + 1) * 512],
                    start=True, stop=False),
                    reads=["ones", "bsrow"], writes=[bkey(bk)])
                for g4 in range(4):
                    gi = half * 4 + g4
                    S.op("pe", lambda e, gi=gi, g4=g4, bk=bk: e.matmul(
                        bank(bk)[:, g4 * 128:(g4 + 1) * 128], lhsT=vn[:, t, gi * 128:(gi + 1) * 128],
                        rhs=wsT[:, l, gi, :], start=False, stop=(g4 == 3)),
                        reads=[XK(12 + t), "wsT"], writes=[bkey(bk)])
            S.op("dve", lambda e: e.tensor_tensor(out=actT[:, :, tile_cols(t)], in0=gview,
                                                  in1=uT[:, :, tile_cols(t)], op=ALU.mult),
                 reads=[bkey(6), bkey(7)] + [XK(i) for i in range(4)], writes=[("actT", t)])

        def proj_tile(t):
            pi = ppairs[t]
            for half, (slab, ks) in enumerate(((slabA, kA), (slabB, kB))):
                for kc in range(8):
                    S.op("pe", lambda e, kc=kc, slab=slab, half=half: e.matmul(
                        pairs[pi][:, half * 512:(half + 1) * 512], lhsT=actT[:, kc, tile_cols(t)],
                        rhs=slab[:, kc, :], start=(kc == 0), stop=(kc == 7)),
                        reads=[ks, ("actT", t)], writes=[bkey(2 * pi + half)])

        for t in range(4):
            gate_tile(t)
            if t < 3:
                proj_tile(t)
        postnorm_batch([0, 1], [ppairs[0], ppairs[1]], gb[0], "gb0")
        pendA = norm_batch1([0, 1])
        proj_tile(3)
        for t, hidx in pendA:
            norm_part2(t, hidx, ffn_gcol)
        postnorm_batch([2, 3], [ppairs[2], ppairs[3]], gb[0], "gb0")
        pendB = norm_batch1([2, 3])
        ring_release()
        ring_release()
        for t, hidx in pendB:
            norm_part2(t, hidx, ffn_gcol)
        ffn(l, next_gcol, extra_gcol)

    def kv_stage(g):
        mark(f"KV g{g}")
        cur = g % 2
        for s in range(2):
            slab, ks = ring_next("panel")
            for f4 in range(4):
                fc = s * 4 + f4
                b = fm_bank()
                for kc in range(8):
                    S.op("pe", lambda e, kc=kc, slab=slab, f4=f4, b=b: e.matmul(
                        bank(b), lhsT=slab[:, kc, f4 * 128:(f4 + 1) * 128], rhs=hnT[:, kc, :],
                        start=(kc == 0), stop=(kc == 7)),
                        reads=[ks] + HN_ALL, writes=[bkey(b)])
                S.op("act", lambda e, b=b, fc=fc: e.activation(out=KT[cur][:, fc, :], in_=bank(b), func=AF.Copy),
                     reads=[bkey(b)], writes=[("KT", cur)])
            ring_release()
        for s in range(2):
            slab, ks = ring_next("panel")
            for t in range(4):
                b = tm_bank()
                for kc in range(8):
                    S.op("pe", lambda e, kc=kc, slab=slab, t=t, b=b: e.matmul(
                        bank(b), lhsT=hnT[:, kc, tile_cols(t)], rhs=slab[:, kc, :],
                        start=(kc == 0), stop=(kc == 7)),
                        reads=[ks, ("hnT", t)], writes=[bkey(b)])
                src = bank(b).rearrange("p (h d) -> p h d", h=8)
                dst = VA[cur][:, t, s * 8:(s + 1) * 8, 0:64]
                if g == 0:
                    S.op("dve", lambda e, src=src, dst=dst: e.tensor_scalar(out=dst, in0=src, scalar1=hv[:, 0:1],
                                                                            scalar2=None, op0=ALU.mult),
                         reads=[bkey(b), "hv"], writes=[("VA", cur, t)])
                else:
                    S.op("act", lambda e, src=src, dst=dst: e.activation(out=dst, in_=src, func=AF.Copy),
                         reads=[bkey(b)], writes=[("VA", cur, t)])
            ring_release()
        for t in range(4):
            dst = VA[cur][:, t, :, 64:65]
            if g == 0:
                S.op("dve", lambda e, dst=dst: e.tensor_copy(out=dst, in_=hv[:, 0:1].unsqueeze(1).to_broadcast([128, 16, 1])),
                     reads=["hv"], writes=[("VA", cur, t)])
            else:
                S.op("dve", lambda e, dst=dst: e.memset(dst, 1.0), writes=[("VA", cur, t)])

    def b_layer(g, l, next_gcol, qsrc=None):
        mark(f"B{l} g{g} q")
        j = l - 2
        prev, cur = (g - 1) % 2, g % 2
        load_bcast(gb[0], "gb0", c_gb[0], ng_d[l * 4 + 1:l * 4 + 2, :])
        load_bcast(gb[1], "gb1", c_gb[1], ng_d[l * 4 + 3:l * 4 + 4, :])
        S.dma("sp", lambda e: e.dma_start(out=X[:, :].bitcast(F32), in_=rb_d[j]), c_rb,
              writes=RB_KEYS)
        qs, qkeys = (actT, ACT_ALL) if qsrc == "actT" else (hnT, HN_ALL)
        for s in range(2):
            slab, ks = ring_next("panel")
            for f4 in range(4):
                fc = s * 4 + f4
                b = fm_bank()
                for kc in range(8):
                    S.op("pe", lambda e, kc=kc, slab=slab, f4=f4, b=b: e.matmul(
                        bank(b), lhsT=slab[:, kc, f4 * 128:(f4 + 1) * 128], rhs=qs[:, kc, :],
                        start=(kc == 0), stop=(kc == 7)),
                        reads=[ks] + qkeys, writes=[bkey(b)])
                S.op("act", lambda e, b=b, fc=fc: e.activation(out=qT[0][0:64, fc, :], in_=bank(b)[0:64, :],
                                                                func=AF.Copy, scale=0.125),
                     reads=[bkey(b)], writes=["qTA"])
                S.op("dve", lambda e, b=b, fc=fc: e.tensor_scalar(out=qT[1][64:128, fc, :], in0=bank(b)[64:128, :],
                                                                   scalar1=0.125, scalar2=None, op0=ALU.mult),
                     reads=[bkey(b)], writes=["qTB"])
            ring_release()

        mark("attn")
        items = [(ti, h) for ti in range(4) for h in range(16)]
        ssets = [(1, 2), (3, 4), (5, 6)]
        PCOL = {0: 0, 2: 1, 3: 2, 4: 3, 1: 4}
        PASSES = [(0, 7), (7, 7), (14, 2)]

        def obank(ti, ps_i):
            return (0, 7)[(ti + ps_i) % 2]

        def opass(h):
            return 0 if h < 7 else (1 if h < 14 else 2)

        def win(ti, kb):
            w = ti + kb
            return (prev if w < 4 else cur), w % 4

        def emit_S(n):
            ti, h = items[n]
            fc = h // 2
            sa, sbk = ssets[n % 3]
            q = qT[h % 2][:, fc, tile_cols(ti)]
            for kb in range(5):
                buf, wt = win(ti, kb)
                if kb == 1:
                    o = bank(sbk)[:, 0:128]
                    ok = bkey(sbk)
                else:
                    o = bank(sa)[:, PCOL[kb] * 128:(PCOL[kb] + 1) * 128]
                    ok = bkey(sa)
                S.op("pe", lambda e, o=o, buf=buf, fc=fc, wt=wt, q=q: e.matmul(
                    o, lhsT=KT[buf][:, fc, tile_cols(wt)], rhs=q, start=True, stop=True),
                    reads=[("KT", buf), "qTA" if h % 2 == 0 else "qTB"], writes=[ok])

        def emit_soft_pv(n):
            ti, h = items[n]
            sa, sbk = ssets[n % 3]
            p = n % 3
            si = n % 2
            S.op("dve", lambda e: e.tensor_tensor(out=stmp[si][:, :], in0=bank(sa), in1=Rb[:, h, :], op=ALU.add),
                 reads=[bkey(sa)] + RB_KEYS, writes=[("stmp", si)])
            S.op("act", lambda e: e.activation(out=PT[p][:, 512:640], in_=bank(sbk)[:, 0:128], func=AF.Exp,
                                               bias=ch[:, j * 16 + h:j * 16 + h + 1]),
                 reads=[bkey(sbk), "ch"], writes=[("PT", p)])
            S.op("act", lambda e: e.activation(out=PT[p][:, 0:512], in_=stmp[si][:, :], func=AF.Exp),
                 reads=[("stmp", si)], writes=[("PT", p)])
            h0, nh = PASSES[opass(h)]
            obk = obank(ti, opass(h))
            oo = (h - h0) * 65
            for kb in range(5):
                buf, wt = win(ti, kb)
                pc = PCOL[kb]
                S.op("pe", lambda e, kb=kb, buf=buf, wt=wt, pc=pc: e.matmul(
                    bank(obk)[:, oo:oo + 65], lhsT=PT[p][:, pc * 128:(pc + 1) * 128], rhs=VA[buf][:, wt, h, :],
                    start=(kb == 0), stop=(kb == 4)),
                    reads=[("PT", p), ("VA", buf, wt)], writes=[bkey(obk)])

        def emit_norm_pass(ti, oi, ps_i):
            h0, nh = PASSES[ps_i]
            obk = obank(ti, ps_i)
            c, k = newcols()
            Ob = bank(obk)[:, 0:nh * 65].rearrange("p (h d) -> p h d", h=nh)
            S.op("dve", lambda e: e.reciprocal(out=c[:, 0:nh].unsqueeze(2), in_=Ob[:, :, 64:65]),
                 reads=[bkey(obk)], writes=[k])
            dst = ob[oi][:, h0 * 64:(h0 + nh) * 64].rearrange("p (h d) -> p h d", h=nh)
            S.op("dve", lambda e: e.tensor_tensor(
                out=dst, in0=Ob[:, :, 0:64], in1=c[:, 0:nh].unsqueeze(2).to_broadcast([128, nh, 64]), op=ALU.mult),
                reads=[bkey(obk), k], writes=[("ob", oi)])

        def emit_oT(ti, oi):
            tb = 0 if ti % 2 == 0 else 7
            pT = psT0 if tb == 0 else psT
            for fc in range(8):
                S.op("pe", lambda e, fc=fc: e.transpose(out=pT[:, fc, :], in_=ob[oi][:, fc * 128:(fc + 1) * 128],
                                                        identity=ident[:, :]),
                     reads=[("ob", oi), "ident"], writes=[bkey(tb)])
            S.op("act", lambda e: e.activation(out=actT[:, :, tile_cols(ti)], in_=pT, func=AF.Copy),
                 reads=[bkey(tb)], writes=[("actT", ti)])

        emit_S(0)
        emit_S(1)
        pend_T = None
        pend_norm = []
        for n in range(len(items)):
            ti, h = items[n]
            if n + 2 < len(items):
                emit_S(n + 2)
            emit_soft_pv(n)
            while pend_norm and pend_norm[0][0] <= n:
                pend_norm.pop(0)[1]()
            if pend_T is not None and h == 3:
                pend_T()
                pend_T = None
            if h in (6, 13, 15):
                pend_norm.append((n + 2, lambda ti=ti, ps_i=opass(h): emit_norm_pass(ti, ti % 2, ps_i)))
            if h == 15:
                pend_T = (lambda ti=ti, oi=ti % 2: emit_oT(ti, oi))
        for _, f in pend_norm:
            f()
        pend_T()
        proj_tm_post(lambda t: ("actT", t), actT, gb[0], "gb0", (l * 4 + 2) * 8)
        ffn(l, next_gcol)

    S.dma("sp", lambda e: e.dma_start(out=gT[:, :], in_=gT_d[:, :]), S.chan("c_gT"), writes=["gT"])
    S.dma("sp", lambda e: e.dma_start(out=ch[:, :], in_=ch_d[:, :]), S.chan("c_ch"), writes=["ch"])
    S.dma("sp", lambda e: e.dma_start(out=hv[:, :], in_=hv_d[:, :]), S.chan("c_hv"), writes=["hv"])
    S.dma("sp", lambda e: e.dma_start(out=pnt[0][:, 0:128], in_=id_d[:, :]), c_stage[0], writes=[("pnt", 0)])
    for i in range(min(RING, len(plan))):
        ring_issue()
    S.op("act", lambda e: e.activation(out=ident[:, :], in_=pnt[0][:, 0:128], func=AF.Copy),
         reads=[("pnt", 0)], writes=["ident"])
    S.op("dve", lambda e: e.memset(ones_row[:, :], 1.0), writes=["ones"])
    S.op("dve", lambda e: e.memset(qT[0][:, :, :], 0.0), writes=["qTA"])
    S.op("dve", lambda e: e.memset(qT[1][:, :, :], 0.0), writes=["qTB"])
    for l in range(2):
        S.dma("sp", lambda e, l=l: e.dma_start(out=pnt[l][:, :], in_=wsT_d[l]), c_stage[l], writes=[("pnt", l)])
        stg = pnt[l][:, :].rearrange("p (g i) -> p g i", g=8)
        S.op("dve", lambda e, stg=stg: e.memset(stg[64:128, :, 0:64], 0.0), reads=[("pnt", l)], writes=[("pnt", l)])
        S.op("act", lambda e, stg=stg, l=l: e.activation(out=wsT[:, l, :, :], in_=stg, func=AF.Copy),
             reads=[("pnt", l)], writes=["wsT"])

    def first_stage_gcol(after):
        seq = [s for s in order if do[s]]
        return seq

    for g in range(ngrp):
        for t in range(4):
            S.dma("sp", lambda e, g=g, t=t: e.dma_start(out=hcur[:, t, :], in_=x_d[g * 512 + t * 128:g * 512 + (t + 1) * 128, :]),
                  c_x[t], writes=[("h", t)])
        stages = [s for s in order if do[s] and (g >= 1 or s in ("l0", "l1", "kv"))]
        gcol_of = {"l0": 0, "l1": 32, "kv": 128, "l2": 64, "l3": 96}
        norm_batch([0, 1, 2, 3], gcol_of[stages[0]])
        for si, st in enumerate(stages):
            nxt_st = stages[si + 1] if si + 1 < len(stages) else None
            ngc = gcol_of[nxt_st] if nxt_st else None
            nn_st = stages[si + 2] if si + 2 < len(stages) else None
            if st in ("l0", "l1"):
                a_layer(g, int(st[1]), ngc, gcol_of[nn_st] if (nxt_st == "kv" and nn_st) else None)
            elif st == "kv":
                kv_stage(g)
                if ngc is not None and si == 0:
                    norm_batch([0, 1, 2, 3], ngc)
            else:
                shared = si >= 2 and stages[si - 1] == "kv" and stages[si - 2] in ("l0", "l1")
                b_layer(g, int(st[1]), ngc, "actT" if shared else None)
        if g >= 1:
            for t in range(4):
                S.dma("sp", lambda e, g=g, t=t: e.dma_start(
                    out=out_d[(g - 1) * 512 + t * 128:(g - 1) * 512 + (t + 1) * 128, :], in_=hcur[:, t, :]),
                    c_out[t], reads=[("h", t)])
    S.wait_only("sp", writes=[("h", t) for t in range(4)])
    if dump:
        S.wait_only("sp", writes=HN_ALL + ACT_ALL + [XK(i) for i in range(16)])
    counts = S.emit(nc)
    counts["marks"] = marks
    return nc, counts


def prep_inputs(x, norm_g, a_w_in, a_ln_g, a_ln_b, a_w_s, a_b_s, a_w_out, kv_norm_g, w_k, w_v,
                b_w_q, b_rel_bias, b_w_o, w_up, w_down):
    f = lambda a: np.ascontiguousarray(np.asarray(a, dtype=np.float32))
    x = f(x)
    norm_g = f(norm_g)
    gT = np.empty((128, 136), np.float32)
    gT[:, 0:128] = norm_g.reshape(16, 8, 128).transpose(2, 0, 1).reshape(128, 128)
    gT[:, 128:136] = f(kv_norm_g).reshape(8, 128).T
    wsT = f(np.asarray(a_w_s).transpose(0, 3, 1, 2).reshape(2, 128, 1024))
    bs = f(np.asarray(a_b_s).reshape(2, 1024))
    tab = f(b_rel_bias)
    kp = np.arange(128)[:, None, None]
    kb = np.array([0, 2, 3, 4])[None, :, None]
    q = np.arange(128)[None, None, :]
    idx = np.clip(q + 512 - (128 * kb + kp), -256, 256) + 256
    rb = tab[:, :, idx]
    rb = np.ascontiguousarray(rb.transpose(0, 2, 1, 3, 4))
    rb[:, 64:128, :, 3, 0:64] = -30000.0
    rb[:, 0:64, :, 0, 64:128] = -30000.0
    rb = rb.reshape(2, 128, 8192)
    chc = np.ascontiguousarray(np.broadcast_to(tab[:, :, 512].reshape(1, 32), (128, 32)))
    shared = {
        "gT": gT, "norm_g": norm_g.reshape(16, 1024), "a_w_in": f(a_w_in), "a_ln_g": f(a_ln_g),
        "a_ln_b": f(a_ln_b), "wsT": wsT, "bs": bs, "a_w_out": f(a_w_out), "w_k": f(w_k), "w_v": f(w_v),
        "b_w_q": f(b_w_q), "b_w_o": f(b_w_o), "rb": rb, "ch": chc, "w_up": f(w_up), "w_down": f(w_down),
        "ident": np.eye(128, dtype=np.float32),
    }
    in_maps = []
    for c in range(8):
        b, half = c // 2, c % 2
        xc = np.zeros((2560, 1024), np.float32)
        if half == 0:
            xc[512:] = x[b, 0:2048]
        else:
            xc[:] = x[b, 1536:4096]
        m = dict(shared)
        m["x"] = xc
        m["hv"] = np.full((128, 1), float(half), np.float32)
        in_maps.append(m)
    return in_maps


_CACHE = {}


def run(inputs, stop_after=None, trace=False, noffn=False, dump=False, ngrp=NGRP):
    key = (stop_after, noffn, dump, ngrp)
    if key not in _CACHE:
        _CACHE[key] = build_program(stop_after, noffn=noffn, dump=dump, ngrp=ngrp)
    nc, counts = _CACHE[key]
    in_maps = prep_inputs(**inputs)
    res = run_bass_kernel_spmd(nc, in_maps, core_ids=list(range(8)), trace=trace)
    out = np.empty((4, 4096, 1024), np.float32)
    for c in range(8):
        b, half = c // 2, c % 2
        out[b, half * 2048:(half + 1) * 2048] = res.results[c]["out"]
    return out, res


def kernel(**inputs):
    out, _ = run(inputs)
    return out
```

```python
from contextlib import ExitStack

import os
import numpy as np
import concourse.bass as bass
import concourse.mybir as mybir
from concourse.bass_utils import run_bass_kernel_spmd

F32 = mybir.dt.float32
BF16 = mybir.dt.bfloat16
I32 = mybir.dt.int32
AF = mybir.ActivationFunctionType
ALU = mybir.AluOpType

COMPUTE = ("pe", "act", "dve", "pool")
QUEUES = ("pe", "act", "dve", "pool", "sp")
NGRP = 5
RING = int(os.environ.get('K_RING', '5'))
NHS = int(os.environ.get('K_NHS', '4'))
INTERLEAVE = os.environ.get('K_INTER', '0') == '1'
POOL_ADD = os.environ.get('K_POOLADD', '1') == '1'
BIAS_BCAST = os.environ.get('K_BIASB', '1') == '1'
RMS_EPS = 1e-6
LN_EPS = 1e-5


class Chan:
    def __init__(self, name):
        self.name = name
        self.count = 0
        self.sem = None


class Sched:
    def __init__(self):
        self.ins = {e: [] for e in QUEUES}
        self.last_w = {}
        self.readers = {}
        self.seen = {e: {} for e in QUEUES}
        self.chans = []
        self.waited = {e: set() for e in COMPUTE}

    def chan(self, name):
        c = Chan(name)
        self.chans.append(c)
        return c

    def _collect(self, eng, reads, writes):
        need = {}

        def add(tok, is_raw):
            if tok[0] == "eng":
                _, e, idx = tok
                if e == eng and eng in ("pe", "sp"):
                    return
                key = ("eng", e)
                if need.get(key, -1) < idx:
                    need[key] = idx
            else:
                _, ch, cnt = tok
                key = ("dma", ch)
                if need.get(key, -1) < cnt:
                    need[key] = cnt

        for k in reads:
            t = self.last_w.get(k)
            if t is not None:
                add(t, True)
        for k in writes:
            t = self.last_w.get(k)
            if t is not None:
                add(t, True)
            for r in self.readers.get(k, ()):
                add(r, False)
        deps = []
        seen = self.seen[eng]
        for key, val in need.items():
            if seen.get(key, -1) >= val:
                continue
            seen[key] = val
            deps.append((key, val))
            if key[0] == "eng":
                self.waited[key[1]].add(val)
        return deps

    def _record(self, tok, reads, writes):
        for k in writes:
            self.last_w[k] = tok
            self.readers[k] = []
        for k in reads:
            lst = self.readers.setdefault(k, [])
            src = tok[:2]
            lst[:] = [r for r in lst if r[:2] != src]
            lst.append(tok)

    def op(self, eng, fn, reads=(), writes=()):
        idx = len(self.ins[eng])
        deps = self._collect(eng, reads, writes)
        self.ins[eng].append(dict(fn=fn, deps=deps, chan=None))
        self._record(("eng", eng, idx), reads, writes)
        return idx

    def dma(self, queue, fn, chan, reads=(), writes=()):
        deps = self._collect(queue, reads, writes)
        chan.count += 16
        self.ins[queue].append(dict(fn=fn, deps=deps, chan=chan))
        self._record(("dma", chan, chan.count), reads, writes)

    def wait_only(self, queue, reads=(), writes=()):
        deps = self._collect(queue, reads, writes)
        self.ins[queue].append(dict(fn=None, deps=deps, chan=None))

    def emit(self, nc):
        engobj = {"pe": "tensor", "act": "scalar", "dve": "vector", "pool": "gpsimd", "sp": "sync"}
        with ExitStack() as es:
            esem = {e: es.enter_context(nc.semaphore("prog_" + e)) for e in COMPUTE}
            for c in self.chans:
                c.sem = es.enter_context(nc.semaphore("ch_" + c.name))
            rank = {}
            for e in COMPUTE:
                rank[e] = {idx: i + 1 for i, idx in enumerate(sorted(self.waited[e]))}
            block = es.enter_context(nc.Block())

            def run(eng, name):
                mine = rank.get(name, {})
                for idx, r in enumerate(self.ins[name]):
                    for key, val in r["deps"]:
                        if key[0] == "eng":
                            eng.wait_ge(esem[key[1]], rank[key[1]][val])
                        else:
                            eng.wait_ge(key[1].sem, val)
                    if r["fn"] is None:
                        continue
                    ins = r["fn"](eng)
                    if r["chan"] is not None:
                        ins.then_inc(r["chan"].sem, 16)
                    elif idx in mine:
                        ins.then_inc(esem[name], 1)

            for name in QUEUES:
                if self.ins[name]:
                    getattr(block, engobj[name])(lambda eng, name=name: run(eng, name))
        return {e: len(self.ins[e]) for e in QUEUES}


def build_program(stop_after=None, ngrp=NGRP, noffn=False, dump=False):
    order = ["l0", "l1", "kv", "l2", "l3"]
    last = order.index(stop_after) if stop_after else len(order) - 1
    do = {s: (order.index(s) <= last) for s in order}

    nc = bass.Bass("TRN2", target_bir_lowering=False)
    S = Sched()
    marks = []

    def mark(label):
        marks.append((len(S.ins["pe"]), label))

    def dram(name, shape, kind="ExternalInput", dt=F32):
        return nc.dram_tensor(name, list(shape), dt, kind=kind).ap()

    x_d = dram("x", [2560, 1024])
    out_d = dram("out", [2048, 1024], kind="ExternalOutput")
    hv_d = dram("hv", [128, 1])
    gT_d = dram("gT", [128, 136])
    ng_d = dram("norm_g", [16, 1024])
    win_d = dram("a_w_in", [2, 1024, 2048])
    lng_d = dram("a_ln_g", [2, 1024])
    lnb_d = dram("a_ln_b", [2, 1024])
    wsT_d = dram("wsT", [2, 128, 1024])
    bs_d = dram("bs", [2, 1024])
    wout_d = dram("a_w_out", [2, 1024, 1024])
    wk_d = dram("w_k", [1024, 1024])
    wv_d = dram("w_v", [1024, 1024])
    wq_d = dram("b_w_q", [2, 1024, 1024])
    wo_d = dram("b_w_o", [2, 1024, 1024])
    rb_d = dram("rb", [2, 128, 8192])
    ch_d = dram("ch", [128, 32])
    wup_d = dram("w_up", [4, 1024, 4096])
    wdn_d = dram("w_down", [4, 4096, 1024])
    id_d = dram("ident", [128, 128])
    if dump:
        dbg = {
            "hnT": dram("dbg_hnT", [128, 4096], kind="ExternalOutput", dt=BF16),
            "uT": dram("dbg_uT", [128, 4096], kind="ExternalOutput", dt=BF16),
            "vF": dram("dbg_vF", [128, 4096], kind="ExternalOutput", dt=F32),
            "vn": dram("dbg_vn", [128, 4096], kind="ExternalOutput", dt=BF16),
            "gu": dram("dbg_gu", [128, 4096], kind="ExternalOutput", dt=BF16),
        }

    def sb(name, shape, dt):
        return nc.alloc_sbuf_tensor(name, list(shape), dt)

    hcur = sb("hcur", [128, 4, 1024], F32)
    KT = [sb(f"KT{i}", [128, 8, 512], BF16) for i in range(2)]
    VA = [sb(f"VA{i}", [128, 4, 16, 65], BF16) for i in range(2)]
    hnT = sb("hnT", [128, 8, 512], BF16)
    actT = sb("actT", [128, 8, 512], BF16)
    X = sb("X", [128, 16384], BF16)
    qT = [sb("qTA", [128, 8, 512], BF16), sb("qTB", [128, 8, 512], BF16)]
    hs = [sb(f"hs{i}", [128, 1024], BF16) for i in range(NHS)]
    pnt = [sb(f"pnt{i}", [128, 1024], F32) for i in range(2)]
    rtmp = [sb(f"rtmp{i}", [128, 512], BF16) for i in range(2)]
    PT = [sb(f"PT{i}", [128, 640], BF16) for i in range(3)]
    stmp = [sb(f"stmp{i}", [128, 512], F32) for i in range(2)]
    ob = [sb(f"ob{i}", [128, 1024], BF16) for i in range(2)]
    junk = ob[0]
    stat = sb("stat", [128, 256], F32)
    gT = sb("gTs", [128, 136], F32)
    gb = [sb(f"gb{i}", [128, 1024], F32) for i in range(2)]
    lnt = [sb(f"lnt{i}", [128, 1024], F32) for i in range(2)]
    wsT = sb("wsTs", [128, 2, 8, 128], BF16)
    bsrow = sb("bsrow", [1, 1024], BF16)
    ch = sb("chs", [128, 32], F32)
    ident = sb("identb", [128, 128], BF16)
    ones_row = sb("ones_row", [1, 128], BF16)
    hv = sb("hvs", [128, 1], F32)
    lnstat = [sb(f"lnstat{i}", [128, 12], F32) for i in range(2)]
    slots = [sb(f"slot{i}", [128, 4096], BF16) for i in range(RING)]

    hidT = X[:, :].rearrange("p (j t) -> p j t", j=32)
    uT = X[:, 0:4096].rearrange("p (c t) -> p c t", c=8)
    vF = X[:, 4096:12288].bitcast(F32).rearrange("p (t d) -> p t d", t=4)
    vn = X[:, 12288:16384].rearrange("p (t d) -> p t d", t=4)
    Rb = X[:, :].bitcast(F32).rearrange("p (h x) -> p h x", h=16)
    def XK(i):
        return ("X", i)

    RB_KEYS = [("X", i) for i in range(16)]

    PS = nc.alloc_psum_tensor("PS", [128, 4096], F32)
    pairs = [PS[:, i * 1024:(i + 1) * 1024] for i in range(4)]

    def bank(b):
        return pairs[b // 2][:, (b % 2) * 512:(b % 2) * 512 + 512]

    def bkey(b):
        return ("ps", b)

    psT = bank(7).bitcast(BF16).rearrange("p (c t) -> p c t", c=8)
    psT6 = bank(6).bitcast(BF16).rearrange("p (c t) -> p c t", c=8)
    psT0 = bank(0).bitcast(BF16).rearrange("p (c t) -> p c t", c=8)

    c_stage = [S.chan("stg0"), S.chan("stg1")]
    c_x = [S.chan(f"x{t}") for t in range(4)]
    c_out = [S.chan(f"o{t}") for t in range(4)]
    c_gb = [S.chan("gb0"), S.chan("gb1")]
    c_ln = [S.chan("ln0"), S.chan("ln1")]
    c_rb = S.chan("rb")
    c_bs = [S.chan("bs0"), S.chan("bs1")]
    c_dbg = S.chan("dbg")
    c_slot = [S.chan(f"slot{i}") for i in range(RING)]

    def panel(w2d, s):
        return ("panel", w2d.rearrange("(kc p) n -> p kc n", p=128)[:, :, s * 512:(s + 1) * 512])

    def dslab(w2d, s):
        return ("down", w2d.rearrange("(j p) n -> p j n", p=128)[:, 4 * s:4 * s + 4, :])

    plan = []
    for g in range(ngrp):
        for l in (0, 1):
            if not do[f"l{l}"]:
                continue
            plan += [panel(win_d[l], s) for s in (2, 3, 0, 1)]
            plan += [panel(wout_d[l], s) for s in range(2)]
            if not noffn:
                plan += [panel(wup_d[l], s) for s in range(8)]
                plan += [dslab(wdn_d[l], s) for s in range(8)]
        if do["kv"]:
            plan += [panel(wk_d, s) for s in range(2)]
            plan += [panel(wv_d, s) for s in range(2)]
        if g >= 1:
            for l in (2, 3):
                if not do[f"l{l}"]:
                    continue
                plan += [panel(wq_d[l - 2], s) for s in range(2)]
                plan += [panel(wo_d[l - 2], s) for s in range(2)]
                plan += [panel(wup_d[l], s) for s in range(8)]
                plan += [dslab(wdn_d[l], s) for s in range(8)]

    ring_state = dict(issued=0, taken=0)

    def ring_issue():
        n = ring_state["issued"]
        if n >= len(plan):
            return
        ring_state["issued"] = n + 1
        kind, src = plan[n]
        si = n % RING
        if kind == "panel":
            dst = slots[si][:, :].rearrange("p (a b) -> p a b", a=8)
        else:
            dst = slots[si][:, :].rearrange("p (a b) -> p a b", a=4)
        S.dma("pool", lambda e, dst=dst, src=src: e.dma_start(out=dst, in_=src), c_slot[si],
              writes=[("w", si)])

    def ring_next(kind):
        n = ring_state["taken"]
        ring_state["taken"] = n + 1
        assert plan[n][0] == kind, (n, plan[n][0], kind)
        si = n % RING
        if kind == "panel":
            v = slots[si][:, :].rearrange("p (a b) -> p a b", a=8)
        else:
            v = slots[si][:, :].rearrange("p (a b) -> p a b", a=4)
        return v, ("w", si)

    def ring_release():
        ring_issue()

    stat_i = [0]

    def newcols():
        i = stat_i[0] % 8
        stat_i[0] += 1
        return stat[:, i * 32:(i + 1) * 32], ("stat", i)

    def emit_rsqrt(xin, xin_key, pre_scale, eps, out, out_key, n=1):
        c, k = newcols()
        a, y, p, q = c[:, 0:n], c[:, 8:8 + n], c[:, 12:12 + n], c[:, 16:16 + n]
        S.op("dve", lambda e: e.tensor_scalar(out=a, in0=xin, scalar1=pre_scale, scalar2=eps,
                                              op0=ALU.mult, op1=ALU.add), reads=[xin_key], writes=[k])
        S.op("dve", lambda e: e.tensor_scalar(out=y.bitcast(I32), in0=a.bitcast(I32), scalar1=-0.5,
                                              scalar2=1597463007.0, op0=ALU.mult, op1=ALU.add),
             reads=[k], writes=[k])
        for it in range(2):
            S.op("dve", lambda e: e.tensor_tensor(out=p, in0=y, in1=y, op=ALU.mult), reads=[k], writes=[k])
            S.op("dve", lambda e: e.scalar_tensor_tensor(out=q, in0=p, scalar=-0.5, in1=a, op0=ALU.mult,
                                                         op1=ALU.mult), reads=[k], writes=[k])
            dst, dk = (y, k) if it == 0 else (out, out_key)
            S.op("dve", lambda e, dst=dst: e.scalar_tensor_tensor(out=dst, in0=q, scalar=1.5, in1=y, op0=ALU.add,
                                                                  op1=ALU.mult), reads=[k], writes=[dk])

    rot = dict(fm=0, tm=0, pair=0, hs=0, pnt=0, rt=0, lns=0, tb=0)

    def nxt(name, n):
        v = rot[name] % n
        rot[name] += 1
        return v

    def fm_bank():
        return 4 + nxt("fm", 3)

    def tm_bank():
        return nxt("tm", 4)

    def tm_pair():
        return nxt("pair", 2)

    def tile_cols(t):
        return slice(t * 128, (t + 1) * 128)

    def norm_part2(t, hidx, gcol, extra_gcol=None):
        tb = 7 - nxt("tb", 2)
        pT = psT if tb == 7 else psT6
        for fc in range(8):
            S.op("pe", lambda e, fc=fc: e.transpose(out=pT[:, fc, :], in_=hs[hidx][:, fc * 128:(fc + 1) * 128],
                                                    identity=ident[:, :]),
                 reads=[("hs", hidx), "ident"], writes=[bkey(tb)])
        gbc = gT[:, gcol:gcol + 8].unsqueeze(2).to_broadcast([128, 8, 128])
        S.op("dve", lambda e: e.tensor_tensor(out=hnT[:, :, tile_cols(t)], in0=pT, in1=gbc, op=ALU.mult),
             reads=[bkey(tb), "gT"], writes=[("hnT", t)])
        if extra_gcol is not None:
            gbc2 = gT[:, extra_gcol:extra_gcol + 8].unsqueeze(2).to_broadcast([128, 8, 128])
            S.op("dve", lambda e: e.tensor_tensor(out=actT[:, :, tile_cols(t)], in0=pT, in1=gbc2, op=ALU.mult),
                 reads=[bkey(tb), "gT"], writes=[("actT", t)])

    def norm_batch1(tl, gcol=None, extra_gcol=None):
        n = len(tl)
        c, k = newcols()
        for i, t in enumerate(tl):
            S.op("act", lambda e, i=i, t=t: e.activation(out=junk[:, :], in_=hcur[:, t, :], func=AF.Square,
                                                         accum_out=c[:, i:i + 1]),
                 reads=[("h", t)], writes=[("ob", 0), k])
        emit_rsqrt(c[:, 0:n], k, 1.0 / 1024, RMS_EPS, c[:, 4:4 + n], k, n)
        res = []
        for i, t in enumerate(tl):
            hidx = nxt("hs", NHS)
            S.op("act", lambda e, i=i, t=t, hidx=hidx: e.activation(out=hs[hidx][:, :], in_=hcur[:, t, :], func=AF.Copy,
                                                                    scale=c[:, 4 + i:5 + i]),
                 reads=[("h", t), k], writes=[("hs", hidx)])
            if gcol is not None and res and INTERLEAVE:
                norm_part2(res[-1][0], res[-1][1], gcol, extra_gcol)
            res.append((t, hidx))
        if gcol is not None:
            if INTERLEAVE:
                norm_part2(res[-1][0], res[-1][1], gcol, extra_gcol)
            else:
                for t, hidx in res:
                    norm_part2(t, hidx, gcol, extra_gcol)
        return res

    def norm_batch(tl, gcol, extra_gcol=None):
        norm_batch1(tl, gcol, extra_gcol)

    HN_ALL = [("hnT", t) for t in range(4)]
    ACT_ALL = [("actT", t) for t in range(4)]

    def postnorm_batch(tl, pis, gtile, gkey):
        n = len(tl)
        c, k = newcols()
        for i, (t, pi) in enumerate(zip(tl, pis)):
            S.op("act", lambda e, i=i, pi=pi: e.activation(out=junk[:, :], in_=pairs[pi][:, :], func=AF.Square,
                                                           accum_out=c[:, i:i + 1]),
                 reads=[bkey(2 * pi), bkey(2 * pi + 1)], writes=[("ob", 0), k])
        emit_rsqrt(c[:, 0:n], k, 1.0 / 1024, RMS_EPS, c[:, 4:4 + n], k, n)
        for i, (t, pi) in enumerate(zip(tl, pis)):
            pi2 = nxt("pnt", 2)
            S.op("dve", lambda e, i=i, pi=pi, pi2=pi2: e.scalar_tensor_tensor(
                out=pnt[pi2][:, :], in0=pairs[pi][:, :], scalar=c[:, 4 + i:5 + i], in1=gtile[:, :],
                op0=ALU.mult, op1=ALU.mult),
                reads=[bkey(2 * pi), bkey(2 * pi + 1), k, gkey], writes=[("pnt", pi2)])
            add_eng = "pool" if (POOL_ADD and i % 2 == 0) else "dve"
            S.op(add_eng, lambda e, t=t, pi2=pi2: e.tensor_tensor(out=hcur[:, t, :], in0=hcur[:, t, :], in1=pnt[pi2][:, :],
                                                                   op=ALU.add),
                 reads=[("h", t), ("pnt", pi2)], writes=[("h", t)])

    def load_bcast(dst, key, chan, row_ap):
        S.dma("sp", lambda e: e.dma_start(out=dst[:, :], in_=row_ap.partition_broadcast(128)), chan,
              writes=[key])

    def proj_tm_post(src_keys_fn, srcT, gtile, gkey, next_gcol):
        mark("proj")
        slabA, kA = ring_next("panel")
        slabB, kB = ring_next("panel")
        ppairs = [0, 1, 2, 0]
        pend = []
        for bi, tl in enumerate(((0, 1), (2, 3))):
            pis = []
            for t in tl:
                pi = ppairs[t]
                pis.append(pi)
                for half, (slab, ks) in enumerate(((slabA, kA), (slabB, kB))):
                    for kc in range(8):
                        S.op("pe", lambda e, kc=kc, slab=slab, half=half, pi=pi, t=t: e.matmul(
                            pairs[pi][:, half * 512:(half + 1) * 512], lhsT=srcT[:, kc, tile_cols(t)],
                            rhs=slab[:, kc, :], start=(kc == 0), stop=(kc == 7)),
                            reads=[ks, src_keys_fn(t)], writes=[bkey(2 * pi + half)])
            for f in pend:
                f()
            pend = []
            postnorm_batch(list(tl), pis, gtile, gkey)
            if next_gcol is not None:
                for t, hidx in norm_batch1(list(tl)):
                    pend.append(lambda t=t, hidx=hidx: norm_part2(t, hidx, next_gcol))
        ring_release()
        ring_release()
        for f in pend:
            f()

    def ffn(l, next_gcol, extra_gcol=None):
        if noffn:
            return
        mark("up")
        for s in range(8):
            slab, ks = ring_next("panel")
            for j4 in range(4):
                j = s * 4 + j4
                b = fm_bank()
                for kc in range(8):
                    S.op("pe", lambda e, kc=kc, slab=slab, j4=j4, b=b: e.matmul(
                        bank(b), lhsT=slab[:, kc, j4 * 128:(j4 + 1) * 128], rhs=hnT[:, kc, :],
                        start=(kc == 0), stop=(kc == 7)),
                        reads=[ks] + HN_ALL, writes=[bkey(b)])
                ri = nxt("rt", 2)
                S.op("act", lambda e, b=b, ri=ri: e.activation(out=rtmp[ri][:, :], in_=bank(b), func=AF.Relu),
                     reads=[bkey(b)], writes=[("rt", ri)])
                S.op("dve", lambda e, j=j, ri=ri: e.tensor_tensor(out=hidT[:, j, :], in0=rtmp[ri][:, :],
                                                                   in1=rtmp[ri][:, :], op=ALU.mult),
                     reads=[("rt", ri)], writes=[XK(j // 2)])
            ring_release()
        mark("down")
        dpair = [3, 2, 1, 0]
        for s in range(8):
            slab, ks = ring_next("down")
            for t in range(4):
                pi = dpair[t]
                for half in range(2):
                    for jj in range(4):
                        j = 4 * s + jj
                        S.op("pe", lambda e, slab=slab, jj=jj, j=j, half=half, pi=pi, t=t, s=s: e.matmul(
                            pairs[pi][:, half * 512:(half + 1) * 512], lhsT=hidT[:, j, tile_cols(t)],
                            rhs=slab[:, jj, half * 512:(half + 1) * 512],
                            start=(s == 0 and jj == 0), stop=(s == 7 and jj == 3)),
                            reads=[ks, XK(j // 2)], writes=[bkey(2 * pi + half)])
            ring_release()
        mark("post")
        postnorm_batch([0, 1, 2, 3], dpair, gb[1], "gb1")
        if next_gcol is not None:
            norm_batch([0, 1, 2, 3], next_gcol, extra_gcol)

    def dump_buf(name, src_ap, keys):
        S.dma("sp", lambda e: e.dma_start(out=dbg[name][:, :], in_=src_ap), c_dbg, reads=keys)

    def a_layer(g, l, next_gcol, extra_gcol=None):
        dd = dump and g == 1 and l == 0
        mark(f"A{l} g{g} v")
        if dd:
            dump_buf("hnT", hnT[:, :, :].rearrange("p a b -> p (a b)"), HN_ALL)
        load_bcast(gb[0], "gb0", c_gb[0], ng_d[l * 4 + 1:l * 4 + 2, :])
        load_bcast(gb[1], "gb1", c_gb[1], ng_d[l * 4 + 3:l * 4 + 4, :])
        load_bcast(lnt[0], "ln0", c_ln[0], lng_d[l:l + 1, :])
        load_bcast(lnt[1], "ln1", c_ln[1], lnb_d[l:l + 1, :])
        for hf in range(2):
            S.dma("sp", lambda e, hf=hf: e.dma_start(out=stmp[hf][0:1, :], in_=bs_d[l:l + 1, hf * 512:(hf + 1) * 512]),
                  c_bs[hf], writes=[("stmp", hf)])
            S.op("act", lambda e, hf=hf: e.activation(out=bsrow[0:1, hf * 512:(hf + 1) * 512], in_=stmp[hf][0:1, :],
                                                      func=AF.Copy),
                 reads=[("stmp", hf)], writes=["bsrow"])
        for s in range(2):
            slab, ks = ring_next("panel")
            for t in range(4):
                b = tm_bank()
                for kc in range(8):
                    S.op("pe", lambda e, kc=kc, slab=slab, t=t, b=b: e.matmul(
                        bank(b), lhsT=hnT[:, kc, tile_cols(t)], rhs=slab[:, kc, :],
                        start=(kc == 0), stop=(kc == 7)),
                        reads=[ks, ("hnT", t)], writes=[bkey(b)])
                S.op("act", lambda e, b=b, t=t, s=s: e.activation(out=vF[:, t, s * 512:(s + 1) * 512], in_=bank(b),
                                                                func=AF.Gelu_apprx_tanh),
                     reads=[bkey(b)], writes=[XK(4 + 2 * t + s)])
            ring_release()
        c, k = newcols()
        mv = c[:, 0:8].rearrange("p (t two) -> p t two", two=2)
        for t in range(4):
            li = nxt("lns", 2)
            S.op("dve", lambda e, t=t, li=li: e.bn_stats(out=lnstat[li][:, 0:6], in_=vF[:, t, 0:512]),
                 reads=[XK(4 + 2 * t)], writes=[("lnstat", li)])
            S.op("dve", lambda e, t=t, li=li: e.bn_stats(out=lnstat[li][:, 6:12], in_=vF[:, t, 512:1024]),
                 reads=[XK(5 + 2 * t)], writes=[("lnstat", li)])
            S.op("dve", lambda e, t=t, li=li: e.bn_aggr(out=c[:, 2 * t:2 * t + 2], in_=lnstat[li][:, 0:12]),
                 reads=[("lnstat", li)], writes=[k])
        emit_rsqrt(mv[:, :, 1], k, 1.0, LN_EPS, c[:, 8:12], k, 4)
        S.op("dve", lambda e: e.scalar_tensor_tensor(out=c[:, 12:16], in0=mv[:, :, 0], scalar=-1.0, in1=c[:, 8:12],
                                                     op0=ALU.mult, op1=ALU.mult),
             reads=[k], writes=[k])
        def ln_part_b(t):
            pi2 = nxt("pnt", 2)
            S.op("act", lambda e, t=t, pi2=pi2: e.activation(out=pnt[pi2][:, :], in_=vF[:, t, :], func=AF.Identity,
                                                              scale=c[:, 8 + t:9 + t], bias=c[:, 12 + t:13 + t]),
                 reads=[XK(4 + 2 * t), XK(5 + 2 * t), k], writes=[("pnt", pi2)])
            leng = "dve" if t < 2 else "pool"
            S.op(leng, lambda e, pi2=pi2: e.tensor_tensor(out=pnt[pi2][:, :], in0=pnt[pi2][:, :], in1=lnt[0][:, :],
                                                           op=ALU.mult),
                 reads=[("pnt", pi2), "ln0"], writes=[("pnt", pi2)])
            S.op(leng, lambda e, pi2=pi2, t=t: e.tensor_tensor(out=vn[:, t, :], in0=pnt[pi2][:, :], in1=lnt[1][:, :],
                                                                op=ALU.add),
                 reads=[("pnt", pi2), "ln1"], writes=[XK(12 + t)])
        mark("u")
        for s in range(2):
            slab, ks = ring_next("panel")
            for u4 in range(4):
                uc = s * 4 + u4
                b = fm_bank()
                for kc in range(8):
                    S.op("pe", lambda e, kc=kc, slab=slab, u4=u4, b=b: e.matmul(
                        bank(b), lhsT=slab[:, kc, u4 * 128:(u4 + 1) * 128], rhs=hnT[:, kc, :],
                        start=(kc == 0), stop=(kc == 7)),
                        reads=[ks] + HN_ALL, writes=[bkey(b)])
                S.op("act", lambda e, b=b, uc=uc: e.activation(out=uT[:, uc, :], in_=bank(b), func=AF.Gelu_apprx_tanh),
                     reads=[bkey(b)], writes=[XK(uc // 2)])
                if s == 1:
                    ln_part_b(u4)
            ring_release()
        if dd:
            dump_buf("uT", X[:, 0:4096], [XK(i) for i in range(4)])
            dump_buf("vF", X[:, 4096:12288].bitcast(F32), [XK(i) for i in range(4, 12)])
        if dd:
            dump_buf("vn", X[:, 12288:16384], [XK(i) for i in range(12, 16)])
        mark("gate")
        slabA, kA = ring_next("panel")
        slabB, kB = ring_next("panel")
        ffn_gcol = (l * 4 + 2) * 8
        gpair = [3, 2, 3, 2]
        ppairs = [0, 1, 3, 2]

        def gate_tile(t):
            gp = gpair[t]
            for half in range(2):
                bk = 2 * gp + half
                S.op("pe", lambda e, half=half, bk=bk: e.matmul(
                    bank(bk), lhsT=ones_row[0:1, :], rhs=bsrow[0:1, half * 512:(half + 1) * 512],
                    start=True, stop=False),
                    reads=["ones", "bsrow"], writes=[bkey(bk)])
                for g4 in range(4):
                    gi = half * 4 + g4
                    S.op("pe", lambda e, gi=gi, g4=g4, bk=bk: e.matmul(
                        bank(bk)[:, g4 * 128:(g4 + 1) * 128], lhsT=vn[:, t, gi * 128:(gi + 1) * 128],
                        rhs=wsT[:, l, gi, :], start=False, stop=(g4 == 3)),
                        reads=[XK(12 + t), "wsT"], writes=[bkey(bk)])
            gview = pairs[gp].rearrange("p (g i) -> p g i", g=8)
            S.op("dve", lambda e: e.tensor_tensor(out=actT[:, :, tile_cols(t)], in0=gview,
                                                  in1=uT[:, :, tile_cols(t)], op=ALU.mult),
                 reads=[bkey(2 * gp), bkey(2 * gp + 1)] + [XK(i) for i in range(4)], writes=[("actT", t)])

        def proj_tile(t):
            pi = ppairs[t]
            for half, (slab, ks) in enumerate(((slabA, kA), (slabB, kB))):
                for kc in range(8):
                    S.op("pe", lambda e, kc=kc, slab=slab, half=half: e.matmul(
                        pairs[pi][:, half * 512:(half + 1) * 512], lhsT=actT[:, kc, tile_cols(t)],
                        rhs=slab[:, kc, :], start=(kc == 0), stop=(kc == 7)),
                        reads=[ks, ("actT", t)], writes=[bkey(2 * pi + half)])

        gate_tile(0)
        gate_tile(1)
        proj_tile(0)
        gate_tile(2)
        proj_tile(1)
        gate_tile(3)
        proj_tile(2)
        proj_tile(3)
        ring_release()
        ring_release()
        postnorm_batch([0, 1, 2, 3], ppairs, gb[0], "gb0")
        norm_batch([0, 1, 2, 3], ffn_gcol)
        ffn(l, next_gcol, extra_gcol)

    def kv_stage(g):
        mark(f"KV g{g}")
        cur = g % 2
        for s in range(2):
            slab, ks = ring_next("panel")
            for f4 in range(4):
                fc = s * 4 + f4
                b = fm_bank()
                for kc in range(8):
                    S.op("pe", lambda e, kc=kc, slab=slab, f4=f4, b=b: e.matmul(
                        bank(b), lhsT=slab[:, kc, f4 * 128:(f4 + 1) * 128], rhs=hnT[:, kc, :],
                        start=(kc == 0), stop=(kc == 7)),
                        reads=[ks] + HN_ALL, writes=[bkey(b)])
                S.op("act", lambda e, b=b, fc=fc: e.activation(out=KT[cur][:, fc, :], in_=bank(b), func=AF.Copy),
                     reads=[bkey(b)], writes=[("KT", cur)])
            ring_release()
        for s in range(2):
            slab, ks = ring_next("panel")
            for t in range(4):
                b = tm_bank()
                for kc in range(8):
                    S.op("pe", lambda e, kc=kc, slab=slab, t=t, b=b: e.matmul(
                        bank(b), lhsT=hnT[:, kc, tile_cols(t)], rhs=slab[:, kc, :],
                        start=(kc == 0), stop=(kc == 7)),
                        reads=[ks, ("hnT", t)], writes=[bkey(b)])
                src = bank(b).rearrange("p (h d) -> p h d", h=8)
                dst = VA[cur][:, t, s * 8:(s + 1) * 8, 0:64]
                if g == 0:
                    S.op("dve", lambda e, src=src, dst=dst: e.tensor_scalar(out=dst, in0=src, scalar1=hv[:, 0:1],
                                                                            scalar2=None, op0=ALU.mult),
                         reads=[bkey(b), "hv"], writes=[("VA", cur, t)])
                else:
                    S.op("act", lambda e, src=src, dst=dst: e.activation(out=dst, in_=src, func=AF.Copy),
                         reads=[bkey(b)], writes=[("VA", cur, t)])
            ring_release()
        for t in range(4):
            dst = VA[cur][:, t, :, 64:65]
            if g == 0:
                S.op("dve", lambda e, dst=dst: e.tensor_copy(out=dst, in_=hv[:, 0:1].unsqueeze(1).to_broadcast([128, 16, 1])),
                     reads=["hv"], writes=[("VA", cur, t)])
            else:
                S.op("dve", lambda e, dst=dst: e.memset(dst, 1.0), writes=[("VA", cur, t)])

    def b_layer(g, l, next_gcol, qsrc=None):
        mark(f"B{l} g{g} q")
        j = l - 2
        prev, cur = (g - 1) % 2, g % 2
        load_bcast(gb[0], "gb0", c_gb[0], ng_d[l * 4 + 1:l * 4 + 2, :])
        load_bcast(gb[1], "gb1", c_gb[1], ng_d[l * 4 + 3:l * 4 + 4, :])
        S.dma("sp", lambda e: e.dma_start(out=X[:, :].bitcast(F32), in_=rb_d[j]), c_rb,
              writes=RB_KEYS)
        qs, qkeys = (actT, ACT_ALL) if qsrc == "actT" else (hnT, HN_ALL)
        for s in range(2):
            slab, ks = ring_next("panel")
            for f4 in range(4):
                fc = s * 4 + f4
                b = fm_bank()
                for kc in range(8):
                    S.op("pe", lambda e, kc=kc, slab=slab, f4=f4, b=b: e.matmul(
                        bank(b), lhsT=slab[:, kc, f4 * 128:(f4 + 1) * 128], rhs=qs[:, kc, :],
                        start=(kc == 0), stop=(kc == 7)),
                        reads=[ks] + qkeys, writes=[bkey(b)])
                S.op("act", lambda e, b=b, fc=fc: e.activation(out=qT[0][0:64, fc, :], in_=bank(b)[0:64, :],
                                                                func=AF.Copy, scale=0.125),
                     reads=[bkey(b)], writes=["qTA"])
                S.op("dve", lambda e, b=b, fc=fc: e.tensor_scalar(out=qT[1][64:128, fc, :], in0=bank(b)[64:128, :],
                                                                   scalar1=0.125, scalar2=None, op0=ALU.mult),
                     reads=[bkey(b)], writes=["qTB"])
            ring_release()

        mark("attn")
        items = [(ti, h) for ti in range(4) for h in range(16)]
        ssets = [(1, 2), (3, 4), (5, 6)]
        PCOL = {0: 0, 2: 1, 3: 2, 4: 3, 1: 4}
        PASSES = [(0, 7), (7, 7), (14, 2)]

        def obank(ti, ps_i):
            return (0, 7)[(ti + ps_i) % 2]

        def opass(h):
            return 0 if h < 7 else (1 if h < 14 else 2)

        def win(ti, kb):
            w = ti + kb
            return (prev if w < 4 else cur), w % 4

        def emit_S(n):
            ti, h = items[n]
            fc = h // 2
            sa, sbk = ssets[n % 3]
            q = qT[h % 2][:, fc, tile_cols(ti)]
            for kb in range(5):
                buf, wt = win(ti, kb)
                if kb == 1:
                    o = bank(sbk)[:, 0:128]
                    ok = bkey(sbk)
                else:
                    o = bank(sa)[:, PCOL[kb] * 128:(PCOL[kb] + 1) * 128]
                    ok = bkey(sa)
                S.op("pe", lambda e, o=o, buf=buf, fc=fc, wt=wt, q=q: e.matmul(
                    o, lhsT=KT[buf][:, fc, tile_cols(wt)], rhs=q, start=True, stop=True),
                    reads=[("KT", buf), "qTA" if h % 2 == 0 else "qTB"], writes=[ok])

        def emit_soft_pv(n):
            ti, h = items[n]
            sa, sbk = ssets[n % 3]
            p = n % 3
            si = n % 2
            S.op("dve", lambda e: e.tensor_tensor(out=stmp[si][:, :], in0=bank(sa), in1=Rb[:, h, :], op=ALU.add),
                 reads=[bkey(sa)] + RB_KEYS, writes=[("stmp", si)])
            S.op("act", lambda e: e.activation(out=PT[p][:, 512:640], in_=bank(sbk)[:, 0:128], func=AF.Exp,
                                               bias=ch[:, j * 16 + h:j * 16 + h + 1]),
                 reads=[bkey(sbk), "ch"], writes=[("PT", p)])
            S.op("act", lambda e: e.activation(out=PT[p][:, 0:512], in_=stmp[si][:, :], func=AF.Exp),
                 reads=[("stmp", si)], writes=[("PT", p)])
            h0, nh = PASSES[opass(h)]
            obk = obank(ti, opass(h))
            oo = (h - h0) * 65
            for kb in range(5):
                buf, wt = win(ti, kb)
                pc = PCOL[kb]
                S.op("pe", lambda e, kb=kb, buf=buf, wt=wt, pc=pc: e.matmul(
                    bank(obk)[:, oo:oo + 65], lhsT=PT[p][:, pc * 128:(pc + 1) * 128], rhs=VA[buf][:, wt, h, :],
                    start=(kb == 0), stop=(kb == 4)),
                    reads=[("PT", p), ("VA", buf, wt)], writes=[bkey(obk)])

        def emit_norm_pass(ti, oi, ps_i):
            h0, nh = PASSES[ps_i]
            obk = obank(ti, ps_i)
            c, k = newcols()
            Ob = bank(obk)[:, 0:nh * 65].rearrange("p (h d) -> p h d", h=nh)
            S.op("dve", lambda e: e.reciprocal(out=c[:, 0:nh].unsqueeze(2), in_=Ob[:, :, 64:65]),
                 reads=[bkey(obk)], writes=[k])
            dst = ob[oi][:, h0 * 64:(h0 + nh) * 64].rearrange("p (h d) -> p h d", h=nh)
            S.op("dve", lambda e: e.tensor_tensor(
                out=dst, in0=Ob[:, :, 0:64], in1=c[:, 0:nh].unsqueeze(2).to_broadcast([128, nh, 64]), op=ALU.mult),
                reads=[bkey(obk), k], writes=[("ob", oi)])

        def emit_oT(ti, oi):
            tb = 0 if ti % 2 == 0 else 7
            pT = psT0 if tb == 0 else psT
            for fc in range(8):
                S.op("pe", lambda e, fc=fc: e.transpose(out=pT[:, fc, :], in_=ob[oi][:, fc * 128:(fc + 1) * 128],
                                                        identity=ident[:, :]),
                     reads=[("ob", oi), "ident"], writes=[bkey(tb)])
            S.op("act", lambda e: e.activation(out=actT[:, :, tile_cols(ti)], in_=pT, func=AF.Copy),
                 reads=[bkey(tb)], writes=[("actT", ti)])

        emit_S(0)
        emit_S(1)
        pend_T = None
        pend_norm = []
        for n in range(len(items)):
            ti, h = items[n]
            if n + 2 < len(items):
                emit_S(n + 2)
            emit_soft_pv(n)
            while pend_norm and pend_norm[0][0] <= n:
                pend_norm.pop(0)[1]()
            if pend_T is not None and h == 3:
                pend_T()
                pend_T = None
            if h in (6, 13, 15):
                pend_norm.append((n + 2, lambda ti=ti, ps_i=opass(h): emit_norm_pass(ti, ti % 2, ps_i)))
            if h == 15:
                pend_T = (lambda ti=ti, oi=ti % 2: emit_oT(ti, oi))
        for _, f in pend_norm:
            f()
        pend_T()
        proj_tm_post(lambda t: ("actT", t), actT, gb[0], "gb0", (l * 4 + 2) * 8)
        ffn(l, next_gcol)

    S.dma("sp", lambda e: e.dma_start(out=gT[:, :], in_=gT_d[:, :]), S.chan("c_gT"), writes=["gT"])
    S.dma("sp", lambda e: e.dma_start(out=ch[:, :], in_=ch_d[:, :]), S.chan("c_ch"), writes=["ch"])
    S.dma("sp", lambda e: e.dma_start(out=hv[:, :], in_=hv_d[:, :]), S.chan("c_hv"), writes=["hv"])
    S.dma("sp", lambda e: e.dma_start(out=pnt[0][:, 0:128], in_=id_d[:, :]), c_stage[0], writes=[("pnt", 0)])
    for i in range(min(RING, len(plan))):
        ring_issue()
    S.op("act", lambda e: e.activation(out=ident[:, :], in_=pnt[0][:, 0:128], func=AF.Copy),
         reads=[("pnt", 0)], writes=["ident"])
    S.op("dve", lambda e: e.memset(ones_row[:, :], 1.0), writes=["ones"])
    S.op("dve", lambda e: e.memset(qT[0][:, :, :], 0.0), writes=["qTA"])
    S.op("dve", lambda e: e.memset(qT[1][:, :, :], 0.0), writes=["qTB"])
    for l in range(2):
        S.dma("sp", lambda e, l=l: e.dma_start(out=pnt[l][:, :], in_=wsT_d[l]), c_stage[l], writes=[("pnt", l)])
        stg = pnt[l][:, :].rearrange("p (g i) -> p g i", g=8)
        S.op("dve", lambda e, stg=stg: e.memset(stg[64:128, :, 0:64], 0.0), reads=[("pnt", l)], writes=[("pnt", l)])
        S.op("act", lambda e, stg=stg, l=l: e.activation(out=wsT[:, l, :, :], in_=stg, func=AF.Copy),
             reads=[("pnt", l)], writes=["wsT"])

    def first_stage_gcol(after):
        seq = [s for s in order if do[s]]
        return seq

    for g in range(ngrp):
        for t in range(4):
            S.dma("sp", lambda e, g=g, t=t: e.dma_start(out=hcur[:, t, :], in_=x_d[g * 512 + t * 128:g * 512 + (t + 1) * 128, :]),
                  c_x[t], writes=[("h", t)])
        stages = [s for s in order if do[s] and (g >= 1 or s in ("l0", "l1", "kv"))]
        gcol_of = {"l0": 0, "l1": 32, "kv": 128, "l2": 64, "l3": 96}
        norm_batch([0, 1, 2, 3], gcol_of[stages[0]])
        for si, st in enumerate(stages):
            nxt_st = stages[si + 1] if si + 1 < len(stages) else None
            ngc = gcol_of[nxt_st] if nxt_st else None
            nn_st = stages[si + 2] if si + 2 < len(stages) else None
            if st in ("l0", "l1"):
                a_layer(g, int(st[1]), ngc, gcol_of[nn_st] if (nxt_st == "kv" and nn_st) else None)
            elif st == "kv":
                kv_stage(g)
                if ngc is not None and si == 0:
                    norm_batch([0, 1, 2, 3], ngc)
            else:
                shared = si >= 2 and stages[si - 1] == "kv" and stages[si - 2] in ("l0", "l1")
                b_layer(g, int(st[1]), ngc, "actT" if shared else None)
        if g >= 1:
            for t in range(4):
                S.dma("sp", lambda e, g=g, t=t: e.dma_start(
                    out=out_d[(g - 1) * 512 + t * 128:(g - 1) * 512 + (t + 1) * 128, :], in_=hcur[:, t, :]),
                    c_out[t], reads=[("h", t)])
    S.wait_only("sp", writes=[("h", t) for t in range(4)])
    if dump:
        S.wait_only("sp", writes=HN_ALL + ACT_ALL + [XK(i) for i in range(16)])
    counts = S.emit(nc)
    counts["marks"] = marks
    return nc, counts


def prep_inputs(x, norm_g, a_w_in, a_ln_g, a_ln_b, a_w_s, a_b_s, a_w_out, kv_norm_g, w_k, w_v,
                b_w_q, b_rel_bias, b_w_o, w_up, w_down):
    f = lambda a: np.ascontiguousarray(np.asarray(a, dtype=np.float32))
    x = f(x)
    norm_g = f(norm_g)
    gT = np.empty((128, 136), np.float32)
    gT[:, 0:128] = norm_g.reshape(16, 8, 128).transpose(2, 0, 1).reshape(128, 128)
    gT[:, 128:136] = f(kv_norm_g).reshape(8, 128).T
    wsT = f(np.asarray(a_w_s).transpose(0, 3, 1, 2).reshape(2, 128, 1024))
    bs = f(np.asarray(a_b_s).reshape(2, 1024))
    tab = f(b_rel_bias)
    kp = np.arange(128)[:, None, None]
    kb = np.array([0, 2, 3, 4])[None, :, None]
    q = np.arange(128)[None, None, :]
    idx = np.clip(q + 512 - (128 * kb + kp), -256, 256) + 256
    rb = tab[:, :, idx]
    rb = np.ascontiguousarray(rb.transpose(0, 2, 1, 3, 4))
    rb[:, 64:128, :, 3, 0:64] = -30000.0
    rb[:, 0:64, :, 0, 64:128] = -30000.0
    rb = rb.reshape(2, 128, 8192)
    chc = np.ascontiguousarray(np.broadcast_to(tab[:, :, 512].reshape(1, 32), (128, 32)))
    shared = {
        "gT": gT, "norm_g": norm_g.reshape(16, 1024), "a_w_in": f(a_w_in), "a_ln_g": f(a_ln_g),
        "a_ln_b": f(a_ln_b), "wsT": wsT, "bs": bs, "a_w_out": f(a_w_out), "w_k": f(w_k), "w_v": f(w_v),
        "b_w_q": f(b_w_q), "b_w_o": f(b_w_o), "rb": rb, "ch": chc, "w_up": f(w_up), "w_down": f(w_down),
        "ident": np.eye(128, dtype=np.float32),
    }
    in_maps = []
    for c in range(8):
        b, half = c // 2, c % 2
        xc = np.zeros((2560, 1024), np.float32)
        if half == 0:
            xc[512:] = x[b, 0:2048]
        else:
            xc[:] = x[b, 1536:4096]
        m = dict(shared)
        m["x"] = xc
        m["hv"] = np.full((128, 1), float(half), np.float32)
        in_maps.append(m)
    return in_maps


_CACHE = {}


def run(inputs, stop_after=None, trace=False, noffn=False, dump=False, ngrp=NGRP):
    key = (stop_after, noffn, dump, ngrp)
    if key not in _CACHE:
        _CACHE[key] = build_program(stop_after, noffn=noffn, dump=dump, ngrp=ngrp)
    nc, counts = _CACHE[key]
    in_maps = prep_inputs(**inputs)
    res = run_bass_kernel_spmd(nc, in_maps, core_ids=list(range(8)), trace=trace)
    out = np.empty((4, 4096, 1024), np.float32)
    for c in range(8):
        b, half = c // 2, c % 2
        out[b, half * 2048:(half + 1) * 2048] = res.results[c]["out"]
    return out, res


def kernel(**inputs):
    out, _ = run(inputs)
    return out
```

```python
from contextlib import ExitStack

import os
import numpy as np
import concourse.bass as bass
import concourse.mybir as mybir
from concourse.bass_utils import run_bass_kernel_spmd

F32 = mybir.dt.float32
BF16 = mybir.dt.bfloat16
I32 = mybir.dt.int32
AF = mybir.ActivationFunctionType
ALU = mybir.AluOpType

COMPUTE = ("pe", "act", "dve", "pool")
QUEUES = ("pe", "act", "dve", "pool", "sp")
NGRP = 5
RING = int(os.environ.get('K_RING', '5'))
NHS = int(os.environ.get('K_NHS', '4'))
INTERLEAVE = os.environ.get('K_INTER', '0') == '1'
POOL_ADD = os.environ.get('K_POOLADD', '1') == '1'
BIAS_BCAST = os.environ.get('K_BIASB', '1') == '1'
RMS_EPS = 1e-6
LN_EPS = 1e-5


class Chan:
    def __init__(self, name):
        self.name = name
        self.count = 0
        self.sem = None


class Sched:
    def __init__(self):
        self.ins = {e: [] for e in QUEUES}
        self.last_w = {}
        self.readers = {}
        self.seen = {e: {} for e in QUEUES}
        self.chans = []
        self.waited = {e: set() for e in COMPUTE}

    def chan(self, name):
        c = Chan(name)
        self.chans.append(c)
        return c

    def _collect(self, eng, reads, writes):
        need = {}

        def add(tok, is_raw):
            if tok[0] == "eng":
                _, e, idx = tok
                if e == eng and eng in ("pe", "sp"):
                    return
                key = ("eng", e)
                if need.get(key, -1) < idx:
                    need[key] = idx
            else:
                _, ch, cnt = tok
                key = ("dma", ch)
                if need.get(key, -1) < cnt:
                    need[key] = cnt

        for k in reads:
            t = self.last_w.get(k)
            if t is not None:
                add(t, True)
        for k in writes:
            t = self.last_w.get(k)
            if t is not None:
                add(t, True)
            for r in self.readers.get(k, ()):
                add(r, False)
        deps = []
        seen = self.seen[eng]
        for key, val in need.items():
            if seen.get(key, -1) >= val:
                continue
            seen[key] = val
            deps.append((key, val))
            if key[0] == "eng":
                self.waited[key[1]].add(val)
        return deps

    def _record(self, tok, reads, writes):
        for k in writes:
            self.last_w[k] = tok
            self.readers[k] = []
        for k in reads:
            lst = self.readers.setdefault(k, [])
            src = tok[:2]
            lst[:] = [r for r in lst if r[:2] != src]
            lst.append(tok)

    def op(self, eng, fn, reads=(), writes=()):
        idx = len(self.ins[eng])
        deps = self._collect(eng, reads, writes)
        self.ins[eng].append(dict(fn=fn, deps=deps, chan=None))
        self._record(("eng", eng, idx), reads, writes)
        return idx

    def dma(self, queue, fn, chan, reads=(), writes=()):
        deps = self._collect(queue, reads, writes)
        chan.count += 16
        self.ins[queue].append(dict(fn=fn, deps=deps, chan=chan))
        self._record(("dma", chan, chan.count), reads, writes)

    def wait_only(self, queue, reads=(), writes=()):
        deps = self._collect(queue, reads, writes)
        self.ins[queue].append(dict(fn=None, deps=deps, chan=None))

    def emit(self, nc):
        engobj = {"pe": "tensor", "act": "scalar", "dve": "vector", "pool": "gpsimd", "sp": "sync"}
        with ExitStack() as es:
            esem = {e: es.enter_context(nc.semaphore("prog_" + e)) for e in COMPUTE}
            for c in self.chans:
                c.sem = es.enter_context(nc.semaphore("ch_" + c.name))
            rank = {}
            for e in COMPUTE:
                rank[e] = {idx: i + 1 for i, idx in enumerate(sorted(self.waited[e]))}
            block = es.enter_context(nc.Block())

            def run(eng, name):
                mine = rank.get(name, {})
                for idx, r in enumerate(self.ins[name]):
                    for key, val in r["deps"]:
                        if key[0] == "eng":
                            eng.wait_ge(esem[key[1]], rank[key[1]][val])
                        else:
                            eng.wait_ge(key[1].sem, val)
                    if r["fn"] is None:
                        continue
                    ins = r["fn"](eng)
                    if r["chan"] is not None:
                        ins.then_inc(r["chan"].sem, 16)
                    elif idx in mine:
                        ins.then_inc(esem[name], 1)

            for name in QUEUES:
                if self.ins[name]:
                    getattr(block, engobj[name])(lambda eng, name=name: run(eng, name))
        return {e: len(self.ins[e]) for e in QUEUES}


def build_program(stop_after=None, ngrp=NGRP, noffn=False, dump=False):
    order = ["l0", "l1", "kv", "l2", "l3"]
    last = order.index(stop_after) if stop_after else len(order) - 1
    do = {s: (order.index(s) <= last) for s in order}

    nc = bass.Bass("TRN2", target_bir_lowering=False)
    S = Sched()
    marks = []

    def mark(label):
        marks.append((len(S.ins["pe"]), label))

    def dram(name, shape, kind="ExternalInput", dt=F32):
        return nc.dram_tensor(name, list(shape), dt, kind=kind).ap()

    x_d = dram("x", [2560, 1024])
    out_d = dram("out", [2048, 1024], kind="ExternalOutput")
    hv_d = dram("hv", [128, 1])
    gT_d = dram("gT", [128, 136])
    ng_d = dram("norm_g", [16, 1024])
    win_d = dram("a_w_in", [2, 1024, 2048])
    lng_d = dram("a_ln_g", [2, 1024])
    lnb_d = dram("a_ln_b", [2, 1024])
    wsT_d = dram("wsT", [2, 128, 1024])
    bs_d = dram("bs", [2, 1024])
    wout_d = dram("a_w_out", [2, 1024, 1024])
    wk_d = dram("w_k", [1024, 1024])
    wv_d = dram("w_v", [1024, 1024])
    wq_d = dram("b_w_q", [2, 1024, 1024])
    wo_d = dram("b_w_o", [2, 1024, 1024])
    rb_d = dram("rb", [2, 128, 8192])
    ch_d = dram("ch", [128, 32])
    wup_d = dram("w_up", [4, 1024, 4096])
    wdn_d = dram("w_down", [4, 4096, 1024])
    id_d = dram("ident", [128, 128])
    if dump:
        dbg = {
            "hnT": dram("dbg_hnT", [128, 4096], kind="ExternalOutput", dt=BF16),
            "uT": dram("dbg_uT", [128, 4096], kind="ExternalOutput", dt=BF16),
            "vF": dram("dbg_vF", [128, 4096], kind="ExternalOutput", dt=F32),
            "vn": dram("dbg_vn", [128, 4096], kind="ExternalOutput", dt=BF16),
            "gu": dram("dbg_gu", [128, 4096], kind="ExternalOutput", dt=BF16),
        }

    def sb(name, shape, dt):
        return nc.alloc_sbuf_tensor(name, list(shape), dt)

    hcur = sb("hcur", [128, 4, 1024], F32)
    KT = [sb(f"KT{i}", [128, 8, 512], BF16) for i in range(2)]
    VA = [sb(f"VA{i}", [128, 4, 16, 65], BF16) for i in range(2)]
    hnT = sb("hnT", [128, 8, 512], BF16)
    actT = sb("actT", [128, 8, 512], BF16)
    X = sb("X", [128, 16384], BF16)
    qT = [sb("qTA", [128, 8, 512], BF16), sb("qTB", [128, 8, 512], BF16)]
    hs = [sb(f"hs{i}", [128, 1024], BF16) for i in range(NHS)]
    pnt = [sb(f"pnt{i}", [128, 1024], F32) for i in range(2)]
    rtmp = [sb(f"rtmp{i}", [128, 512], BF16) for i in range(2)]
    PT = [sb(f"PT{i}", [128, 640], BF16) for i in range(3)]
    stmp = [sb(f"stmp{i}", [128, 512], F32) for i in range(2)]
    ob = [sb(f"ob{i}", [128, 1024], BF16) for i in range(2)]
    junk = ob[0]
    stat = sb("stat", [128, 256], F32)
    gT = sb("gTs", [128, 136], F32)
    gb = [sb(f"gb{i}", [128, 1024], F32) for i in range(2)]
    lnt = [sb(f"lnt{i}", [128, 1024], F32) for i in range(2)]
    wsT = sb("wsTs", [128, 2, 8, 128], BF16)
    bsrow = sb("bsrow", [1, 1024], BF16)
    ch = sb("chs", [128, 32], F32)
    ident = sb("identb", [128, 128], BF16)
    ones_row = sb("ones_row", [1, 128], BF16)
    hv = sb("hvs", [128, 1], F32)
    lnstat = [sb(f"lnstat{i}", [128, 12], F32) for i in range(2)]
    slots = [sb(f"slot{i}", [128, 4096], BF16) for i in range(RING)]

    hidT = X[:, :].rearrange("p (j t) -> p j t", j=32)
    uT = X[:, 0:4096].rearrange("p (c t) -> p c t", c=8)
    vF = X[:, 4096:12288].bitcast(F32).rearrange("p (t d) -> p t d", t=4)
    vn = X[:, 12288:16384].rearrange("p (t d) -> p t d", t=4)
    Rb = X[:, :].bitcast(F32).rearrange("p (h x) -> p h x", h=16)
    def XK(i):
        return ("X", i)

    RB_KEYS = [("X", i) for i in range(16)]

    PS = nc.alloc_psum_tensor("PS", [128, 4096], F32)
    pairs = [PS[:, i * 1024:(i + 1) * 1024] for i in range(4)]

    def bank(b):
        return pairs[b // 2][:, (b % 2) * 512:(b % 2) * 512 + 512]

    def bkey(b):
        return ("ps", b)

    psT = bank(7).bitcast(BF16).rearrange("p (c t) -> p c t", c=8)
    psT6 = bank(6).bitcast(BF16).rearrange("p (c t) -> p c t", c=8)
    psT0 = bank(0).bitcast(BF16).rearrange("p (c t) -> p c t", c=8)

    c_stage = [S.chan("stg0"), S.chan("stg1")]
    c_x = [S.chan(f"x{t}") for t in range(4)]
    c_out = [S.chan(f"o{t}") for t in range(4)]
    c_gb = [S.chan("gb0"), S.chan("gb1")]
    c_ln = [S.chan("ln0"), S.chan("ln1")]
    c_rb = S.chan("rb")
    c_bs = [S.chan("bs0"), S.chan("bs1")]
    c_dbg = S.chan("dbg")
    c_slot = [S.chan(f"slot{i}") for i in range(RING)]

    def panel(w2d, s):
        return ("panel", w2d.rearrange("(kc p) n -> p kc n", p=128)[:, :, s * 512:(s + 1) * 512])

    def dslab(w2d, s):
        return ("down", w2d.rearrange("(j p) n -> p j n", p=128)[:, 4 * s:4 * s + 4, :])

    plan = []
    for g in range(ngrp):
        for l in (0, 1):
            if not do[f"l{l}"]:
                continue
            plan += [panel(win_d[l], s) for s in (2, 3, 0, 1)]
            plan += [panel(wout_d[l], s) for s in range(2)]
            if not noffn:
                plan += [panel(wup_d[l], s) for s in range(8)]
                plan += [dslab(wdn_d[l], s) for s in range(8)]
        if do["kv"]:
            plan += [panel(wk_d, s) for s in range(2)]
            plan += [panel(wv_d, s) for s in range(2)]
        if g >= 1:
            for l in (2, 3):
                if not do[f"l{l}"]:
                    continue
                plan += [panel(wq_d[l - 2], s) for s in range(2)]
                plan += [panel(wo_d[l - 2], s) for s in range(2)]
                plan += [panel(wup_d[l], s) for s in range(8)]
                plan += [dslab(wdn_d[l], s) for s in range(8)]

    ring_state = dict(issued=0, taken=0)

    def ring_issue():
        n = ring_state["issued"]
        if n >= len(plan):
            return
        ring_state["issued"] = n + 1
        kind, src = plan[n]
        si = n % RING
        if kind == "panel":
            dst = slots[si][:, :].rearrange("p (a b) -> p a b", a=8)
        else:
            dst = slots[si][:, :].rearrange("p (a b) -> p a b", a=4)
        S.dma("pool", lambda e, dst=dst, src=src: e.dma_start(out=dst, in_=src), c_slot[si],
              writes=[("w", si)])

    def ring_next(kind):
        n = ring_state["taken"]
        ring_state["taken"] = n + 1
        assert plan[n][0] == kind, (n, plan[n][0], kind)
        si = n % RING
        if kind == "panel":
            v = slots[si][:, :].rearrange("p (a b) -> p a b", a=8)
        else:
            v = slots[si][:, :].rearrange("p (a b) -> p a b", a=4)
        return v, ("w", si)

    def ring_release():
        ring_issue()

    stat_i = [0]

    def newcols():
        i = stat_i[0] % 8
        stat_i[0] += 1
        return stat[:, i * 32:(i + 1) * 32], ("stat", i)

    def emit_rsqrt(xin, xin_key, pre_scale, eps, out, out_key, n=1):
        c, k = newcols()
        a, y, p, q = c[:, 0:n], c[:, 8:8 + n], c[:, 12:12 + n], c[:, 16:16 + n]
        S.op("dve", lambda e: e.tensor_scalar(out=a, in0=xin, scalar1=pre_scale, scalar2=eps,
                                              op0=ALU.mult, op1=ALU.add), reads=[xin_key], writes=[k])
        S.op("dve", lambda e: e.tensor_scalar(out=y.bitcast(I32), in0=a.bitcast(I32), scalar1=-0.5,
                                              scalar2=1597463007.0, op0=ALU.mult, op1=ALU.add),
             reads=[k], writes=[k])
        for it in range(2):
            S.op("dve", lambda e: e.tensor_tensor(out=p, in0=y, in1=y, op=ALU.mult), reads=[k], writes=[k])
            S.op("dve", lambda e: e.scalar_tensor_tensor(out=q, in0=p, scalar=-0.5, in1=a, op0=ALU.mult,
                                                         op1=ALU.mult), reads=[k], writes=[k])
            dst, dk = (y, k) if it == 0 else (out, out_key)
            S.op("dve", lambda e, dst=dst: e.scalar_tensor_tensor(out=dst, in0=q, scalar=1.5, in1=y, op0=ALU.add,
                                                                  op1=ALU.mult), reads=[k], writes=[dk])

    rot = dict(fm=0, tm=0, pair=0, hs=0, pnt=0, rt=0, lns=0, tb=0)

    def nxt(name, n):
        v = rot[name] % n
        rot[name] += 1
        return v

    def fm_bank():
        return 4 + nxt("fm", 3)

    def tm_bank():
        return nxt("tm", 4)

    def tm_pair():
        return nxt("pair", 2)

    def tile_cols(t):
        return slice(t * 128, (t + 1) * 128)

    def norm_part2(t, hidx, gcol, extra_gcol=None):
        tb = 7 - nxt("tb", 2)
        pT = psT if tb == 7 else psT6
        for fc in range(8):
            S.op("pe", lambda e, fc=fc: e.transpose(out=pT[:, fc, :], in_=hs[hidx][:, fc * 128:(fc + 1) * 128],
                                                    identity=ident[:, :]),
                 reads=[("hs", hidx), "ident"], writes=[bkey(tb)])
        gbc = gT[:, gcol:gcol + 8].unsqueeze(2).to_broadcast([128, 8, 128])
        S.op("dve", lambda e: e.tensor_tensor(out=hnT[:, :, tile_cols(t)], in0=pT, in1=gbc, op=ALU.mult),
             reads=[bkey(tb), "gT"], writes=[("hnT", t)])
        if extra_gcol is not None:
            gbc2 = gT[:, extra_gcol:extra_gcol + 8].unsqueeze(2).to_broadcast([128, 8, 128])
            S.op("dve", lambda e: e.tensor_tensor(out=actT[:, :, tile_cols(t)], in0=pT, in1=gbc2, op=ALU.mult),
                 reads=[bkey(tb), "gT"], writes=[("actT", t)])

    def norm_batch1(tl, gcol=None, extra_gcol=None):
        n = len(tl)
        c, k = newcols()
        for i, t in enumerate(tl):
            S.op("act", lambda e, i=i, t=t: e.activation(out=junk[:, :], in_=hcur[:, t, :], func=AF.Square,
                                                         accum_out=c[:, i:i + 1]),
                 reads=[("h", t)], writes=[("ob", 0), k])
        emit_rsqrt(c[:, 0:n], k, 1.0 / 1024, RMS_EPS, c[:, 4:4 + n], k, n)
        res = []
        for i, t in enumerate(tl):
            hidx = nxt("hs", NHS)
            S.op("act", lambda e, i=i, t=t, hidx=hidx: e.activation(out=hs[hidx][:, :], in_=hcur[:, t, :], func=AF.Copy,
                                                                    scale=c[:, 4 + i:5 + i]),
                 reads=[("h", t), k], writes=[("hs", hidx)])
            if gcol is not None and res and INTERLEAVE:
                norm_part2(res[-1][0], res[-1][1], gcol, extra_gcol)
            res.append((t, hidx))
        if gcol is not None:
            if INTERLEAVE:
                norm_part2(res[-1][0], res[-1][1], gcol, extra_gcol)
            else:
                for t, hidx in res:
                    norm_part2(t, hidx, gcol, extra_gcol)
        return res

    def norm_batch(tl, gcol, extra_gcol=None):
        norm_batch1(tl, gcol, extra_gcol)

    HN_ALL = [("hnT", t) for t in range(4)]
    ACT_ALL = [("actT", t) for t in range(4)]

    def postnorm_batch(tl, pis, gtile, gkey):
        n = len(tl)
        c, k = newcols()
        for i, (t, pi) in enumerate(zip(tl, pis)):
            S.op("act", lambda e, i=i, pi=pi: e.activation(out=junk[:, :], in_=pairs[pi][:, :], func=AF.Square,
                                                           accum_out=c[:, i:i + 1]),
                 reads=[bkey(2 * pi), bkey(2 * pi + 1)], writes=[("ob", 0), k])
        emit_rsqrt(c[:, 0:n], k, 1.0 / 1024, RMS_EPS, c[:, 4:4 + n], k, n)
        for i, (t, pi) in enumerate(zip(tl, pis)):
            pi2 = nxt("pnt", 2)
            S.op("dve", lambda e, i=i, pi=pi, pi2=pi2: e.scalar_tensor_tensor(
                out=pnt[pi2][:, :], in0=pairs[pi][:, :], scalar=c[:, 4 + i:5 + i], in1=gtile[:, :],
                op0=ALU.mult, op1=ALU.mult),
                reads=[bkey(2 * pi), bkey(2 * pi + 1), k, gkey], writes=[("pnt", pi2)])
            add_eng = "pool" if (POOL_ADD and i % 2 == 0) else "dve"
            S.op(add_eng, lambda e, t=t, pi2=pi2: e.tensor_tensor(out=hcur[:, t, :], in0=hcur[:, t, :], in1=pnt[pi2][:, :],
                                                                   op=ALU.add),
                 reads=[("h", t), ("pnt", pi2)], writes=[("h", t)])

    def load_bcast(dst, key, chan, row_ap):
        S.dma("sp", lambda e: e.dma_start(out=dst[:, :], in_=row_ap.partition_broadcast(128)), chan,
              writes=[key])

    def proj_tm_post(src_keys_fn, srcT, gtile, gkey, next_gcol):
        mark("proj")
        slabA, kA = ring_next("panel")
        slabB, kB = ring_next("panel")
        for t in range(4):
            for half, (slab, ks) in enumerate(((slabA, kA), (slabB, kB))):
                for kc in range(8):
                    S.op("pe", lambda e, kc=kc, slab=slab, half=half, t=t: e.matmul(
                        pairs[t][:, half * 512:(half + 1) * 512], lhsT=srcT[:, kc, tile_cols(t)],
                        rhs=slab[:, kc, :], start=(kc == 0), stop=(kc == 7)),
                        reads=[ks, src_keys_fn(t)], writes=[bkey(2 * t + half)])
        ring_release()
        ring_release()
        postnorm_batch([0, 1, 2, 3], [0, 1, 2, 3], gtile, gkey)
        if next_gcol is not None:
            norm_batch([0, 1, 2, 3], next_gcol)

    def ffn(l, next_gcol, extra_gcol=None):
        if noffn:
            return
        mark("up")
        for s in range(8):
            slab, ks = ring_next("panel")
            for j4 in range(4):
                j = s * 4 + j4
                b = fm_bank()
                for kc in range(8):
                    S.op("pe", lambda e, kc=kc, slab=slab, j4=j4, b=b: e.matmul(
                        bank(b), lhsT=slab[:, kc, j4 * 128:(j4 + 1) * 128], rhs=hnT[:, kc, :],
                        start=(kc == 0), stop=(kc == 7)),
                        reads=[ks] + HN_ALL, writes=[bkey(b)])
                ri = nxt("rt", 2)
                S.op("act", lambda e, b=b, ri=ri: e.activation(out=rtmp[ri][:, :], in_=bank(b), func=AF.Relu),
                     reads=[bkey(b)], writes=[("rt", ri)])
                S.op("dve", lambda e, j=j, ri=ri: e.tensor_tensor(out=hidT[:, j, :], in0=rtmp[ri][:, :],
                                                                   in1=rtmp[ri][:, :], op=ALU.mult),
                     reads=[("rt", ri)], writes=[XK(j // 2)])
            ring_release()
        mark("down")
        dpair = [3, 2, 1, 0]
        for s in range(8):
            slab, ks = ring_next("down")
            for t in range(4):
                pi = dpair[t]
                for half in range(2):
                    for jj in range(4):
                        j = 4 * s + jj
                        S.op("pe", lambda e, slab=slab, jj=jj, j=j, half=half, pi=pi, t=t, s=s: e.matmul(
                            pairs[pi][:, half * 512:(half + 1) * 512], lhsT=hidT[:, j, tile_cols(t)],
                            rhs=slab[:, jj, half * 512:(half + 1) * 512],
                            start=(s == 0 and jj == 0), stop=(s == 7 and jj == 3)),
                            reads=[ks, XK(j // 2)], writes=[bkey(2 * pi + half)])
            ring_release()
        mark("post")
        postnorm_batch([0, 1, 2, 3], dpair, gb[1], "gb1")
        if next_gcol is not None:
            norm_batch([0, 1, 2, 3], next_gcol, extra_gcol)

    def dump_buf(name, src_ap, keys):
        S.dma("sp", lambda e: e.dma_start(out=dbg[name][:, :], in_=src_ap), c_dbg, reads=keys)

    def a_layer(g, l, next_gcol, extra_gcol=None):
        dd = dump and g == 1 and l == 0
        mark(f"A{l} g{g} v")
        if dd:
            dump_buf("hnT", hnT[:, :, :].rearrange("p a b -> p (a b)"), HN_ALL)
        load_bcast(gb[0], "gb0", c_gb[0], ng_d[l * 4 + 1:l * 4 + 2, :])
        load_bcast(gb[1], "gb1", c_gb[1], ng_d[l * 4 + 3:l * 4 + 4, :])
        load_bcast(lnt[0], "ln0", c_ln[0], lng_d[l:l + 1, :])
        load_bcast(lnt[1], "ln1", c_ln[1], lnb_d[l:l + 1, :])
        for hf in range(2):
            S.dma("sp", lambda e, hf=hf: e.dma_start(out=stmp[hf][0:1, :], in_=bs_d[l:l + 1, hf * 512:(hf + 1) * 512]),
                  c_bs[hf], writes=[("stmp", hf)])
            S.op("act", lambda e, hf=hf: e.activation(out=bsrow[0:1, hf * 512:(hf + 1) * 512], in_=stmp[hf][0:1, :],
                                                      func=AF.Copy),
                 reads=[("stmp", hf)], writes=["bsrow"])
        for s in range(2):
            slab, ks = ring_next("panel")
            for t in range(4):
                b = tm_bank()
                for kc in range(8):
                    S.op("pe", lambda e, kc=kc, slab=slab, t=t, b=b: e.matmul(
                        bank(b), lhsT=hnT[:, kc, tile_cols(t)], rhs=slab[:, kc, :],
                        start=(kc == 0), stop=(kc == 7)),
                        reads=[ks, ("hnT", t)], writes=[bkey(b)])
                S.op("act", lambda e, b=b, t=t, s=s: e.activation(out=vF[:, t, s * 512:(s + 1) * 512], in_=bank(b),
                                                                func=AF.Gelu_apprx_tanh),
                     reads=[bkey(b)], writes=[XK(4 + 2 * t + s)])
            ring_release()
        c, k = newcols()
        mv = c[:, 0:8].rearrange("p (t two) -> p t two", two=2)
        for t in range(4):
            li = nxt("lns", 2)
            S.op("dve", lambda e, t=t, li=li: e.bn_stats(out=lnstat[li][:, 0:6], in_=vF[:, t, 0:512]),
                 reads=[XK(4 + 2 * t)], writes=[("lnstat", li)])
            S.op("dve", lambda e, t=t, li=li: e.bn_stats(out=lnstat[li][:, 6:12], in_=vF[:, t, 512:1024]),
                 reads=[XK(5 + 2 * t)], writes=[("lnstat", li)])
            S.op("dve", lambda e, t=t, li=li: e.bn_aggr(out=c[:, 2 * t:2 * t + 2], in_=lnstat[li][:, 0:12]),
                 reads=[("lnstat", li)], writes=[k])
        emit_rsqrt(mv[:, :, 1], k, 1.0, LN_EPS, c[:, 8:12], k, 4)
        S.op("dve", lambda e: e.scalar_tensor_tensor(out=c[:, 12:16], in0=mv[:, :, 0], scalar=-1.0, in1=c[:, 8:12],
                                                     op0=ALU.mult, op1=ALU.mult),
             reads=[k], writes=[k])
        def ln_part_b(t):
            pi2 = nxt("pnt", 2)
            S.op("act", lambda e, t=t, pi2=pi2: e.activation(out=pnt[pi2][:, :], in_=vF[:, t, :], func=AF.Identity,
                                                              scale=c[:, 8 + t:9 + t], bias=c[:, 12 + t:13 + t]),
                 reads=[XK(4 + 2 * t), XK(5 + 2 * t), k], writes=[("pnt", pi2)])
            leng = "dve" if t < 2 else "pool"
            S.op(leng, lambda e, pi2=pi2: e.tensor_tensor(out=pnt[pi2][:, :], in0=pnt[pi2][:, :], in1=lnt[0][:, :],
                                                           op=ALU.mult),
                 reads=[("pnt", pi2), "ln0"], writes=[("pnt", pi2)])
            S.op(leng, lambda e, pi2=pi2, t=t: e.tensor_tensor(out=vn[:, t, :], in0=pnt[pi2][:, :], in1=lnt[1][:, :],
                                                                op=ALU.add),
                 reads=[("pnt", pi2), "ln1"], writes=[XK(12 + t)])
        mark("u")
        for s in range(2):
            slab, ks = ring_next("panel")
            for u4 in range(4):
                uc = s * 4 + u4
                b = fm_bank()
                for kc in range(8):
                    S.op("pe", lambda e, kc=kc, slab=slab, u4=u4, b=b: e.matmul(
                        bank(b), lhsT=slab[:, kc, u4 * 128:(u4 + 1) * 128], rhs=hnT[:, kc, :],
                        start=(kc == 0), stop=(kc == 7)),
                        reads=[ks] + HN_ALL, writes=[bkey(b)])
                S.op("act", lambda e, b=b, uc=uc: e.activation(out=uT[:, uc, :], in_=bank(b), func=AF.Gelu_apprx_tanh),
                     reads=[bkey(b)], writes=[XK(uc // 2)])
                if s == 1:
                    ln_part_b(u4)
            ring_release()
        if dd:
            dump_buf("uT", X[:, 0:4096], [XK(i) for i in range(4)])
            dump_buf("vF", X[:, 4096:12288].bitcast(F32), [XK(i) for i in range(4, 12)])
        if dd:
            dump_buf("vn", X[:, 12288:16384], [XK(i) for i in range(12, 16)])
        mark("gate")
        slabA, kA = ring_next("panel")
        slabB, kB = ring_next("panel")
        ffn_gcol = (l * 4 + 2) * 8
        gpair = [3, 2, 3, 2]
        ppairs = [0, 1, 3, 2]

        def gate_tile(t):
            gp = gpair[t]
            for half in range(2):
                bk = 2 * gp + half
                S.op("pe", lambda e, half=half, bk=bk: e.matmul(
                    bank(bk), lhsT=ones_row[0:1, :], rhs=bsrow[0:1, half * 512:(half + 1) * 512],
                    start=True, stop=False),
                    reads=["ones", "bsrow"], writes=[bkey(bk)])
                for g4 in range(4):
                    gi = half * 4 + g4
                    S.op("pe", lambda e, gi=gi, g4=g4, bk=bk: e.matmul(
                        bank(bk)[:, g4 * 128:(g4 + 1) * 128], lhsT=vn[:, t, gi * 128:(gi + 1) * 128],
                        rhs=wsT[:, l, gi, :], start=False, stop=(g4 == 3)),
                        reads=[XK(12 + t), "wsT"], writes=[bkey(bk)])
            gview = pairs[gp].rearrange("p (g i) -> p g i", g=8)
            S.op("dve", lambda e: e.tensor_tensor(out=actT[:, :, tile_cols(t)], in0=gview,
                                                  in1=uT[:, :, tile_cols(t)], op=ALU.mult),
                 reads=[bkey(2 * gp), bkey(2 * gp + 1)] + [XK(i) for i in range(4)], writes=[("actT", t)])

        def proj_tile(t):
            pi = ppairs[t]
            for half, (slab, ks) in enumerate(((slabA, kA), (slabB, kB))):
                for kc in range(8):
                    S.op("pe", lambda e, kc=kc, slab=slab, half=half: e.matmul(
                        pairs[pi][:, half * 512:(half + 1) * 512], lhsT=actT[:, kc, tile_cols(t)],
                        rhs=slab[:, kc, :], start=(kc == 0), stop=(kc == 7)),
                        reads=[ks, ("actT", t)], writes=[bkey(2 * pi + half)])

        gate_tile(0)
        gate_tile(1)
        proj_tile(0)
        gate_tile(2)
        proj_tile(1)
        gate_tile(3)
        proj_tile(2)
        proj_tile(3)
        ring_release()
        ring_release()
        postnorm_batch([0, 1, 2, 3], ppairs, gb[0], "gb0")
        norm_batch([0, 1, 2, 3], ffn_gcol)
        ffn(l, next_gcol, extra_gcol)

    def kv_stage(g):
        mark(f"KV g{g}")
        cur = g % 2
        for s in range(2):
            slab, ks = ring_next("panel")
            for f4 in range(4):
                fc = s * 4 + f4
                b = fm_bank()
                for kc in range(8):
                    S.op("pe", lambda e, kc=kc, slab=slab, f4=f4, b=b: e.matmul(
                        bank(b), lhsT=slab[:, kc, f4 * 128:(f4 + 1) * 128], rhs=hnT[:, kc, :],
                        start=(kc == 0), stop=(kc == 7)),
                        reads=[ks] + HN_ALL, writes=[bkey(b)])
                S.op("act", lambda e, b=b, fc=fc: e.activation(out=KT[cur][:, fc, :], in_=bank(b), func=AF.Copy),
                     reads=[bkey(b)], writes=[("KT", cur)])
            ring_release()
        for s in range(2):
            slab, ks = ring_next("panel")
            for t in range(4):
                b = tm_bank()
                for kc in range(8):
                    S.op("pe", lambda e, kc=kc, slab=slab, t=t, b=b: e.matmul(
                        bank(b), lhsT=hnT[:, kc, tile_cols(t)], rhs=slab[:, kc, :],
                        start=(kc == 0), stop=(kc == 7)),
                        reads=[ks, ("hnT", t)], writes=[bkey(b)])
                src = bank(b).rearrange("p (h d) -> p h d", h=8)
                dst = VA[cur][:, t, s * 8:(s + 1) * 8, 0:64]
                if g == 0:
                    S.op("dve", lambda e, src=src, dst=dst: e.tensor_scalar(out=dst, in0=src, scalar1=hv[:, 0:1],
                                                                            scalar2=None, op0=ALU.mult),
                         reads=[bkey(b), "hv"], writes=[("VA", cur, t)])
                else:
                    S.op("act", lambda e, src=src, dst=dst: e.activation(out=dst, in_=src, func=AF.Copy),
                         reads=[bkey(b)], writes=[("VA", cur, t)])
            ring_release()
        for t in range(4):
            dst = VA[cur][:, t, :, 64:65]
            if g == 0:
                S.op("dve", lambda e, dst=dst: e.tensor_copy(out=dst, in_=hv[:, 0:1].unsqueeze(1).to_broadcast([128, 16, 1])),
                     reads=["hv"], writes=[("VA", cur, t)])
            else:
                S.op("dve", lambda e, dst=dst: e.memset(dst, 1.0), writes=[("VA", cur, t)])

    def b_layer(g, l, next_gcol, qsrc=None):
        mark(f"B{l} g{g} q")
        j = l - 2
        prev, cur = (g - 1) % 2, g % 2
        load_bcast(gb[0], "gb0", c_gb[0], ng_d[l * 4 + 1:l * 4 + 2, :])
        load_bcast(gb[1], "gb1", c_gb[1], ng_d[l * 4 + 3:l * 4 + 4, :])
        S.dma("sp", lambda e: e.dma_start(out=X[:, :].bitcast(F32), in_=rb_d[j]), c_rb,
              writes=RB_KEYS)
        qs, qkeys = (actT, ACT_ALL) if qsrc == "actT" else (hnT, HN_ALL)
        for s in range(2):
            slab, ks = ring_next("panel")
            for f4 in range(4):
                fc = s * 4 + f4
                b = fm_bank()
                for kc in range(8):
                    S.op("pe", lambda e, kc=kc, slab=slab, f4=f4, b=b: e.matmul(
                        bank(b), lhsT=slab[:, kc, f4 * 128:(f4 + 1) * 128], rhs=qs[:, kc, :],
                        start=(kc == 0), stop=(kc == 7)),
                        reads=[ks] + qkeys, writes=[bkey(b)])
                S.op("act", lambda e, b=b, fc=fc: e.activation(out=qT[0][0:64, fc, :], in_=bank(b)[0:64, :],
                                                                func=AF.Copy, scale=0.125),
                     reads=[bkey(b)], writes=["qTA"])
                S.op("dve", lambda e, b=b, fc=fc: e.tensor_scalar(out=qT[1][64:128, fc, :], in0=bank(b)[64:128, :],
                                                                   scalar1=0.125, scalar2=None, op0=ALU.mult),
                     reads=[bkey(b)], writes=["qTB"])
            ring_release()

        mark("attn")
        items = [(ti, h) for ti in range(4) for h in range(16)]
        ssets = [(1, 2), (3, 4), (5, 6)]
        PCOL = {0: 0, 2: 1, 3: 2, 4: 3, 1: 4}
        PASSES = [(0, 7), (7, 7), (14, 2)]

        def obank(ti, ps_i):
            return (0, 7)[(ti + ps_i) % 2]

        def opass(h):
            return 0 if h < 7 else (1 if h < 14 else 2)

        def win(ti, kb):
            w = ti + kb
            return (prev if w < 4 else cur), w % 4

        def emit_S(n):
            ti, h = items[n]
            fc = h // 2
            sa, sbk = ssets[n % 3]
            q = qT[h % 2][:, fc, tile_cols(ti)]
            for kb in range(5):
                buf, wt = win(ti, kb)
                if kb == 1:
                    o = bank(sbk)[:, 0:128]
                    ok = bkey(sbk)
                else:
                    o = bank(sa)[:, PCOL[kb] * 128:(PCOL[kb] + 1) * 128]
                    ok = bkey(sa)
                S.op("pe", lambda e, o=o, buf=buf, fc=fc, wt=wt, q=q: e.matmul(
                    o, lhsT=KT[buf][:, fc, tile_cols(wt)], rhs=q, start=True, stop=True),
                    reads=[("KT", buf), "qTA" if h % 2 == 0 else "qTB"], writes=[ok])

        def emit_soft_pv(n):
            ti, h = items[n]
            sa, sbk = ssets[n % 3]
            p = n % 3
            si = n % 2
            S.op("dve", lambda e: e.tensor_tensor(out=stmp[si][:, :], in0=bank(sa), in1=Rb[:, h, :], op=ALU.add),
                 reads=[bkey(sa)] + RB_KEYS, writes=[("stmp", si)])
            S.op("act", lambda e: e.activation(out=PT[p][:, 512:640], in_=bank(sbk)[:, 0:128], func=AF.Exp,
                                               bias=ch[:, j * 16 + h:j * 16 + h + 1]),
                 reads=[bkey(sbk), "ch"], writes=[("PT", p)])
            S.op("act", lambda e: e.activation(out=PT[p][:, 0:512], in_=stmp[si][:, :], func=AF.Exp),
                 reads=[("stmp", si)], writes=[("PT", p)])
            h0, nh = PASSES[opass(h)]
            obk = obank(ti, opass(h))
            oo = (h - h0) * 65
            for kb in range(5):
                buf, wt = win(ti, kb)
                pc = PCOL[kb]
                S.op("pe", lambda e, kb=kb, buf=buf, wt=wt, pc=pc: e.matmul(
                    bank(obk)[:, oo:oo + 65], lhsT=PT[p][:, pc * 128:(pc + 1) * 128], rhs=VA[buf][:, wt, h, :],
                    start=(kb == 0), stop=(kb == 4)),
                    reads=[("PT", p), ("VA", buf, wt)], writes=[bkey(obk)])

        def emit_norm_pass(ti, oi, ps_i):
            h0, nh = PASSES[ps_i]
            obk = obank(ti, ps_i)
            c, k = newcols()
            Ob = bank(obk)[:, 0:nh * 65].rearrange("p (h d) -> p h d", h=nh)
            S.op("dve", lambda e: e.reciprocal(out=c[:, 0:nh].unsqueeze(2), in_=Ob[:, :, 64:65]),
                 reads=[bkey(obk)], writes=[k])
            dst = ob[oi][:, h0 * 64:(h0 + nh) * 64].rearrange("p (h d) -> p h d", h=nh)
            S.op("dve", lambda e: e.tensor_tensor(
                out=dst, in0=Ob[:, :, 0:64], in1=c[:, 0:nh].unsqueeze(2).to_broadcast([128, nh, 64]), op=ALU.mult),
                reads=[bkey(obk), k], writes=[("ob", oi)])

        def emit_oT(ti, oi):
            tb = 0 if ti % 2 == 0 else 7
            pT = psT0 if tb == 0 else psT
            for fc in range(8):
                S.op("pe", lambda e, fc=fc: e.transpose(out=pT[:, fc, :], in_=ob[oi][:, fc * 128:(fc + 1) * 128],
                                                        identity=ident[:, :]),
                     reads=[("ob", oi), "ident"], writes=[bkey(tb)])
            S.op("act", lambda e: e.activation(out=actT[:, :, tile_cols(ti)], in_=pT, func=AF.Copy),
                 reads=[bkey(tb)], writes=[("actT", ti)])

        emit_S(0)
        emit_S(1)
        pend_T = None
        pend_norm = []
        for n in range(len(items)):
            ti, h = items[n]
            if n + 2 < len(items):
                emit_S(n + 2)
            emit_soft_pv(n)
            while pend_norm and pend_norm[0][0] <= n:
                pend_norm.pop(0)[1]()
            if pend_T is not None and h == 3:
                pend_T()
                pend_T = None
            if h in (6, 13, 15):
                pend_norm.append((n + 2, lambda ti=ti, ps_i=opass(h): emit_norm_pass(ti, ti % 2, ps_i)))
            if h == 15:
                pend_T = (lambda ti=ti, oi=ti % 2: emit_oT(ti, oi))
        for _, f in pend_norm:
            f()
        pend_T()
        proj_tm_post(lambda t: ("actT", t), actT, gb[0], "gb0", (l * 4 + 2) * 8)
        ffn(l, next_gcol)

    S.dma("sp", lambda e: e.dma_start(out=gT[:, :], in_=gT_d[:, :]), S.chan("c_gT"), writes=["gT"])
    S.dma("sp", lambda e: e.dma_start(out=ch[:, :], in_=ch_d[:, :]), S.chan("c_ch"), writes=["ch"])
    S.dma("sp", lambda e: e.dma_start(out=hv[:, :], in_=hv_d[:, :]), S.chan("c_hv"), writes=["hv"])
    S.dma("sp", lambda e: e.dma_start(out=pnt[0][:, 0:128], in_=id_d[:, :]), c_stage[0], writes=[("pnt", 0)])
    for i in range(min(RING, len(plan))):
        ring_issue()
    S.op("act", lambda e: e.activation(out=ident[:, :], in_=pnt[0][:, 0:128], func=AF.Copy),
         reads=[("pnt", 0)], writes=["ident"])
    S.op("dve", lambda e: e.memset(ones_row[:, :], 1.0), writes=["ones"])
    S.op("dve", lambda e: e.memset(qT[0][:, :, :], 0.0), writes=["qTA"])
    S.op("dve", lambda e: e.memset(qT[1][:, :, :], 0.0), writes=["qTB"])
    for l in range(2):
        S.dma("sp", lambda e, l=l: e.dma_start(out=pnt[l][:, :], in_=wsT_d[l]), c_stage[l], writes=[("pnt", l)])
        stg = pnt[l][:, :].rearrange("p (g i) -> p g i", g=8)
        S.op("dve", lambda e, stg=stg: e.memset(stg[64:128, :, 0:64], 0.0), reads=[("pnt", l)], writes=[("pnt", l)])
        S.op("act", lambda e, stg=stg, l=l: e.activation(out=wsT[:, l, :, :], in_=stg, func=AF.Copy),
             reads=[("pnt", l)], writes=["wsT"])

    def first_stage_gcol(after):
        seq = [s for s in order if do[s]]
        return seq

    for g in range(ngrp):
        for t in range(4):
            S.dma("sp", lambda e, g=g, t=t: e.dma_start(out=hcur[:, t, :], in_=x_d[g * 512 + t * 128:g * 512 + (t + 1) * 128, :]),
                  c_x[t], writes=[("h", t)])
        stages = [s for s in order if do[s] and (g >= 1 or s in ("l0", "l1", "kv"))]
        gcol_of = {"l0": 0, "l1": 32, "kv": 128, "l2": 64, "l3": 96}
        norm_batch([0, 1, 2, 3], gcol_of[stages[0]])
        for si, st in enumerate(stages):
            nxt_st = stages[si + 1] if si + 1 < len(stages) else None
            ngc = gcol_of[nxt_st] if nxt_st else None
            nn_st = stages[si + 2] if si + 2 < len(stages) else None
            if st in ("l0", "l1"):
                a_layer(g, int(st[1]), ngc, gcol_of[nn_st] if (nxt_st == "kv" and nn_st) else None)
            elif st == "kv":
                kv_stage(g)
                if ngc is not None and si == 0:
                    norm_batch([0, 1, 2, 3], ngc)
            else:
                shared = si >= 2 and stages[si - 1] == "kv" and stages[si - 2] in ("l0", "l1")
                b_layer(g, int(st[1]), ngc, "actT" if shared else None)
        if g >= 1:
            for t in range(4):
                S.dma("sp", lambda e, g=g, t=t: e.dma_start(
                    out=out_d[(g - 1) * 512 + t * 128:(g - 1) * 512 + (t + 1) * 128, :], in_=hcur[:, t, :]),
                    c_out[t], reads=[("h", t)])
    S.wait_only("sp", writes=[("h", t) for t in range(4)])
    if dump:
        S.wait_only("sp", writes=HN_ALL + ACT_ALL + [XK(i) for i in range(16)])
    counts = S.emit(nc)
    counts["marks"] = marks
    return nc, counts


def prep_inputs(x, norm_g, a_w_in, a_ln_g, a_ln_b, a_w_s, a_b_s, a_w_out, kv_norm_g, w_k, w_v,
                b_w_q, b_rel_bias, b_w_o, w_up, w_down):
    f = lambda a: np.ascontiguousarray(np.asarray(a, dtype=np.float32))
    x = f(x)
    norm_g = f(norm_g)
    gT = np.empty((128, 136), np.float32)
    gT[:, 0:128] = norm_g.reshape(16, 8, 128).transpose(2, 0, 1).reshape(128, 128)
    gT[:, 128:136] = f(kv_norm_g).reshape(8, 128).T
    wsT = f(np.asarray(a_w_s).transpose(0, 3, 1, 2).reshape(2, 128, 1024))
    bs = f(np.asarray(a_b_s).reshape(2, 1024))
    tab = f(b_rel_bias)
    kp = np.arange(128)[:, None, None]
    kb = np.array([0, 2, 3, 4])[None, :, None]
    q = np.arange(128)[None, None, :]
    idx = np.clip(q + 512 - (128 * kb + kp), -256, 256) + 256
    rb = tab[:, :, idx]
    rb = np.ascontiguousarray(rb.transpose(0, 2, 1, 3, 4))
    rb[:, 64:128, :, 3, 0:64] = -30000.0
    rb[:, 0:64, :, 0, 64:128] = -30000.0
    rb = rb.reshape(2, 128, 8192)
    chc = np.ascontiguousarray(np.broadcast_to(tab[:, :, 512].reshape(1, 32), (128, 32)))
    shared = {
        "gT": gT, "norm_g": norm_g.reshape(16, 1024), "a_w_in": f(a_w_in), "a_ln_g": f(a_ln_g),
        "a_ln_b": f(a_ln_b), "wsT": wsT, "bs": bs, "a_w_out": f(a_w_out), "w_k": f(w_k), "w_v": f(w_v),
        "b_w_q": f(b_w_q), "b_w_o": f(b_w_o), "rb": rb, "ch": chc, "w_up": f(w_up), "w_down": f(w_down),
        "ident": np.eye(128, dtype=np.float32),
    }
    in_maps = []
    for c in range(8):
        b, half = c // 2, c % 2
        xc = np.zeros((2560, 1024), np.float32)
        if half == 0:
            xc[512:] = x[b, 0:2048]
        else:
            xc[:] = x[b, 1536:4096]
        m = dict(shared)
        m["x"] = xc
        m["hv"] = np.full((128, 1), float(half), np.float32)
        in_maps.append(m)
    return in_maps


_CACHE = {}


def run(inputs, stop_after=None, trace=False, noffn=False, dump=False, ngrp=NGRP):
    key = (stop_after, noffn, dump, ngrp)
    if key not in _CACHE:
        _CACHE[key] = build_program(stop_after, noffn=noffn, dump=dump, ngrp=ngrp)
    nc, counts = _CACHE[key]
    in_maps = prep_inputs(**inputs)
    res = run_bass_kernel_spmd(nc, in_maps, core_ids=list(range(8)), trace=trace)
    out = np.empty((4, 4096, 1024), np.float32)
    for c in range(8):
        b, half = c // 2, c % 2
        out[b, half * 2048:(half + 1) * 2048] = res.results[c]["out"]
    return out, res


def kernel(**inputs):
    out, _ = run(inputs)
    return out
```

```python
from contextlib import ExitStack

import os
import numpy as np
import concourse.bass as bass
import concourse.mybir as mybir
from concourse.bass_utils import run_bass_kernel_spmd

F32 = mybir.dt.float32
BF16 = mybir.dt.bfloat16
I32 = mybir.dt.int32
AF = mybir.ActivationFunctionType
ALU = mybir.AluOpType

COMPUTE = ("pe", "act", "dve", "pool")
QUEUES = ("pe", "act", "dve", "pool", "sp")
NGRP = 5
RING = int(os.environ.get('K_RING', '5'))
NHS = int(os.environ.get('K_NHS', '4'))
INTERLEAVE = os.environ.get('K_INTER', '0') == '1'
POOL_ADD = os.environ.get('K_POOLADD', '1') == '1'
BIAS_BCAST = os.environ.get('K_BIASB', '1') == '1'
RMS_EPS = 1e-6
LN_EPS = 1e-5


class Chan:
    def __init__(self, name):
        self.name = name
        self.count = 0
        self.sem = None


class Sched:
    def __init__(self):
        self.ins = {e: [] for e in QUEUES}
        self.last_w = {}
        self.readers = {}
        self.seen = {e: {} for e in QUEUES}
        self.chans = []
        self.waited = {e: set() for e in COMPUTE}

    def chan(self, name):
        c = Chan(name)
        self.chans.append(c)
        return c

    def _collect(self, eng, reads, writes):
        need = {}

        def add(tok, is_raw):
            if tok[0] == "eng":
                _, e, idx = tok
                if e == eng and eng in ("pe", "sp"):
                    return
                key = ("eng", e)
                if need.get(key, -1) < idx:
                    need[key] = idx
            else:
                _, ch, cnt = tok
                key = ("dma", ch)
                if need.get(key, -1) < cnt:
                    need[key] = cnt

        for k in reads:
            t = self.last_w.get(k)
            if t is not None:
                add(t, True)
        for k in writes:
            t = self.last_w.get(k)
            if t is not None:
                add(t, True)
            for r in self.readers.get(k, ()):
                add(r, False)
        deps = []
        seen = self.seen[eng]
        for key, val in need.items():
            if seen.get(key, -1) >= val:
                continue
            seen[key] = val
            deps.append((key, val))
            if key[0] == "eng":
                self.waited[key[1]].add(val)
        return deps

    def _record(self, tok, reads, writes):
        for k in writes:
            self.last_w[k] = tok
            self.readers[k] = []
        for k in reads:
            lst = self.readers.setdefault(k, [])
            src = tok[:2]
            lst[:] = [r for r in lst if r[:2] != src]
            lst.append(tok)

    def op(self, eng, fn, reads=(), writes=()):
        idx = len(self.ins[eng])
        deps = self._collect(eng, reads, writes)
        self.ins[eng].append(dict(fn=fn, deps=deps, chan=None))
        self._record(("eng", eng, idx), reads, writes)
        return idx

    def dma(self, queue, fn, chan, reads=(), writes=()):
        deps = self._collect(queue, reads, writes)
        chan.count += 16
        self.ins[queue].append(dict(fn=fn, deps=deps, chan=chan))
        self._record(("dma", chan, chan.count), reads, writes)

    def wait_only(self, queue, reads=(), writes=()):
        deps = self._collect(queue, reads, writes)
        self.ins[queue].append(dict(fn=None, deps=deps, chan=None))

    def emit(self, nc):
        engobj = {"pe": "tensor", "act": "scalar", "dve": "vector", "pool": "gpsimd", "sp": "sync"}
        with ExitStack() as es:
            esem = {e: es.enter_context(nc.semaphore("prog_" + e)) for e in COMPUTE}
            for c in self.chans:
                c.sem = es.enter_context(nc.semaphore("ch_" + c.name))
            rank = {}
            for e in COMPUTE:
                rank[e] = {idx: i + 1 for i, idx in enumerate(sorted(self.waited[e]))}
            block = es.enter_context(nc.Block())

            def run(eng, name):
                mine = rank.get(name, {})
                for idx, r in enumerate(self.ins[name]):
                    for key, val in r["deps"]:
                        if key[0] == "eng":
                            eng.wait_ge(esem[key[1]], rank[key[1]][val])
                        else:
                            eng.wait_ge(key[1].sem, val)
                    if r["fn"] is None:
                        continue
                    ins = r["fn"](eng)
                    if r["chan"] is not None:
                        ins.then_inc(r["chan"].sem, 16)
                    elif idx in mine:
                        ins.then_inc(esem[name], 1)

            for name in QUEUES:
                if self.ins[name]:
                    getattr(block, engobj[name])(lambda eng, name=name: run(eng, name))
        return {e: len(self.ins[e]) for e in QUEUES}


def build_program(stop_after=None, ngrp=NGRP, noffn=False, dump=False):
    order = ["l0", "l1", "kv", "l2", "l3"]
    last = order.index(stop_after) if stop_after else len(order) - 1
    do = {s: (order.index(s) <= last) for s in order}

    nc = bass.Bass("TRN2", target_bir_lowering=False)
    S = Sched()
    marks = []

    def mark(label):
        marks.append((len(S.ins["pe"]), label))

    def dram(name, shape, kind="ExternalInput", dt=F32):
        return nc.dram_tensor(name, list(shape), dt, kind=kind).ap()

    x_d = dram("x", [2560, 1024])
    out_d = dram("out", [2048, 1024], kind="ExternalOutput")
    hv_d = dram("hv", [128, 1])
    gT_d = dram("gT", [128, 136])
    ng_d = dram("norm_g", [16, 1024])
    win_d = dram("a_w_in", [2, 1024, 2048])
    lng_d = dram("a_ln_g", [2, 1024])
    lnb_d = dram("a_ln_b", [2, 1024])
    wsT_d = dram("wsT", [2, 128, 1024])
    bs_d = dram("bs", [2, 1024])
    wout_d = dram("a_w_out", [2, 1024, 1024])
    wk_d = dram("w_k", [1024, 1024])
    wv_d = dram("w_v", [1024, 1024])
    wq_d = dram("b_w_q", [2, 1024, 1024])
    wo_d = dram("b_w_o", [2, 1024, 1024])
    rb_d = dram("rb", [2, 128, 8192])
    ch_d = dram("ch", [128, 32])
    wup_d = dram("w_up", [4, 1024, 4096])
    wdn_d = dram("w_down", [4, 4096, 1024])
    id_d = dram("ident", [128, 128])
    if dump:
        dbg = {
            "hnT": dram("dbg_hnT", [128, 4096], kind="ExternalOutput", dt=BF16),
            "uT": dram("dbg_uT", [128, 4096], kind="ExternalOutput", dt=BF16),
            "vF": dram("dbg_vF", [128, 4096], kind="ExternalOutput", dt=F32),
            "vn": dram("dbg_vn", [128, 4096], kind="ExternalOutput", dt=BF16),
            "gu": dram("dbg_gu", [128, 4096], kind="ExternalOutput", dt=BF16),
        }

    def sb(name, shape, dt):
        return nc.alloc_sbuf_tensor(name, list(shape), dt)

    hcur = sb("hcur", [128, 4, 1024], F32)
    KT = [sb(f"KT{i}", [128, 8, 512], BF16) for i in range(2)]
    VA = [sb(f"VA{i}", [128, 4, 16, 65], BF16) for i in range(2)]
    hnT = sb("hnT", [128, 8, 512], BF16)
    actT = sb("actT", [128, 8, 512], BF16)
    X = sb("X", [128, 16384], BF16)
    qT = [sb("qTA", [128, 8, 512], BF16), sb("qTB", [128, 8, 512], BF16)]
    hs = [sb(f"hs{i}", [128, 1024], BF16) for i in range(NHS)]
    pnt = [sb(f"pnt{i}", [128, 1024], F32) for i in range(2)]
    rtmp = [sb(f"rtmp{i}", [128, 512], BF16) for i in range(2)]
    PT = [sb(f"PT{i}", [128, 640], BF16) for i in range(3)]
    stmp = [sb(f"stmp{i}", [128, 512], F32) for i in range(2)]
    ob = [sb(f"ob{i}", [128, 1024], BF16) for i in range(2)]
    junk = ob[0]
    stat = sb("stat", [128, 256], F32)
    gT = sb("gTs", [128, 136], F32)
    gb = [sb(f"gb{i}", [128, 1024], F32) for i in range(2)]
    lnt = [sb(f"lnt{i}", [128, 1024], F32) for i in range(2)]
    wsT = sb("wsTs", [128, 2, 8, 128], BF16)
    bsrow = sb("bsrow", [1, 1024], BF16)
    ch = sb("chs", [128, 32], F32)
    ident = sb("identb", [128, 128], BF16)
    ones_row = sb("ones_row", [1, 128], BF16)
    hv = sb("hvs", [128, 1], F32)
    lnstat = [sb(f"lnstat{i}", [128, 12], F32) for i in range(2)]
    slots = [sb(f"slot{i}", [128, 4096], BF16) for i in range(RING)]

    hidT = X[:, :].rearrange("p (j t) -> p j t", j=32)
    uT = X[:, 0:4096].rearrange("p (c t) -> p c t", c=8)
    vF = X[:, 4096:12288].bitcast(F32).rearrange("p (t d) -> p t d", t=4)
    vn = X[:, 12288:16384].rearrange("p (t d) -> p t d", t=4)
    Rb = X[:, :].bitcast(F32).rearrange("p (h x) -> p h x", h=16)
    def XK(i):
        return ("X", i)

    RB_KEYS = [("X", i) for i in range(16)]

    PS = nc.alloc_psum_tensor("PS", [128, 4096], F32)
    pairs = [PS[:, i * 1024:(i + 1) * 1024] for i in range(4)]

    def bank(b):
        return pairs[b // 2][:, (b % 2) * 512:(b % 2) * 512 + 512]

    def bkey(b):
        return ("ps", b)

    psT = bank(7).bitcast(BF16).rearrange("p (c t) -> p c t", c=8)
    psT6 = bank(6).bitcast(BF16).rearrange("p (c t) -> p c t", c=8)
    psT0 = bank(0).bitcast(BF16).rearrange("p (c t) -> p c t", c=8)

    c_stage = [S.chan("stg0"), S.chan("stg1")]
    c_x = [S.chan(f"x{t}") for t in range(4)]
    c_out = [S.chan(f"o{t}") for t in range(4)]
    c_gb = [S.chan("gb0"), S.chan("gb1")]
    c_ln = [S.chan("ln0"), S.chan("ln1")]
    c_rb = S.chan("rb")
    c_bs = [S.chan("bs0"), S.chan("bs1")]
    c_dbg = S.chan("dbg")
    c_slot = [S.chan(f"slot{i}") for i in range(RING)]

    def panel(w2d, s):
        return ("panel", w2d.rearrange("(kc p) n -> p kc n", p=128)[:, :, s * 512:(s + 1) * 512])

    def dslab(w2d, s):
        return ("down", w2d.rearrange("(j p) n -> p j n", p=128)[:, 4 * s:4 * s + 4, :])

    plan = []
    for g in range(ngrp):
        for l in (0, 1):
            if not do[f"l{l}"]:
                continue
            plan += [panel(win_d[l], s) for s in (2, 3, 0, 1)]
            plan += [panel(wout_d[l], s) for s in range(2)]
            if not noffn:
                plan += [panel(wup_d[l], s) for s in range(8)]
                plan += [dslab(wdn_d[l], s) for s in range(8)]
        if do["kv"]:
            plan += [panel(wk_d, s) for s in range(2)]
            plan += [panel(wv_d, s) for s in range(2)]
        if g >= 1:
            for l in (2, 3):
                if not do[f"l{l}"]:
                    continue
                plan += [panel(wq_d[l - 2], s) for s in range(2)]
                plan += [panel(wo_d[l - 2], s) for s in range(2)]
                plan += [panel(wup_d[l], s) for s in range(8)]
                plan += [dslab(wdn_d[l], s) for s in range(8)]

    ring_state = dict(issued=0, taken=0)

    def ring_issue():
        n = ring_state["issued"]
        if n >= len(plan):
            return
        ring_state["issued"] = n + 1
        kind, src = plan[n]
        si = n % RING
        if kind == "panel":
            dst = slots[si][:, :].rearrange("p (a b) -> p a b", a=8)
        else:
            dst = slots[si][:, :].rearrange("p (a b) -> p a b", a=4)
        S.dma("pool", lambda e, dst=dst, src=src: e.dma_start(out=dst, in_=src), c_slot[si],
              writes=[("w", si)])

    def ring_next(kind):
        n = ring_state["taken"]
        ring_state["taken"] = n + 1
        assert plan[n][0] == kind, (n, plan[n][0], kind)
        si = n % RING
        if kind == "panel":
            v = slots[si][:, :].rearrange("p (a b) -> p a b", a=8)
        else:
            v = slots[si][:, :].rearrange("p (a b) -> p a b", a=4)
        return v, ("w", si)

    def ring_release():
        ring_issue()

    stat_i = [0]

    def newcols():
        i = stat_i[0] % 8
        stat_i[0] += 1
        return stat[:, i * 32:(i + 1) * 32], ("stat", i)

    def emit_rsqrt(xin, xin_key, pre_scale, eps, out, out_key, n=1):
        c, k = newcols()
        a, y, p, q = c[:, 0:n], c[:, 8:8 + n], c[:, 12:12 + n], c[:, 16:16 + n]
        S.op("dve", lambda e: e.tensor_scalar(out=a, in0=xin, scalar1=pre_scale, scalar2=eps,
                                              op0=ALU.mult, op1=ALU.add), reads=[xin_key], writes=[k])
        S.op("dve", lambda e: e.tensor_scalar(out=y.bitcast(I32), in0=a.bitcast(I32), scalar1=-0.5,
                                              scalar2=1597463007.0, op0=ALU.mult, op1=ALU.add),
             reads=[k], writes=[k])
        for it in range(2):
            S.op("dve", lambda e: e.tensor_tensor(out=p, in0=y, in1=y, op=ALU.mult), reads=[k], writes=[k])
            S.op("dve", lambda e: e.scalar_tensor_tensor(out=q, in0=p, scalar=-0.5, in1=a, op0=ALU.mult,
                                                         op1=ALU.mult), reads=[k], writes=[k])
            dst, dk = (y, k) if it == 0 else (out, out_key)
            S.op("dve", lambda e, dst=dst: e.scalar_tensor_tensor(out=dst, in0=q, scalar=1.5, in1=y, op0=ALU.add,
                                                                  op1=ALU.mult), reads=[k], writes=[dk])

    rot = dict(fm=0, tm=0, pair=0, hs=0, pnt=0, rt=0, lns=0, tb=0)

    def nxt(name, n):
        v = rot[name] % n
        rot[name] += 1
        return v

    def fm_bank():
        return 4 + nxt("fm", 3)

    def tm_bank():
        return nxt("tm", 4)

    def tm_pair():
        return nxt("pair", 2)

    def tile_cols(t):
        return slice(t * 128, (t + 1) * 128)

    def norm_part2(t, hidx, gcol, extra_gcol=None):
        tb = 7 - nxt("tb", 2)
        pT = psT if tb == 7 else psT6
        for fc in range(8):
            S.op("pe", lambda e, fc=fc: e.transpose(out=pT[:, fc, :], in_=hs[hidx][:, fc * 128:(fc + 1) * 128],
                                                    identity=ident[:, :]),
                 reads=[("hs", hidx), "ident"], writes=[bkey(tb)])
        gbc = gT[:, gcol:gcol + 8].unsqueeze(2).to_broadcast([128, 8, 128])
        S.op("dve", lambda e: e.tensor_tensor(out=hnT[:, :, tile_cols(t)], in0=pT, in1=gbc, op=ALU.mult),
             reads=[bkey(tb), "gT"], writes=[("hnT", t)])
        if extra_gcol is not None:
            gbc2 = gT[:, extra_gcol:extra_gcol + 8].unsqueeze(2).to_broadcast([128, 8, 128])
            S.op("dve", lambda e: e.tensor_tensor(out=actT[:, :, tile_cols(t)], in0=pT, in1=gbc2, op=ALU.mult),
                 reads=[bkey(tb), "gT"], writes=[("actT", t)])

    def norm_batch1(tl, gcol=None, extra_gcol=None):
        n = len(tl)
        c, k = newcols()
        for i, t in enumerate(tl):
            S.op("act", lambda e, i=i, t=t: e.activation(out=junk[:, :], in_=hcur[:, t, :], func=AF.Square,
                                                         accum_out=c[:, i:i + 1]),
                 reads=[("h", t)], writes=[("ob", 0), k])
        emit_rsqrt(c[:, 0:n], k, 1.0 / 1024, RMS_EPS, c[:, 4:4 + n], k, n)
        res = []
        for i, t in enumerate(tl):
            hidx = nxt("hs", NHS)
            S.op("act", lambda e, i=i, t=t, hidx=hidx: e.activation(out=hs[hidx][:, :], in_=hcur[:, t, :], func=AF.Copy,
                                                                    scale=c[:, 4 + i:5 + i]),
                 reads=[("h", t), k], writes=[("hs", hidx)])
            if gcol is not None and res and INTERLEAVE:
                norm_part2(res[-1][0], res[-1][1], gcol, extra_gcol)
            res.append((t, hidx))
        if gcol is not None:
            if INTERLEAVE:
                norm_part2(res[-1][0], res[-1][1], gcol, extra_gcol)
            else:
                for t, hidx in res:
                    norm_part2(t, hidx, gcol, extra_gcol)
        return res

    def norm_batch(tl, gcol, extra_gcol=None):
        norm_batch1(tl, gcol, extra_gcol)

    HN_ALL = [("hnT", t) for t in range(4)]
    ACT_ALL = [("actT", t) for t in range(4)]

    def postnorm_batch(tl, pis, gtile, gkey):
        n = len(tl)
        c, k = newcols()
        for i, (t, pi) in enumerate(zip(tl, pis)):
            S.op("act", lambda e, i=i, pi=pi: e.activation(out=junk[:, :], in_=pairs[pi][:, :], func=AF.Square,
                                                           accum_out=c[:, i:i + 1]),
                 reads=[bkey(2 * pi), bkey(2 * pi + 1)], writes=[("ob", 0), k])
        emit_rsqrt(c[:, 0:n], k, 1.0 / 1024, RMS_EPS, c[:, 4:4 + n], k, n)
        for i, (t, pi) in enumerate(zip(tl, pis)):
            pi2 = nxt("pnt", 2)
            S.op("dve", lambda e, i=i, pi=pi, pi2=pi2: e.scalar_tensor_tensor(
                out=pnt[pi2][:, :], in0=pairs[pi][:, :], scalar=c[:, 4 + i:5 + i], in1=gtile[:, :],
                op0=ALU.mult, op1=ALU.mult),
                reads=[bkey(2 * pi), bkey(2 * pi + 1), k, gkey], writes=[("pnt", pi2)])
            add_eng = "pool" if (POOL_ADD and i % 2 == 0) else "dve"
            S.op(add_eng, lambda e, t=t, pi2=pi2: e.tensor_tensor(out=hcur[:, t, :], in0=hcur[:, t, :], in1=pnt[pi2][:, :],
                                                                   op=ALU.add),
                 reads=[("h", t), ("pnt", pi2)], writes=[("h", t)])

    def postnorm_split(tl, pis, gtile, gkey):
        postnorm_batch(tl[0:2], pis[0:2], gtile, gkey)
        postnorm_batch(tl[2:4], pis[2:4], gtile, gkey)

    def load_bcast(dst, key, chan, row_ap):
        S.dma("sp", lambda e: e.dma_start(out=dst[:, :], in_=row_ap.partition_broadcast(128)), chan,
              writes=[key])

    def proj_tm_post(src_keys_fn, srcT, gtile, gkey, next_gcol):
        mark("proj")
        slabA, kA = ring_next("panel")
        slabB, kB = ring_next("panel")
        for t in range(4):
            for half, (slab, ks) in enumerate(((slabA, kA), (slabB, kB))):
                for kc in range(8):
                    S.op("pe", lambda e, kc=kc, slab=slab, half=half, t=t: e.matmul(
                        pairs[t][:, half * 512:(half + 1) * 512], lhsT=srcT[:, kc, tile_cols(t)],
                        rhs=slab[:, kc, :], start=(kc == 0), stop=(kc == 7)),
                        reads=[ks, src_keys_fn(t)], writes=[bkey(2 * t + half)])
        ring_release()
        ring_release()
        postnorm_split([0, 1, 2, 3], [0, 1, 2, 3], gtile, gkey)
        if next_gcol is not None:
            norm_batch([0, 1, 2, 3], next_gcol)

    def ffn(l, next_gcol, extra_gcol=None):
        if noffn:
            return
        mark("up")
        for s in range(8):
            slab, ks = ring_next("panel")
            for j4 in range(4):
                j = s * 4 + j4
                b = fm_bank()
                for kc in range(8):
                    S.op("pe", lambda e, kc=kc, slab=slab, j4=j4, b=b: e.matmul(
                        bank(b), lhsT=slab[:, kc, j4 * 128:(j4 + 1) * 128], rhs=hnT[:, kc, :],
                        start=(kc == 0), stop=(kc == 7)),
                        reads=[ks] + HN_ALL, writes=[bkey(b)])
                ri = nxt("rt", 2)
                S.op("act", lambda e, b=b, ri=ri: e.activation(out=rtmp[ri][:, :], in_=bank(b), func=AF.Relu),
                     reads=[bkey(b)], writes=[("rt", ri)])
                S.op("dve", lambda e, j=j, ri=ri: e.tensor_tensor(out=hidT[:, j, :], in0=rtmp[ri][:, :],
                                                                   in1=rtmp[ri][:, :], op=ALU.mult),
                     reads=[("rt", ri)], writes=[XK(j // 2)])
            ring_release()
        mark("down")
        dpair = [3, 2, 1, 0]
        for s in range(8):
            slab, ks = ring_next("down")
            for t in range(4):
                pi = dpair[t]
                for half in range(2):
                    for jj in range(4):
                        j = 4 * s + jj
                        S.op("pe", lambda e, slab=slab, jj=jj, j=j, half=half, pi=pi, t=t, s=s: e.matmul(
                            pairs[pi][:, half * 512:(half + 1) * 512], lhsT=hidT[:, j, tile_cols(t)],
                            rhs=slab[:, jj, half * 512:(half + 1) * 512],
                            start=(s == 0 and jj == 0), stop=(s == 7 and jj == 3)),
                            reads=[ks, XK(j // 2)], writes=[bkey(2 * pi + half)])
            ring_release()
        mark("post")
        postnorm_split([0, 1, 2, 3], dpair, gb[1], "gb1")
        if next_gcol is not None:
            norm_batch([0, 1, 2, 3], next_gcol, extra_gcol)

    def dump_buf(name, src_ap, keys):
        S.dma("sp", lambda e: e.dma_start(out=dbg[name][:, :], in_=src_ap), c_dbg, reads=keys)

    def a_layer(g, l, next_gcol, extra_gcol=None):
        dd = dump and g == 1 and l == 0
        mark(f"A{l} g{g} v")
        if dd:
            dump_buf("hnT", hnT[:, :, :].rearrange("p a b -> p (a b)"), HN_ALL)
        load_bcast(gb[0], "gb0", c_gb[0], ng_d[l * 4 + 1:l * 4 + 2, :])
        load_bcast(gb[1], "gb1", c_gb[1], ng_d[l * 4 + 3:l * 4 + 4, :])
        load_bcast(lnt[0], "ln0", c_ln[0], lng_d[l:l + 1, :])
        load_bcast(lnt[1], "ln1", c_ln[1], lnb_d[l:l + 1, :])
        for hf in range(2):
            S.dma("sp", lambda e, hf=hf: e.dma_start(out=stmp[hf][0:1, :], in_=bs_d[l:l + 1, hf * 512:(hf + 1) * 512]),
                  c_bs[hf], writes=[("stmp", hf)])
            S.op("act", lambda e, hf=hf: e.activation(out=bsrow[0:1, hf * 512:(hf + 1) * 512], in_=stmp[hf][0:1, :],
                                                      func=AF.Copy),
                 reads=[("stmp", hf)], writes=["bsrow"])
        for s in range(2):
            slab, ks = ring_next("panel")
            for t in range(4):
                b = tm_bank()
                for kc in range(8):
                    S.op("pe", lambda e, kc=kc, slab=slab, t=t, b=b: e.matmul(
                        bank(b), lhsT=hnT[:, kc, tile_cols(t)], rhs=slab[:, kc, :],
                        start=(kc == 0), stop=(kc == 7)),
                        reads=[ks, ("hnT", t)], writes=[bkey(b)])
                S.op("act", lambda e, b=b, t=t, s=s: e.activation(out=vF[:, t, s * 512:(s + 1) * 512], in_=bank(b),
                                                                func=AF.Gelu_apprx_tanh),
                     reads=[bkey(b)], writes=[XK(4 + 2 * t + s)])
            ring_release()
        c, k = newcols()
        mv = c[:, 0:8].rearrange("p (t two) -> p t two", two=2)
        for t in range(4):
            li = nxt("lns", 2)
            S.op("dve", lambda e, t=t, li=li: e.bn_stats(out=lnstat[li][:, 0:6], in_=vF[:, t, 0:512]),
                 reads=[XK(4 + 2 * t)], writes=[("lnstat", li)])
            S.op("dve", lambda e, t=t, li=li: e.bn_stats(out=lnstat[li][:, 6:12], in_=vF[:, t, 512:1024]),
                 reads=[XK(5 + 2 * t)], writes=[("lnstat", li)])
            S.op("dve", lambda e, t=t, li=li: e.bn_aggr(out=c[:, 2 * t:2 * t + 2], in_=lnstat[li][:, 0:12]),
                 reads=[("lnstat", li)], writes=[k])
        emit_rsqrt(mv[:, :, 1], k, 1.0, LN_EPS, c[:, 8:12], k, 4)
        S.op("dve", lambda e: e.scalar_tensor_tensor(out=c[:, 12:16], in0=mv[:, :, 0], scalar=-1.0, in1=c[:, 8:12],
                                                     op0=ALU.mult, op1=ALU.mult),
             reads=[k], writes=[k])
        def ln_part_b(t):
            pi2 = nxt("pnt", 2)
            S.op("act", lambda e, t=t, pi2=pi2: e.activation(out=pnt[pi2][:, :], in_=vF[:, t, :], func=AF.Identity,
                                                              scale=c[:, 8 + t:9 + t], bias=c[:, 12 + t:13 + t]),
                 reads=[XK(4 + 2 * t), XK(5 + 2 * t), k], writes=[("pnt", pi2)])
            leng = "dve" if t < 2 else "pool"
            S.op(leng, lambda e, pi2=pi2: e.tensor_tensor(out=pnt[pi2][:, :], in0=pnt[pi2][:, :], in1=lnt[0][:, :],
                                                           op=ALU.mult),
                 reads=[("pnt", pi2), "ln0"], writes=[("pnt", pi2)])
            S.op(leng, lambda e, pi2=pi2, t=t: e.tensor_tensor(out=vn[:, t, :], in0=pnt[pi2][:, :], in1=lnt[1][:, :],
                                                                op=ALU.add),
                 reads=[("pnt", pi2), "ln1"], writes=[XK(12 + t)])
        mark("u")
        for s in range(2):
            slab, ks = ring_next("panel")
            for u4 in range(4):
                uc = s * 4 + u4
                b = fm_bank()
                for kc in range(8):
                    S.op("pe", lambda e, kc=kc, slab=slab, u4=u4, b=b: e.matmul(
                        bank(b), lhsT=slab[:, kc, u4 * 128:(u4 + 1) * 128], rhs=hnT[:, kc, :],
                        start=(kc == 0), stop=(kc == 7)),
                        reads=[ks] + HN_ALL, writes=[bkey(b)])
                S.op("act", lambda e, b=b, uc=uc: e.activation(out=uT[:, uc, :], in_=bank(b), func=AF.Gelu_apprx_tanh),
                     reads=[bkey(b)], writes=[XK(uc // 2)])
                if s == 1:
                    ln_part_b(u4)
            ring_release()
        if dd:
            dump_buf("uT", X[:, 0:4096], [XK(i) for i in range(4)])
            dump_buf("vF", X[:, 4096:12288].bitcast(F32), [XK(i) for i in range(4, 12)])
        if dd:
            dump_buf("vn", X[:, 12288:16384], [XK(i) for i in range(12, 16)])
        mark("gate")
        slabA, kA = ring_next("panel")
        slabB, kB = ring_next("panel")
        ffn_gcol = (l * 4 + 2) * 8
        gpair = [3, 2, 3, 2]
        ppairs = [0, 1, 3, 2]

        def gate_tile(t):
            gp = gpair[t]
            for half in range(2):
                bk = 2 * gp + half
                S.op("pe", lambda e, half=half, bk=bk: e.matmul(
                    bank(bk), lhsT=ones_row[0:1, :], rhs=bsrow[0:1, half * 512:(half + 1) * 512],
                    start=True, stop=False),
                    reads=["ones", "bsrow"], writes=[bkey(bk)])
                for g4 in range(4):
                    gi = half * 4 + g4
                    S.op("pe", lambda e, gi=gi, g4=g4, bk=bk: e.matmul(
                        bank(bk)[:, g4 * 128:(g4 + 1) * 128], lhsT=vn[:, t, gi * 128:(gi + 1) * 128],
                        rhs=wsT[:, l, gi, :], start=False, stop=(g4 == 3)),
                        reads=[XK(12 + t), "wsT"], writes=[bkey(bk)])
            gview = pairs[gp].rearrange("p (g i) -> p g i", g=8)
            S.op("dve", lambda e: e.tensor_tensor(out=actT[:, :, tile_cols(t)], in0=gview,
                                                  in1=uT[:, :, tile_cols(t)], op=ALU.mult),
                 reads=[bkey(2 * gp), bkey(2 * gp + 1)] + [XK(i) for i in range(4)], writes=[("actT", t)])

        def proj_tile(t):
            pi = ppairs[t]
            for half, (slab, ks) in enumerate(((slabA, kA), (slabB, kB))):
                for kc in range(8):
                    S.op("pe", lambda e, kc=kc, slab=slab, half=half: e.matmul(
                        pairs[pi][:, half * 512:(half + 1) * 512], lhsT=actT[:, kc, tile_cols(t)],
                        rhs=slab[:, kc, :], start=(kc == 0), stop=(kc == 7)),
                        reads=[ks, ("actT", t)], writes=[bkey(2 * pi + half)])

        gate_tile(0)
        gate_tile(1)
        proj_tile(0)
        gate_tile(2)
        proj_tile(1)
        gate_tile(3)
        proj_tile(2)
        proj_tile(3)
        ring_release()
        ring_release()
        postnorm_split([0, 1, 2, 3], ppairs, gb[0], "gb0")
        norm_batch([0, 1, 2, 3], ffn_gcol)
        ffn(l, next_gcol, extra_gcol)

    def kv_stage(g):
        mark(f"KV g{g}")
        cur = g % 2
        for s in range(2):
            slab, ks = ring_next("panel")
            for f4 in range(4):
                fc = s * 4 + f4
                b = fm_bank()
                for kc in range(8):
                    S.op("pe", lambda e, kc=kc, slab=slab, f4=f4, b=b: e.matmul(
                        bank(b), lhsT=slab[:, kc, f4 * 128:(f4 + 1) * 128], rhs=hnT[:, kc, :],
                        start=(kc == 0), stop=(kc == 7)),
                        reads=[ks] + HN_ALL, writes=[bkey(b)])
                S.op("act", lambda e, b=b, fc=fc: e.activation(out=KT[cur][:, fc, :], in_=bank(b), func=AF.Copy),
                     reads=[bkey(b)], writes=[("KT", cur)])
            ring_release()
        for s in range(2):
            slab, ks = ring_next("panel")
            for t in range(4):
                b = tm_bank()
                for kc in range(8):
                    S.op("pe", lambda e, kc=kc, slab=slab, t=t, b=b: e.matmul(
                        bank(b), lhsT=hnT[:, kc, tile_cols(t)], rhs=slab[:, kc, :],
                        start=(kc == 0), stop=(kc == 7)),
                        reads=[ks, ("hnT", t)], writes=[bkey(b)])
                src = bank(b).rearrange("p (h d) -> p h d", h=8)
                dst = VA[cur][:, t, s * 8:(s + 1) * 8, 0:64]
                if g == 0:
                    S.op("dve", lambda e, src=src, dst=dst: e.tensor_scalar(out=dst, in0=src, scalar1=hv[:, 0:1],
                                                                            scalar2=None, op0=ALU.mult),
                         reads=[bkey(b), "hv"], writes=[("VA", cur, t)])
                else:
                    S.op("act", lambda e, src=src, dst=dst: e.activation(out=dst, in_=src, func=AF.Copy),
                         reads=[bkey(b)], writes=[("VA", cur, t)])
            ring_release()
        for t in range(4):
            dst = VA[cur][:, t, :, 64:65]
            if g == 0:
                S.op("dve", lambda e, dst=dst: e.tensor_copy(out=dst, in_=hv[:, 0:1].unsqueeze(1).to_broadcast([128, 16, 1])),
                     reads=["hv"], writes=[("VA", cur, t)])
            else:
                S.op("dve", lambda e, dst=dst: e.memset(dst, 1.0), writes=[("VA", cur, t)])

    def b_layer(g, l, next_gcol, qsrc=None):
        mark(f"B{l} g{g} q")
        j = l - 2
        prev, cur = (g - 1) % 2, g % 2
        load_bcast(gb[0], "gb0", c_gb[0], ng_d[l * 4 + 1:l * 4 + 2, :])
        load_bcast(gb[1], "gb1", c_gb[1], ng_d[l * 4 + 3:l * 4 + 4, :])
        S.dma("sp", lambda e: e.dma_start(out=X[:, :].bitcast(F32), in_=rb_d[j]), c_rb,
              writes=RB_KEYS)
        qs, qkeys = (actT, ACT_ALL) if qsrc == "actT" else (hnT, HN_ALL)
        for s in range(2):
            slab, ks = ring_next("panel")
            for f4 in range(4):
                fc = s * 4 + f4
                b = fm_bank()
                for kc in range(8):
                    S.op("pe", lambda e, kc=kc, slab=slab, f4=f4, b=b: e.matmul(
                        bank(b), lhsT=slab[:, kc, f4 * 128:(f4 + 1) * 128], rhs=qs[:, kc, :],
                        start=(kc == 0), stop=(kc == 7)),
                        reads=[ks] + qkeys, writes=[bkey(b)])
                S.op("act", lambda e, b=b, fc=fc: e.activation(out=qT[0][0:64, fc, :], in_=bank(b)[0:64, :],
                                                                func=AF.Copy, scale=0.125),
                     reads=[bkey(b)], writes=["qTA"])
                S.op("dve", lambda e, b=b, fc=fc: e.tensor_scalar(out=qT[1][64:128, fc, :], in0=bank(b)[64:128, :],
                                                                   scalar1=0.125, scalar2=None, op0=ALU.mult),
                     reads=[bkey(b)], writes=["qTB"])
            ring_release()

        mark("attn")
        items = [(ti, h) for ti in range(4) for h in range(16)]
        ssets = [(1, 2), (3, 4), (5, 6)]
        PCOL = {0: 0, 2: 1, 3: 2, 4: 3, 1: 4}
        PASSES = [(0, 7), (7, 7), (14, 2)]

        def obank(ti, ps_i):
            return (0, 7)[(ti + ps_i) % 2]

        def opass(h):
            return 0 if h < 7 else (1 if h < 14 else 2)

        def win(ti, kb):
            w = ti + kb
            return (prev if w < 4 else cur), w % 4

        def emit_S(n):
            ti, h = items[n]
            fc = h // 2
            sa, sbk = ssets[n % 3]
            q = qT[h % 2][:, fc, tile_cols(ti)]
            for kb in range(5):
                buf, wt = win(ti, kb)
                if kb == 1:
                    o = bank(sbk)[:, 0:128]
                    ok = bkey(sbk)
                else:
                    o = bank(sa)[:, PCOL[kb] * 128:(PCOL[kb] + 1) * 128]
                    ok = bkey(sa)
                S.op("pe", lambda e, o=o, buf=buf, fc=fc, wt=wt, q=q: e.matmul(
                    o, lhsT=KT[buf][:, fc, tile_cols(wt)], rhs=q, start=True, stop=True),
                    reads=[("KT", buf), "qTA" if h % 2 == 0 else "qTB"], writes=[ok])

        def emit_soft_pv(n):
            ti, h = items[n]
            sa, sbk = ssets[n % 3]
            p = n % 3
            si = n % 2
            S.op("dve", lambda e: e.tensor_tensor(out=stmp[si][:, :], in0=bank(sa), in1=Rb[:, h, :], op=ALU.add),
                 reads=[bkey(sa)] + RB_KEYS, writes=[("stmp", si)])
            S.op("act", lambda e: e.activation(out=PT[p][:, 512:640], in_=bank(sbk)[:, 0:128], func=AF.Exp,
                                               bias=ch[:, j * 16 + h:j * 16 + h + 1]),
                 reads=[bkey(sbk), "ch"], writes=[("PT", p)])
            S.op("act", lambda e: e.activation(out=PT[p][:, 0:512], in_=stmp[si][:, :], func=AF.Exp),
                 reads=[("stmp", si)], writes=[("PT", p)])
            h0, nh = PASSES[opass(h)]
            obk = obank(ti, opass(h))
            oo = (h - h0) * 65
            for kb in range(5):
                buf, wt = win(ti, kb)
                pc = PCOL[kb]
                S.op("pe", lambda e, kb=kb, buf=buf, wt=wt, pc=pc: e.matmul(
                    bank(obk)[:, oo:oo + 65], lhsT=PT[p][:, pc * 128:(pc + 1) * 128], rhs=VA[buf][:, wt, h, :],
                    start=(kb == 0), stop=(kb == 4)),
                    reads=[("PT", p), ("VA", buf, wt)], writes=[bkey(obk)])

        def emit_norm_pass(ti, oi, ps_i):
            h0, nh = PASSES[ps_i]
            obk = obank(ti, ps_i)
            c, k = newcols()
            Ob = bank(obk)[:, 0:nh * 65].rearrange("p (h d) -> p h d", h=nh)
            S.op("dve", lambda e: e.reciprocal(out=c[:, 0:nh].unsqueeze(2), in_=Ob[:, :, 64:65]),
                 reads=[bkey(obk)], writes=[k])
            dst = ob[oi][:, h0 * 64:(h0 + nh) * 64].rearrange("p (h d) -> p h d", h=nh)
            S.op("dve", lambda e: e.tensor_tensor(
                out=dst, in0=Ob[:, :, 0:64], in1=c[:, 0:nh].unsqueeze(2).to_broadcast([128, nh, 64]), op=ALU.mult),
                reads=[bkey(obk), k], writes=[("ob", oi)])

        def emit_oT(ti, oi):
            tb = 0 if ti % 2 == 0 else 7
            pT = psT0 if tb == 0 else psT
            for fc in range(8):
                S.op("pe", lambda e, fc=fc: e.transpose(out=pT[:, fc, :], in_=ob[oi][:, fc * 128:(fc + 1) * 128],
                                                        identity=ident[:, :]),
                     reads=[("ob", oi), "ident"], writes=[bkey(tb)])
            S.op("act", lambda e: e.activation(out=actT[:, :, tile_cols(ti)], in_=pT, func=AF.Copy),
                 reads=[bkey(tb)], writes=[("actT", ti)])

        emit_S(0)
        emit_S(1)
        pend_T = None
        pend_norm = []
        for n in range(len(items)):
            ti, h = items[n]
            if n + 2 < len(items):
                emit_S(n + 2)
            emit_soft_pv(n)
            while pend_norm and pend_norm[0][0] <= n:
                pend_norm.pop(0)[1]()
            if pend_T is not None and h == 3:
                pend_T()
                pend_T = None
            if h in (6, 13, 15):
                pend_norm.append((n + 2, lambda ti=ti, ps_i=opass(h): emit_norm_pass(ti, ti % 2, ps_i)))
            if h == 15:
                pend_T = (lambda ti=ti, oi=ti % 2: emit_oT(ti, oi))
        for _, f in pend_norm:
            f()
        pend_T()
        proj_tm_post(lambda t: ("actT", t), actT, gb[0], "gb0", (l * 4 + 2) * 8)
        ffn(l, next_gcol)

    S.dma("sp", lambda e: e.dma_start(out=gT[:, :], in_=gT_d[:, :]), S.chan("c_gT"), writes=["gT"])
    S.dma("sp", lambda e: e.dma_start(out=ch[:, :], in_=ch_d[:, :]), S.chan("c_ch"), writes=["ch"])
    S.dma("sp", lambda e: e.dma_start(out=hv[:, :], in_=hv_d[:, :]), S.chan("c_hv"), writes=["hv"])
    S.dma("sp", lambda e: e.dma_start(out=pnt[0][:, 0:128], in_=id_d[:, :]), c_stage[0], writes=[("pnt", 0)])
    for i in range(min(RING, len(plan))):
        ring_issue()
    S.op("act", lambda e: e.activation(out=ident[:, :], in_=pnt[0][:, 0:128], func=AF.Copy),
         reads=[("pnt", 0)], writes=["ident"])
    S.op("dve", lambda e: e.memset(ones_row[:, :], 1.0), writes=["ones"])
    S.op("dve", lambda e: e.memset(qT[0][:, :, :], 0.0), writes=["qTA"])
    S.op("dve", lambda e: e.memset(qT[1][:, :, :], 0.0), writes=["qTB"])
    for l in range(2):
        S.dma("sp", lambda e, l=l: e.dma_start(out=pnt[l][:, :], in_=wsT_d[l]), c_stage[l], writes=[("pnt", l)])
        stg = pnt[l][:, :].rearrange("p (g i) -> p g i", g=8)
        S.op("dve", lambda e, stg=stg: e.memset(stg[64:128, :, 0:64], 0.0), reads=[("pnt", l)], writes=[("pnt", l)])
        S.op("act", lambda e, stg=stg, l=l: e.activation(out=wsT[:, l, :, :], in_=stg, func=AF.Copy),
             reads=[("pnt", l)], writes=["wsT"])

    def first_stage_gcol(after):
        seq = [s for s in order if do[s]]
        return seq

    for g in range(ngrp):
        for t in range(4):
            S.dma("sp", lambda e, g=g, t=t: e.dma_start(out=hcur[:, t, :], in_=x_d[g * 512 + t * 128:g * 512 + (t + 1) * 128, :]),
                  c_x[t], writes=[("h", t)])
        stages = [s for s in order if do[s] and (g >= 1 or s in ("l0", "l1", "kv"))]
        gcol_of = {"l0": 0, "l1": 32, "kv": 128, "l2": 64, "l3": 96}
        norm_batch([0, 1, 2, 3], gcol_of[stages[0]])
        for si, st in enumerate(stages):
            nxt_st = stages[si + 1] if si + 1 < len(stages) else None
            ngc = gcol_of[nxt_st] if nxt_st else None
            nn_st = stages[si + 2] if si + 2 < len(stages) else None
            if st in ("l0", "l1"):
                a_layer(g, int(st[1]), ngc, gcol_of[nn_st] if (nxt_st == "kv" and nn_st) else None)
            elif st == "kv":
                kv_stage(g)
                if ngc is not None and si == 0:
                    norm_batch([0, 1, 2, 3], ngc)
            else:
                shared = si >= 2 and stages[si - 1] == "kv" and stages[si - 2] in ("l0", "l1")
                b_layer(g, int(st[1]), ngc, "actT" if shared else None)
        if g >= 1:
            for t in range(4):
                S.dma("sp", lambda e, g=g, t=t: e.dma_start(
                    out=out_d[(g - 1) * 512 + t * 128:(g - 1) * 512 + (t + 1) * 128, :], in_=hcur[:, t, :]),
                    c_out[t], reads=[("h", t)])
    S.wait_only("sp", writes=[("h", t) for t in range(4)])
    if dump:
        S.wait_only("sp", writes=HN_ALL + ACT_ALL + [XK(i) for i in range(16)])
    counts = S.emit(nc)
    counts["marks"] = marks
    return nc, counts


def prep_inputs(x, norm_g, a_w_in, a_ln_g, a_ln_b, a_w_s, a_b_s, a_w_out, kv_norm_g, w_k, w_v,
                b_w_q, b_rel_bias, b_w_o, w_up, w_down):
    f = lambda a: np.ascontiguousarray(np.asarray(a, dtype=np.float32))
    x = f(x)
    norm_g = f(norm_g)
    gT = np.empty((128, 136), np.float32)
    gT[:, 0:128] = norm_g.reshape(16, 8, 128).transpose(2, 0, 1).reshape(128, 128)
    gT[:, 128:136] = f(kv_norm_g).reshape(8, 128).T
    wsT = f(np.asarray(a_w_s).transpose(0, 3, 1, 2).reshape(2, 128, 1024))
    bs = f(np.asarray(a_b_s).reshape(2, 1024))
    tab = f(b_rel_bias)
    kp = np.arange(128)[:, None, None]
    kb = np.array([0, 2, 3, 4])[None, :, None]
    q = np.arange(128)[None, None, :]
    idx = np.clip(q + 512 - (128 * kb + kp), -256, 256) + 256
    rb = tab[:, :, idx]
    rb = np.ascontiguousarray(rb.transpose(0, 2, 1, 3, 4))
    rb[:, 64:128, :, 3, 0:64] = -30000.0
    rb[:, 0:64, :, 0, 64:128] = -30000.0
    rb = rb.reshape(2, 128, 8192)
    chc = np.ascontiguousarray(np.broadcast_to(tab[:, :, 512].reshape(1, 32), (128, 32)))
    shared = {
        "gT": gT, "norm_g": norm_g.reshape(16, 1024), "a_w_in": f(a_w_in), "a_ln_g": f(a_ln_g),
        "a_ln_b": f(a_ln_b), "wsT": wsT, "bs": bs, "a_w_out": f(a_w_out), "w_k": f(w_k), "w_v": f(w_v),
        "b_w_q": f(b_w_q), "b_w_o": f(b_w_o), "rb": rb, "ch": chc, "w_up": f(w_up), "w_down": f(w_down),
        "ident": np.eye(128, dtype=np.float32),
    }
    in_maps = []
    for c in range(8):
        b, half = c // 2, c % 2
        xc = np.zeros((2560, 1024), np.float32)
        if half == 0:
            xc[512:] = x[b, 0:2048]
        else:
            xc[:] = x[b, 1536:4096]
        m = dict(shared)
        m["x"] = xc
        m["hv"] = np.full((128, 1), float(half), np.float32)
        in_maps.append(m)
    return in_maps


_CACHE = {}


def run(inputs, stop_after=None, trace=False, noffn=False, dump=False, ngrp=NGRP):
    key = (stop_after, noffn, dump, ngrp)
    if key not in _CACHE:
        _CACHE[key] = build_program(stop_after, noffn=noffn, dump=dump, ngrp=ngrp)
    nc, counts = _CACHE[key]
    in_maps = prep_inputs(**inputs)
    res = run_bass_kernel_spmd(nc, in_maps, core_ids=list(range(8)), trace=trace)
    out = np.empty((4, 4096, 1024), np.float32)
    for c in range(8):
        b, half = c // 2, c % 2
        out[b, half * 2048:(half + 1) * 2048] = res.results[c]["out"]
    return out, res


def kernel(**inputs):
    out, _ = run(inputs)
    return out
```

```python
from contextlib import ExitStack

import os
import numpy as np
import concourse.bass as bass
import concourse.mybir as mybir
from concourse.bass_utils import run_bass_kernel_spmd

F32 = mybir.dt.float32
BF16 = mybir.dt.bfloat16
I32 = mybir.dt.int32
AF = mybir.ActivationFunctionType
ALU = mybir.AluOpType

COMPUTE = ("pe", "act", "dve", "pool")
QUEUES = ("pe", "act", "dve", "pool", "sp")
NGRP = 5
RING = int(os.environ.get('K_RING', '5'))
NHS = int(os.environ.get('K_NHS', '4'))
INTERLEAVE = os.environ.get('K_INTER', '0') == '1'
POOL_ADD = os.environ.get('K_POOLADD', '1') == '1'
BIAS_BCAST = os.environ.get('K_BIASB', '1') == '1'
RMS_EPS = 1e-6
LN_EPS = 1e-5


class Chan:
    def __init__(self, name):
        self.name = name
        self.count = 0
        self.sem = None


class Sched:
    def __init__(self):
        self.ins = {e: [] for e in QUEUES}
        self.last_w = {}
        self.readers = {}
        self.seen = {e: {} for e in QUEUES}
        self.chans = []
        self.waited = {e: set() for e in COMPUTE}

    def chan(self, name):
        c = Chan(name)
        self.chans.append(c)
        return c

    def _collect(self, eng, reads, writes):
        need = {}

        def add(tok, is_raw):
            if tok[0] == "eng":
                _, e, idx = tok
                if e == eng and eng in ("pe", "sp"):
                    return
                key = ("eng", e)
                if need.get(key, -1) < idx:
                    need[key] = idx
            else:
                _, ch, cnt = tok
                key = ("dma", ch)
                if need.get(key, -1) < cnt:
                    need[key] = cnt

        for k in reads:
            t = self.last_w.get(k)
            if t is not None:
                add(t, True)
        for k in writes:
            t = self.last_w.get(k)
            if t is not None:
                add(t, True)
            for r in self.readers.get(k, ()):
                add(r, False)
        deps = []
        seen = self.seen[eng]
        for key, val in need.items():
            if seen.get(key, -1) >= val:
                continue
            seen[key] = val
            deps.append((key, val))
            if key[0] == "eng":
                self.waited[key[1]].add(val)
        return deps

    def _record(self, tok, reads, writes):
        for k in writes:
            self.last_w[k] = tok
            self.readers[k] = []
        for k in reads:
            lst = self.readers.setdefault(k, [])
            src = tok[:2]
            lst[:] = [r for r in lst if r[:2] != src]
            lst.append(tok)

    def op(self, eng, fn, reads=(), writes=()):
        idx = len(self.ins[eng])
        deps = self._collect(eng, reads, writes)
        self.ins[eng].append(dict(fn=fn, deps=deps, chan=None))
        self._record(("eng", eng, idx), reads, writes)
        return idx

    def dma(self, queue, fn, chan, reads=(), writes=()):
        deps = self._collect(queue, reads, writes)
        chan.count += 16
        self.ins[queue].append(dict(fn=fn, deps=deps, chan=chan))
        self._record(("dma", chan, chan.count), reads, writes)

    def wait_only(self, queue, reads=(), writes=()):
        deps = self._collect(queue, reads, writes)
        self.ins[queue].append(dict(fn=None, deps=deps, chan=None))

    def emit(self, nc):
        engobj = {"pe": "tensor", "act": "scalar", "dve": "vector", "pool": "gpsimd", "sp": "sync"}
        with ExitStack() as es:
            esem = {e: es.enter_context(nc.semaphore("prog_" + e)) for e in COMPUTE}
            for c in self.chans:
                c.sem = es.enter_context(nc.semaphore("ch_" + c.name))
            rank = {}
            for e in COMPUTE:
                rank[e] = {idx: i + 1 for i, idx in enumerate(sorted(self.waited[e]))}
            block = es.enter_context(nc.Block())

            def run(eng, name):
                mine = rank.get(name, {})
                for idx, r in enumerate(self.ins[name]):
                    for key, val in r["deps"]:
                        if key[0] == "eng":
                            eng.wait_ge(esem[key[1]], rank[key[1]][val])
                        else:
                            eng.wait_ge(key[1].sem, val)
                    if r["fn"] is None:
                        continue
                    ins = r["fn"](eng)
                    if r["chan"] is not None:
                        ins.then_inc(r["chan"].sem, 16)
                    elif idx in mine:
                        ins.then_inc(esem[name], 1)

            for name in QUEUES:
                if self.ins[name]:
                    getattr(block, engobj[name])(lambda eng, name=name: run(eng, name))
        return {e: len(self.ins[e]) for e in QUEUES}


def build_program(stop_after=None, ngrp=NGRP, noffn=False, dump=False):
    order = ["l0", "l1", "kv", "l2", "l3"]
    last = order.index(stop_after) if stop_after else len(order) - 1
    do = {s: (order.index(s) <= last) for s in order}

    nc = bass.Bass("TRN2", target_bir_lowering=False)
    S = Sched()
    marks = []

    def mark(label):
        marks.append((len(S.ins["pe"]), label))

    def dram(name, shape, kind="ExternalInput", dt=F32):
        return nc.dram_tensor(name, list(shape), dt, kind=kind).ap()

    x_d = dram("x", [2560, 1024])
    out_d = dram("out", [2048, 1024], kind="ExternalOutput")
    hv_d = dram("hv", [128, 1])
    gT_d = dram("gT", [128, 136])
    ng_d = dram("norm_g", [16, 1024])
    win_d = dram("a_w_in", [2, 1024, 2048])
    lng_d = dram("a_ln_g", [2, 1024])
    lnb_d = dram("a_ln_b", [2, 1024])
    wsT_d = dram("wsT", [2, 128, 1024])
    bs_d = dram("bs", [2, 1024])
    wout_d = dram("a_w_out", [2, 1024, 1024])
    wk_d = dram("w_k", [1024, 1024])
    wv_d = dram("w_v", [1024, 1024])
    wq_d = dram("b_w_q", [2, 1024, 1024])
    wo_d = dram("b_w_o", [2, 1024, 1024])
    rb_d = dram("rb", [2, 128, 8192])
    ch_d = dram("ch", [128, 32])
    wup_d = dram("w_up", [4, 1024, 4096])
    wdn_d = dram("w_down", [4, 4096, 1024])
    id_d = dram("ident", [128, 128])
    if dump:
        dbg = {
            "hnT": dram("dbg_hnT", [128, 4096], kind="ExternalOutput", dt=BF16),
            "uT": dram("dbg_uT", [128, 4096], kind="ExternalOutput", dt=BF16),
            "vF": dram("dbg_vF", [128, 4096], kind="ExternalOutput", dt=F32),
            "vn": dram("dbg_vn", [128, 4096], kind="ExternalOutput", dt=BF16),
            "gu": dram("dbg_gu", [128, 4096], kind="ExternalOutput", dt=BF16),
        }

    def sb(name, shape, dt):
        return nc.alloc_sbuf_tensor(name, list(shape), dt)

    hcur = sb("hcur", [128, 4, 1024], F32)
    KT = [sb(f"KT{i}", [128, 8, 512], BF16) for i in range(2)]
    VA = [sb(f"VA{i}", [128, 4, 16, 65], BF16) for i in range(2)]
    hnT = sb("hnT", [128, 8, 512], BF16)
    actT = sb("actT", [128, 8, 512], BF16)
    X = sb("X", [128, 16384], BF16)
    qT = [sb("qTA", [128, 8, 512], BF16), sb("qTB", [128, 8, 512], BF16)]
    hs = [sb(f"hs{i}", [128, 1024], BF16) for i in range(NHS)]
    pnt = [sb(f"pnt{i}", [128, 1024], F32) for i in range(2)]
    rtmp = [sb(f"rtmp{i}", [128, 512], BF16) for i in range(2)]
    PT = [sb(f"PT{i}", [128, 640], BF16) for i in range(3)]
    stmp = [sb(f"stmp{i}", [128, 512], F32) for i in range(2)]
    ob = [sb(f"ob{i}", [128, 1024], BF16) for i in range(2)]
    junk = ob[0]
    stat = sb("stat", [128, 256], F32)
    gT = sb("gTs", [128, 136], F32)
    gb = [sb(f"gb{i}", [128, 1024], F32) for i in range(2)]
    lnt = [sb(f"lnt{i}", [128, 1024], F32) for i in range(2)]
    wsT = sb("wsTs", [128, 2, 8, 128], BF16)
    bsrow = sb("bsrow", [1, 1024], BF16)
    ch = sb("chs", [128, 32], F32)
    ident = sb("identb", [128, 128], BF16)
    ones_row = sb("ones_row", [1, 128], BF16)
    hv = sb("hvs", [128, 1], F32)
    lnstat = [sb(f"lnstat{i}", [128, 12], F32) for i in range(2)]
    slots = [sb(f"slot{i}", [128, 4096], BF16) for i in range(RING)]

    hidT = X[:, :].rearrange("p (j t) -> p j t", j=32)
    uT = X[:, 0:4096].rearrange("p (c t) -> p c t", c=8)
    vF = X[:, 4096:12288].bitcast(F32).rearrange("p (t d) -> p t d", t=4)
    vn = X[:, 12288:16384].rearrange("p (t d) -> p t d", t=4)
    Rb = X[:, :].bitcast(F32).rearrange("p (h x) -> p h x", h=16)
    def XK(i):
        return ("X", i)

    RB_KEYS = [("X", i) for i in range(16)]

    PS = nc.alloc_psum_tensor("PS", [128, 4096], F32)
    pairs = [PS[:, i * 1024:(i + 1) * 1024] for i in range(4)]

    def bank(b):
        return pairs[b // 2][:, (b % 2) * 512:(b % 2) * 512 + 512]

    def bkey(b):
        return ("ps", b)

    psT = bank(7).bitcast(BF16).rearrange("p (c t) -> p c t", c=8)
    psT6 = bank(6).bitcast(BF16).rearrange("p (c t) -> p c t", c=8)
    psT0 = bank(0).bitcast(BF16).rearrange("p (c t) -> p c t", c=8)

    c_stage = [S.chan("stg0"), S.chan("stg1")]
    c_x = [S.chan(f"x{t}") for t in range(4)]
    c_out = [S.chan(f"o{t}") for t in range(4)]
    c_gb = [S.chan("gb0"), S.chan("gb1")]
    c_ln = [S.chan("ln0"), S.chan("ln1")]
    c_rb = S.chan("rb")
    c_bs = [S.chan("bs0"), S.chan("bs1")]
    c_dbg = S.chan("dbg")
    c_slot = [S.chan(f"slot{i}") for i in range(RING)]

    def panel(w2d, s):
        return ("panel", w2d.rearrange("(kc p) n -> p kc n", p=128)[:, :, s * 512:(s + 1) * 512])

    def dslab(w2d, s):
        return ("down", w2d.rearrange("(j p) n -> p j n", p=128)[:, 4 * s:4 * s + 4, :])

    plan = []
    for g in range(ngrp):
        for l in (0, 1):
            if not do[f"l{l}"]:
                continue
            plan += [panel(win_d[l], s) for s in (2, 3, 0, 1)]
            plan += [panel(wout_d[l], s) for s in range(2)]
            if not noffn:
                plan += [panel(wup_d[l], s) for s in range(8)]
                plan += [dslab(wdn_d[l], s) for s in range(8)]
        if do["kv"]:
            plan += [panel(wk_d, s) for s in range(2)]
            plan += [panel(wv_d, s) for s in range(2)]
        if g >= 1:
            for l in (2, 3):
                if not do[f"l{l}"]:
                    continue
                plan += [panel(wq_d[l - 2], s) for s in range(2)]
                plan += [panel(wo_d[l - 2], s) for s in range(2)]
                plan += [panel(wup_d[l], s) for s in range(8)]
                plan += [dslab(wdn_d[l], s) for s in range(8)]

    ring_state = dict(issued=0, taken=0)

    def ring_issue():
        n = ring_state["issued"]
        if n >= len(plan):
            return
        ring_state["issued"] = n + 1
        kind, src = plan[n]
        si = n % RING
        if kind == "panel":
            dst = slots[si][:, :].rearrange("p (a b) -> p a b", a=8)
        else:
            dst = slots[si][:, :].rearrange("p (a b) -> p a b", a=4)
        S.dma("pool", lambda e, dst=dst, src=src: e.dma_start(out=dst, in_=src), c_slot[si],
              writes=[("w", si)])

    def ring_next(kind):
        n = ring_state["taken"]
        ring_state["taken"] = n + 1
        assert plan[n][0] == kind, (n, plan[n][0], kind)
        si = n % RING
        if kind == "panel":
            v = slots[si][:, :].rearrange("p (a b) -> p a b", a=8)
        else:
            v = slots[si][:, :].rearrange("p (a b) -> p a b", a=4)
        return v, ("w", si)

    def ring_release():
        ring_issue()

    stat_i = [0]

    def newcols():
        i = stat_i[0] % 8
        stat_i[0] += 1
        return stat[:, i * 32:(i + 1) * 32], ("stat", i)

    def emit_rsqrt(xin, xin_key, pre_scale, eps, out, out_key, n=1):
        c, k = newcols()
        a, y, p, q = c[:, 0:n], c[:, 8:8 + n], c[:, 12:12 + n], c[:, 16:16 + n]
        S.op("dve", lambda e: e.tensor_scalar(out=a, in0=xin, scalar1=pre_scale, scalar2=eps,
                                              op0=ALU.mult, op1=ALU.add), reads=[xin_key], writes=[k])
        S.op("dve", lambda e: e.tensor_scalar(out=y.bitcast(I32), in0=a.bitcast(I32), scalar1=-0.5,
                                              scalar2=1597463007.0, op0=ALU.mult, op1=ALU.add),
             reads=[k], writes=[k])
        for it in range(2):
            S.op("dve", lambda e: e.tensor_tensor(out=p, in0=y, in1=y, op=ALU.mult), reads=[k], writes=[k])
            S.op("dve", lambda e: e.scalar_tensor_tensor(out=q, in0=p, scalar=-0.5, in1=a, op0=ALU.mult,
                                                         op1=ALU.mult), reads=[k], writes=[k])
            dst, dk = (y, k) if it == 0 else (out, out_key)
            S.op("dve", lambda e, dst=dst: e.scalar_tensor_tensor(out=dst, in0=q, scalar=1.5, in1=y, op0=ALU.add,
                                                                  op1=ALU.mult), reads=[k], writes=[dk])

    rot = dict(fm=0, tm=0, pair=0, hs=0, pnt=0, rt=0, lns=0, tb=0)

    def nxt(name, n):
        v = rot[name] % n
        rot[name] += 1
        return v

    def fm_bank():
        return 4 + nxt("fm", 3)

    def tm_bank():
        return nxt("tm", 4)

    def tm_pair():
        return nxt("pair", 2)

    def tile_cols(t):
        return slice(t * 128, (t + 1) * 128)

    def norm_part2(t, hidx, gcol, extra_gcol=None):
        tb = 7 - nxt("tb", 2)
        pT = psT if tb == 7 else psT6
        for fc in range(8):
            S.op("pe", lambda e, fc=fc: e.transpose(out=pT[:, fc, :], in_=hs[hidx][:, fc * 128:(fc + 1) * 128],
                                                    identity=ident[:, :]),
                 reads=[("hs", hidx), "ident"], writes=[bkey(tb)])
        gbc = gT[:, gcol:gcol + 8].unsqueeze(2).to_broadcast([128, 8, 128])
        S.op("dve", lambda e: e.tensor_tensor(out=hnT[:, :, tile_cols(t)], in0=pT, in1=gbc, op=ALU.mult),
             reads=[bkey(tb), "gT"], writes=[("hnT", t)])
        if extra_gcol is not None:
            gbc2 = gT[:, extra_gcol:extra_gcol + 8].unsqueeze(2).to_broadcast([128, 8, 128])
            S.op("dve", lambda e: e.tensor_tensor(out=actT[:, :, tile_cols(t)], in0=pT, in1=gbc2, op=ALU.mult),
                 reads=[bkey(tb), "gT"], writes=[("actT", t)])

    def norm_batch1(tl, gcol=None, extra_gcol=None):
        n = len(tl)
        c, k = newcols()
        for i, t in enumerate(tl):
            S.op("act", lambda e, i=i, t=t: e.activation(out=junk[:, :], in_=hcur[:, t, :], func=AF.Square,
                                                         accum_out=c[:, i:i + 1]),
                 reads=[("h", t)], writes=[("ob", 0), k])
        emit_rsqrt(c[:, 0:n], k, 1.0 / 1024, RMS_EPS, c[:, 4:4 + n], k, n)
        res = []
        for i, t in enumerate(tl):
            hidx = nxt("hs", NHS)
            S.op("act", lambda e, i=i, t=t, hidx=hidx: e.activation(out=hs[hidx][:, :], in_=hcur[:, t, :], func=AF.Copy,
                                                                    scale=c[:, 4 + i:5 + i]),
                 reads=[("h", t), k], writes=[("hs", hidx)])
            if gcol is not None and res and INTERLEAVE:
                norm_part2(res[-1][0], res[-1][1], gcol, extra_gcol)
            res.append((t, hidx))
        if gcol is not None:
            if INTERLEAVE:
                norm_part2(res[-1][0], res[-1][1], gcol, extra_gcol)
            else:
                for t, hidx in res:
                    norm_part2(t, hidx, gcol, extra_gcol)
        return res

    def norm_batch(tl, gcol, extra_gcol=None):
        norm_batch1(tl, gcol, extra_gcol)

    HN_ALL = [("hnT", t) for t in range(4)]
    ACT_ALL = [("actT", t) for t in range(4)]

    def postnorm_batch(tl, pis, gtile, gkey):
        n = len(tl)
        c, k = newcols()
        for i, (t, pi) in enumerate(zip(tl, pis)):
            S.op("act", lambda e, i=i, pi=pi: e.activation(out=junk[:, :], in_=pairs[pi][:, :], func=AF.Square,
                                                           accum_out=c[:, i:i + 1]),
                 reads=[bkey(2 * pi), bkey(2 * pi + 1)], writes=[("ob", 0), k])
        emit_rsqrt(c[:, 0:n], k, 1.0 / 1024, RMS_EPS, c[:, 4:4 + n], k, n)
        for i, (t, pi) in enumerate(zip(tl, pis)):
            pi2 = nxt("pnt", 2)
            S.op("dve", lambda e, i=i, pi=pi, pi2=pi2: e.scalar_tensor_tensor(
                out=pnt[pi2][:, :], in0=pairs[pi][:, :], scalar=c[:, 4 + i:5 + i], in1=gtile[:, :],
                op0=ALU.mult, op1=ALU.mult),
                reads=[bkey(2 * pi), bkey(2 * pi + 1), k, gkey], writes=[("pnt", pi2)])
            add_eng = "pool" if (POOL_ADD and i % 2 == 0) else "dve"
            S.op(add_eng, lambda e, t=t, pi2=pi2: e.tensor_tensor(out=hcur[:, t, :], in0=hcur[:, t, :], in1=pnt[pi2][:, :],
                                                                   op=ALU.add),
                 reads=[("h", t), ("pnt", pi2)], writes=[("h", t)])

    def postnorm_split(tl, pis, gtile, gkey):
        postnorm_batch(tl[0:2], pis[0:2], gtile, gkey)
        postnorm_batch(tl[2:4], pis[2:4], gtile, gkey)

    def load_bcast(dst, key, chan, row_ap):
        S.dma("sp", lambda e: e.dma_start(out=dst[:, :], in_=row_ap.partition_broadcast(128)), chan,
              writes=[key])

    def proj_tm_post(src_keys_fn, srcT, gtile, gkey, next_gcol):
        mark("proj")
        slabA, kA = ring_next("panel")
        slabB, kB = ring_next("panel")
        for t in range(4):
            for half, (slab, ks) in enumerate(((slabA, kA), (slabB, kB))):
                for kc in range(8):
                    S.op("pe", lambda e, kc=kc, slab=slab, half=half, t=t: e.matmul(
                        pairs[t][:, half * 512:(half + 1) * 512], lhsT=srcT[:, kc, tile_cols(t)],
                        rhs=slab[:, kc, :], start=(kc == 0), stop=(kc == 7)),
                        reads=[ks, src_keys_fn(t)], writes=[bkey(2 * t + half)])
        ring_release()
        ring_release()
        postnorm_split([0, 1, 2, 3], [0, 1, 2, 3], gtile, gkey)
        if next_gcol is not None:
            norm_batch([0, 1, 2, 3], next_gcol)

    def ffn(l, next_gcol, extra_gcol=None):
        if noffn:
            return
        mark("up")
        for s in range(8):
            slab, ks = ring_next("panel")
            for j4 in range(4):
                j = s * 4 + j4
                b = fm_bank()
                for kc in range(8):
                    S.op("pe", lambda e, kc=kc, slab=slab, j4=j4, b=b: e.matmul(
                        bank(b), lhsT=slab[:, kc, j4 * 128:(j4 + 1) * 128], rhs=hnT[:, kc, :],
                        start=(kc == 0), stop=(kc == 7)),
                        reads=[ks] + HN_ALL, writes=[bkey(b)])
                ri = nxt("rt", 2)
                S.op("act", lambda e, b=b, ri=ri: e.activation(out=rtmp[ri][:, :], in_=bank(b), func=AF.Relu),
                     reads=[bkey(b)], writes=[("rt", ri)])
                S.op("dve", lambda e, j=j, ri=ri: e.tensor_tensor(out=hidT[:, j, :], in0=rtmp[ri][:, :],
                                                                   in1=rtmp[ri][:, :], op=ALU.mult),
                     reads=[("rt", ri)], writes=[XK(j // 2)])
            ring_release()
        mark("down")
        dpair = [3, 2, 1, 0]
        def down_mm(slab, ks, s, t):
            pi = dpair[t]
            for half in range(2):
                for jj in range(4):
                    j = 4 * s + jj
                    S.op("pe", lambda e, jj=jj, j=j, half=half: e.matmul(
                        pairs[pi][:, half * 512:(half + 1) * 512], lhsT=hidT[:, j, tile_cols(t)],
                        rhs=slab[:, jj, half * 512:(half + 1) * 512],
                        start=(s == 0 and jj == 0), stop=(s == 7 and jj == 3)),
                        reads=[ks, XK(j // 2)], writes=[bkey(2 * pi + half)])

        for s in range(6):
            slab, ks = ring_next("down")
            for t in range(4):
                down_mm(slab, ks, s, t)
            ring_release()
        slab6, k6 = ring_next("down")
        slab7, k7 = ring_next("down")
        for t in range(4):
            down_mm(slab6, k6, 6, t)
            down_mm(slab7, k7, 7, t)
        ring_release()
        ring_release()
        mark("post")
        postnorm_split([0, 1, 2, 3], dpair, gb[1], "gb1")
        if next_gcol is not None:
            norm_batch([0, 1, 2, 3], next_gcol, extra_gcol)

    def dump_buf(name, src_ap, keys):
        S.dma("sp", lambda e: e.dma_start(out=dbg[name][:, :], in_=src_ap), c_dbg, reads=keys)

    def a_layer(g, l, next_gcol, extra_gcol=None):
        dd = dump and g == 1 and l == 0
        mark(f"A{l} g{g} v")
        if dd:
            dump_buf("hnT", hnT[:, :, :].rearrange("p a b -> p (a b)"), HN_ALL)
        load_bcast(gb[0], "gb0", c_gb[0], ng_d[l * 4 + 1:l * 4 + 2, :])
        load_bcast(gb[1], "gb1", c_gb[1], ng_d[l * 4 + 3:l * 4 + 4, :])
        load_bcast(lnt[0], "ln0", c_ln[0], lng_d[l:l + 1, :])
        load_bcast(lnt[1], "ln1", c_ln[1], lnb_d[l:l + 1, :])
        for hf in range(2):
            S.dma("sp", lambda e, hf=hf: e.dma_start(out=stmp[hf][0:1, :], in_=bs_d[l:l + 1, hf * 512:(hf + 1) * 512]),
                  c_bs[hf], writes=[("stmp", hf)])
            S.op("act", lambda e, hf=hf: e.activation(out=bsrow[0:1, hf * 512:(hf + 1) * 512], in_=stmp[hf][0:1, :],
                                                      func=AF.Copy),
                 reads=[("stmp", hf)], writes=["bsrow"])
        for s in range(2):
            slab, ks = ring_next("panel")
            for t in range(4):
                b = tm_bank()
                for kc in range(8):
                    S.op("pe", lambda e, kc=kc, slab=slab, t=t, b=b: e.matmul(
                        bank(b), lhsT=hnT[:, kc, tile_cols(t)], rhs=slab[:, kc, :],
                        start=(kc == 0), stop=(kc == 7)),
                        reads=[ks, ("hnT", t)], writes=[bkey(b)])
                S.op("act", lambda e, b=b, t=t, s=s: e.activation(out=vF[:, t, s * 512:(s + 1) * 512], in_=bank(b),
                                                                func=AF.Gelu_apprx_tanh),
                     reads=[bkey(b)], writes=[XK(4 + 2 * t + s)])
            ring_release()
        c, k = newcols()
        mv = c[:, 0:8].rearrange("p (t two) -> p t two", two=2)
        for t in range(4):
            li = nxt("lns", 2)
            S.op("dve", lambda e, t=t, li=li: e.bn_stats(out=lnstat[li][:, 0:6], in_=vF[:, t, 0:512]),
                 reads=[XK(4 + 2 * t)], writes=[("lnstat", li)])
            S.op("dve", lambda e, t=t, li=li: e.bn_stats(out=lnstat[li][:, 6:12], in_=vF[:, t, 512:1024]),
                 reads=[XK(5 + 2 * t)], writes=[("lnstat", li)])
            S.op("dve", lambda e, t=t, li=li: e.bn_aggr(out=c[:, 2 * t:2 * t + 2], in_=lnstat[li][:, 0:12]),
                 reads=[("lnstat", li)], writes=[k])
        emit_rsqrt(mv[:, :, 1], k, 1.0, LN_EPS, c[:, 8:12], k, 4)
        S.op("dve", lambda e: e.scalar_tensor_tensor(out=c[:, 12:16], in0=mv[:, :, 0], scalar=-1.0, in1=c[:, 8:12],
                                                     op0=ALU.mult, op1=ALU.mult),
             reads=[k], writes=[k])
        def ln_part_b(t):
            pi2 = nxt("pnt", 2)
            S.op("act", lambda e, t=t, pi2=pi2: e.activation(out=pnt[pi2][:, :], in_=vF[:, t, :], func=AF.Identity,
                                                              scale=c[:, 8 + t:9 + t], bias=c[:, 12 + t:13 + t]),
                 reads=[XK(4 + 2 * t), XK(5 + 2 * t), k], writes=[("pnt", pi2)])
            leng = "dve" if t < 2 else "pool"
            S.op(leng, lambda e, pi2=pi2: e.tensor_tensor(out=pnt[pi2][:, :], in0=pnt[pi2][:, :], in1=lnt[0][:, :],
                                                           op=ALU.mult),
                 reads=[("pnt", pi2), "ln0"], writes=[("pnt", pi2)])
            S.op(leng, lambda e, pi2=pi2, t=t: e.tensor_tensor(out=vn[:, t, :], in0=pnt[pi2][:, :], in1=lnt[1][:, :],
                                                                op=ALU.add),
                 reads=[("pnt", pi2), "ln1"], writes=[XK(12 + t)])
        mark("u")
        for s in range(2):
            slab, ks = ring_next("panel")
            for u4 in range(4):
                uc = s * 4 + u4
                b = fm_bank()
                for kc in range(8):
                    S.op("pe", lambda e, kc=kc, slab=slab, u4=u4, b=b: e.matmul(
                        bank(b), lhsT=slab[:, kc, u4 * 128:(u4 + 1) * 128], rhs=hnT[:, kc, :],
                        start=(kc == 0), stop=(kc == 7)),
                        reads=[ks] + HN_ALL, writes=[bkey(b)])
                S.op("act", lambda e, b=b, uc=uc: e.activation(out=uT[:, uc, :], in_=bank(b), func=AF.Gelu_apprx_tanh),
                     reads=[bkey(b)], writes=[XK(uc // 2)])
                if s == 1:
                    ln_part_b(u4)
            ring_release()
        if dd:
            dump_buf("uT", X[:, 0:4096], [XK(i) for i in range(4)])
            dump_buf("vF", X[:, 4096:12288].bitcast(F32), [XK(i) for i in range(4, 12)])
        if dd:
            dump_buf("vn", X[:, 12288:16384], [XK(i) for i in range(12, 16)])
        mark("gate")
        slabA, kA = ring_next("panel")
        slabB, kB = ring_next("panel")
        ffn_gcol = (l * 4 + 2) * 8
        gpair = [3, 2, 3, 2]
        ppairs = [0, 1, 3, 2]

        def gate_tile(t):
            gp = gpair[t]
            for half in range(2):
                bk = 2 * gp + half
                S.op("pe", lambda e, half=half, bk=bk: e.matmul(
                    bank(bk), lhsT=ones_row[0:1, :], rhs=bsrow[0:1, half * 512:(half + 1) * 512],
                    start=True, stop=False),
                    reads=["ones", "bsrow"], writes=[bkey(bk)])
                for g4 in range(4):
                    gi = half * 4 + g4
                    S.op("pe", lambda e, gi=gi, g4=g4, bk=bk: e.matmul(
                        bank(bk)[:, g4 * 128:(g4 + 1) * 128], lhsT=vn[:, t, gi * 128:(gi + 1) * 128],
                        rhs=wsT[:, l, gi, :], start=False, stop=(g4 == 3)),
                        reads=[XK(12 + t), "wsT"], writes=[bkey(bk)])
            gview = pairs[gp].rearrange("p (g i) -> p g i", g=8)
            S.op("dve", lambda e: e.tensor_tensor(out=actT[:, :, tile_cols(t)], in0=gview,
                                                  in1=uT[:, :, tile_cols(t)], op=ALU.mult),
                 reads=[bkey(2 * gp), bkey(2 * gp + 1)] + [XK(i) for i in range(4)], writes=[("actT", t)])

        def proj_tile(t):
            pi = ppairs[t]
            for half, (slab, ks) in enumerate(((slabA, kA), (slabB, kB))):
                for kc in range(8):
                    S.op("pe", lambda e, kc=kc, slab=slab, half=half: e.matmul(
                        pairs[pi][:, half * 512:(half + 1) * 512], lhsT=actT[:, kc, tile_cols(t)],
                        rhs=slab[:, kc, :], start=(kc == 0), stop=(kc == 7)),
                        reads=[ks, ("actT", t)], writes=[bkey(2 * pi + half)])

        gate_tile(0)
        gate_tile(1)
        proj_tile(0)
        gate_tile(2)
        proj_tile(1)
        gate_tile(3)
        proj_tile(2)
        proj_tile(3)
        ring_release()
        ring_release()
        postnorm_split([0, 1, 2, 3], ppairs, gb[0], "gb0")
        norm_batch([0, 1, 2, 3], ffn_gcol)
        ffn(l, next_gcol, extra_gcol)

    def kv_stage(g):
        mark(f"KV g{g}")
        cur = g % 2
        for s in range(2):
            slab, ks = ring_next("panel")
            for f4 in range(4):
                fc = s * 4 + f4
                b = fm_bank()
                for kc in range(8):
                    S.op("pe", lambda e, kc=kc, slab=slab, f4=f4, b=b: e.matmul(
                        bank(b), lhsT=slab[:, kc, f4 * 128:(f4 + 1) * 128], rhs=hnT[:, kc, :],
                        start=(kc == 0), stop=(kc == 7)),
                        reads=[ks] + HN_ALL, writes=[bkey(b)])
                S.op("act", lambda e, b=b, fc=fc: e.activation(out=KT[cur][:, fc, :], in_=bank(b), func=AF.Copy),
                     reads=[bkey(b)], writes=[("KT", cur)])
            ring_release()
        for s in range(2):
            slab, ks = ring_next("panel")
            for t in range(4):
                b = tm_bank()
                for kc in range(8):
                    S.op("pe", lambda e, kc=kc, slab=slab, t=t, b=b: e.matmul(
                        bank(b), lhsT=hnT[:, kc, tile_cols(t)], rhs=slab[:, kc, :],
                        start=(kc == 0), stop=(kc == 7)),
                        reads=[ks, ("hnT", t)], writes=[bkey(b)])
                src = bank(b).rearrange("p (h d) -> p h d", h=8)
                dst = VA[cur][:, t, s * 8:(s + 1) * 8, 0:64]
                if g == 0:
                    S.op("dve", lambda e, src=src, dst=dst: e.tensor_scalar(out=dst, in0=src, scalar1=hv[:, 0:1],
                                                                            scalar2=None, op0=ALU.mult),
                         reads=[bkey(b), "hv"], writes=[("VA", cur, t)])
                else:
                    S.op("act", lambda e, src=src, dst=dst: e.activation(out=dst, in_=src, func=AF.Copy),
                         reads=[bkey(b)], writes=[("VA", cur, t)])
            ring_release()
        for t in range(4):
            dst = VA[cur][:, t, :, 64:65]
            if g == 0:
                S.op("dve", lambda e, dst=dst: e.tensor_copy(out=dst, in_=hv[:, 0:1].unsqueeze(1).to_broadcast([128, 16, 1])),
                     reads=["hv"], writes=[("VA", cur, t)])
            else:
                S.op("dve", lambda e, dst=dst: e.memset(dst, 1.0), writes=[("VA", cur, t)])

    def b_layer(g, l, next_gcol, qsrc=None):
        mark(f"B{l} g{g} q")
        j = l - 2
        prev, cur = (g - 1) % 2, g % 2
        load_bcast(gb[0], "gb0", c_gb[0], ng_d[l * 4 + 1:l * 4 + 2, :])
        load_bcast(gb[1], "gb1", c_gb[1], ng_d[l * 4 + 3:l * 4 + 4, :])
        S.dma("sp", lambda e: e.dma_start(out=X[:, :].bitcast(F32), in_=rb_d[j]), c_rb,
              writes=RB_KEYS)
        qs, qkeys = (actT, ACT_ALL) if qsrc == "actT" else (hnT, HN_ALL)
        for s in range(2):
            slab, ks = ring_next("panel")
            for f4 in range(4):
                fc = s * 4 + f4
                b = fm_bank()
                for kc in range(8):
                    S.op("pe", lambda e, kc=kc, slab=slab, f4=f4, b=b: e.matmul(
                        bank(b), lhsT=slab[:, kc, f4 * 128:(f4 + 1) * 128], rhs=qs[:, kc, :],
                        start=(kc == 0), stop=(kc == 7)),
                        reads=[ks] + qkeys, writes=[bkey(b)])
                S.op("act", lambda e, b=b, fc=fc: e.activation(out=qT[0][0:64, fc, :], in_=bank(b)[0:64, :],
                                                                func=AF.Copy, scale=0.125),
                     reads=[bkey(b)], writes=["qTA"])
                S.op("dve", lambda e, b=b, fc=fc: e.tensor_scalar(out=qT[1][64:128, fc, :], in0=bank(b)[64:128, :],
                                                                   scalar1=0.125, scalar2=None, op0=ALU.mult),
                     reads=[bkey(b)], writes=["qTB"])
            ring_release()

        mark("attn")
        items = [(ti, h) for ti in range(4) for h in range(16)]
        ssets = [(1, 2), (3, 4), (5, 6)]
        PCOL = {0: 0, 2: 1, 3: 2, 4: 3, 1: 4}
        PASSES = [(0, 7), (7, 7), (14, 2)]

        def obank(ti, ps_i):
            return (0, 7)[(ti + ps_i) % 2]

        def opass(h):
            return 0 if h < 7 else (1 if h < 14 else 2)

        def win(ti, kb):
            w = ti + kb
            return (prev if w < 4 else cur), w % 4

        def emit_S(n):
            ti, h = items[n]
            fc = h // 2
            sa, sbk = ssets[n % 3]
            q = qT[h % 2][:, fc, tile_cols(ti)]
            for kb in range(5):
                buf, wt = win(ti, kb)
                if kb == 1:
                    o = bank(sbk)[:, 0:128]
                    ok = bkey(sbk)
                else:
                    o = bank(sa)[:, PCOL[kb] * 128:(PCOL[kb] + 1) * 128]
                    ok = bkey(sa)
                S.op("pe", lambda e, o=o, buf=buf, fc=fc, wt=wt, q=q: e.matmul(
                    o, lhsT=KT[buf][:, fc, tile_cols(wt)], rhs=q, start=True, stop=True),
                    reads=[("KT", buf), "qTA" if h % 2 == 0 else "qTB"], writes=[ok])

        def emit_soft_pv(n):
            ti, h = items[n]
            sa, sbk = ssets[n % 3]
            p = n % 3
            si = n % 2
            S.op("dve", lambda e: e.tensor_tensor(out=stmp[si][:, :], in0=bank(sa), in1=Rb[:, h, :], op=ALU.add),
                 reads=[bkey(sa)] + RB_KEYS, writes=[("stmp", si)])
            S.op("act", lambda e: e.activation(out=PT[p][:, 512:640], in_=bank(sbk)[:, 0:128], func=AF.Exp,
                                               bias=ch[:, j * 16 + h:j * 16 + h + 1]),
                 reads=[bkey(sbk), "ch"], writes=[("PT", p)])
            S.op("act", lambda e: e.activation(out=PT[p][:, 0:512], in_=stmp[si][:, :], func=AF.Exp),
                 reads=[("stmp", si)], writes=[("PT", p)])
            h0, nh = PASSES[opass(h)]
            obk = obank(ti, opass(h))
            oo = (h - h0) * 65
            for kb in range(5):
                buf, wt = win(ti, kb)
                pc = PCOL[kb]
                S.op("pe", lambda e, kb=kb, buf=buf, wt=wt, pc=pc: e.matmul(
                    bank(obk)[:, oo:oo + 65], lhsT=PT[p][:, pc * 128:(pc + 1) * 128], rhs=VA[buf][:, wt, h, :],
                    start=(kb == 0), stop=(kb == 4)),
                    reads=[("PT", p), ("VA", buf, wt)], writes=[bkey(obk)])

        def emit_norm_pass(ti, oi, ps_i):
            h0, nh = PASSES[ps_i]
            obk = obank(ti, ps_i)
            c, k = newcols()
            Ob = bank(obk)[:, 0:nh * 65].rearrange("p (h d) -> p h d", h=nh)
            S.op("dve", lambda e: e.reciprocal(out=c[:, 0:nh].unsqueeze(2), in_=Ob[:, :, 64:65]),
                 reads=[bkey(obk)], writes=[k])
            dst = ob[oi][:, h0 * 64:(h0 + nh) * 64].rearrange("p (h d) -> p h d", h=nh)
            S.op("dve", lambda e: e.tensor_tensor(
                out=dst, in0=Ob[:, :, 0:64], in1=c[:, 0:nh].unsqueeze(2).to_broadcast([128, nh, 64]), op=ALU.mult),
                reads=[bkey(obk), k], writes=[("ob", oi)])

        def emit_oT(ti, oi):
            tb = 0 if ti % 2 == 0 else 7
            pT = psT0 if tb == 0 else psT
            for fc in range(8):
                S.op("pe", lambda e, fc=fc: e.transpose(out=pT[:, fc, :], in_=ob[oi][:, fc * 128:(fc + 1) * 128],
                                                        identity=ident[:, :]),
                     reads=[("ob", oi), "ident"], writes=[bkey(tb)])
            S.op("act", lambda e: e.activation(out=actT[:, :, tile_cols(ti)], in_=pT, func=AF.Copy),
                 reads=[bkey(tb)], writes=[("actT", ti)])

        emit_S(0)
        emit_S(1)
        pend_T = None
        pend_norm = []
        for n in range(len(items)):
            ti, h = items[n]
            if n + 2 < len(items):
                emit_S(n + 2)
            emit_soft_pv(n)
            while pend_norm and pend_norm[0][0] <= n:
                pend_norm.pop(0)[1]()
            if pend_T is not None and h == 3:
                pend_T()
                pend_T = None
            if h in (6, 13, 15):
                pend_norm.append((n + 2, lambda ti=ti, ps_i=opass(h): emit_norm_pass(ti, ti % 2, ps_i)))
            if h == 15:
                pend_T = (lambda ti=ti, oi=ti % 2: emit_oT(ti, oi))
        for _, f in pend_norm:
            f()
        pend_T()
        proj_tm_post(lambda t: ("actT", t), actT, gb[0], "gb0", (l * 4 + 2) * 8)
        ffn(l, next_gcol)

    S.dma("sp", lambda e: e.dma_start(out=gT[:, :], in_=gT_d[:, :]), S.chan("c_gT"), writes=["gT"])
    S.dma("sp", lambda e: e.dma_start(out=ch[:, :], in_=ch_d[:, :]), S.chan("c_ch"), writes=["ch"])
    S.dma("sp", lambda e: e.dma_start(out=hv[:, :], in_=hv_d[:, :]), S.chan("c_hv"), writes=["hv"])
    S.dma("sp", lambda e: e.dma_start(out=pnt[0][:, 0:128], in_=id_d[:, :]), c_stage[0], writes=[("pnt", 0)])
    for i in range(min(RING, len(plan))):
        ring_issue()
    S.op("act", lambda e: e.activation(out=ident[:, :], in_=pnt[0][:, 0:128], func=AF.Copy),
         reads=[("pnt", 0)], writes=["ident"])
    S.op("dve", lambda e: e.memset(ones_row[:, :], 1.0), writes=["ones"])
    S.op("dve", lambda e: e.memset(qT[0][:, :, :], 0.0), writes=["qTA"])
    S.op("dve", lambda e: e.memset(qT[1][:, :, :], 0.0), writes=["qTB"])
    for l in range(2):
        S.dma("sp", lambda e, l=l: e.dma_start(out=pnt[l][:, :], in_=wsT_d[l]), c_stage[l], writes=[("pnt", l)])
        stg = pnt[l][:, :].rearrange("p (g i) -> p g i", g=8)
        S.op("dve", lambda e, stg=stg: e.memset(stg[64:128, :, 0:64], 0.0), reads=[("pnt", l)], writes=[("pnt", l)])
        S.op("act", lambda e, stg=stg, l=l: e.activation(out=wsT[:, l, :, :], in_=stg, func=AF.Copy),
             reads=[("pnt", l)], writes=["wsT"])

    def first_stage_gcol(after):
        seq = [s for s in order if do[s]]
        return seq

    for g in range(ngrp):
        for t in range(4):
            S.dma("sp", lambda e, g=g, t=t: e.dma_start(out=hcur[:, t, :], in_=x_d[g * 512 + t * 128:g * 512 + (t + 1) * 128, :]),
                  c_x[t], writes=[("h", t)])
        stages = [s for s in order if do[s] and (g >= 1 or s in ("l0", "l1", "kv"))]
        gcol_of = {"l0": 0, "l1": 32, "kv": 128, "l2": 64, "l3": 96}
        norm_batch([0, 1, 2, 3], gcol_of[stages[0]])
        for si, st in enumerate(stages):
            nxt_st = stages[si + 1] if si + 1 < len(stages) else None
            ngc = gcol_of[nxt_st] if nxt_st else None
            nn_st = stages[si + 2] if si + 2 < len(stages) else None
            if st in ("l0", "l1"):
                a_layer(g, int(st[1]), ngc, gcol_of[nn_st] if (nxt_st == "kv" and nn_st) else None)
            elif st == "kv":
                kv_stage(g)
                if ngc is not None and si == 0:
                    norm_batch([0, 1, 2, 3], ngc)
            else:
                shared = si >= 2 and stages[si - 1] == "kv" and stages[si - 2] in ("l0", "l1")
                b_layer(g, int(st[1]), ngc, "actT" if shared else None)
        if g >= 1:
            for t in range(4):
                S.dma("sp", lambda e, g=g, t=t: e.dma_start(
                    out=out_d[(g - 1) * 512 + t * 128:(g - 1) * 512 + (t + 1) * 128, :], in_=hcur[:, t, :]),
                    c_out[t], reads=[("h", t)])
    S.wait_only("sp", writes=[("h", t) for t in range(4)])
    if dump:
        S.wait_only("sp", writes=HN_ALL + ACT_ALL + [XK(i) for i in range(16)])
    counts = S.emit(nc)
    counts["marks"] = marks
    return nc, counts


def prep_inputs(x, norm_g, a_w_in, a_ln_g, a_ln_b, a_w_s, a_b_s, a_w_out, kv_norm_g, w_k, w_v,
                b_w_q, b_rel_bias, b_w_o, w_up, w_down):
    f = lambda a: np.ascontiguousarray(np.asarray(a, dtype=np.float32))
    x = f(x)
    norm_g = f(norm_g)
    gT = np.empty((128, 136), np.float32)
    gT[:, 0:128] = norm_g.reshape(16, 8, 128).transpose(2, 0, 1).reshape(128, 128)
    gT[:, 128:136] = f(kv_norm_g).reshape(8, 128).T
    wsT = f(np.asarray(a_w_s).transpose(0, 3, 1, 2).reshape(2, 128, 1024))
    bs = f(np.asarray(a_b_s).reshape(2, 1024))
    tab = f(b_rel_bias)
    kp = np.arange(128)[:, None, None]
    kb = np.array([0, 2, 3, 4])[None, :, None]
    q = np.arange(128)[None, None, :]
    idx = np.clip(q + 512 - (128 * kb + kp), -256, 256) + 256
    rb = tab[:, :, idx]
    rb = np.ascontiguousarray(rb.transpose(0, 2, 1, 3, 4))
    rb[:, 64:128, :, 3, 0:64] = -30000.0
    rb[:, 0:64, :, 0, 64:128] = -30000.0
    rb = rb.reshape(2, 128, 8192)
    chc = np.ascontiguousarray(np.broadcast_to(tab[:, :, 512].reshape(1, 32), (128, 32)))
    shared = {
        "gT": gT, "norm_g": norm_g.reshape(16, 1024), "a_w_in": f(a_w_in), "a_ln_g": f(a_ln_g),
        "a_ln_b": f(a_ln_b), "wsT": wsT, "bs": bs, "a_w_out": f(a_w_out), "w_k": f(w_k), "w_v": f(w_v),
        "b_w_q": f(b_w_q), "b_w_o": f(b_w_o), "rb": rb, "ch": chc, "w_up": f(w_up), "w_down": f(w_down),
        "ident": np.eye(128, dtype=np.float32),
    }
    in_maps = []
    for c in range(8):
        b, half = c // 2, c % 2
        xc = np.zeros((2560, 1024), np.float32)
        if half == 0:
            xc[512:] = x[b, 0:2048]
        else:
            xc[:] = x[b, 1536:4096]
        m = dict(shared)
        m["x"] = xc
        m["hv"] = np.full((128, 1), float(half), np.float32)
        in_maps.append(m)
    return in_maps


_CACHE = {}


def run(inputs, stop_after=None, trace=False, noffn=False, dump=False, ngrp=NGRP):
    key = (stop_after, noffn, dump, ngrp)
    if key not in _CACHE:
        _CACHE[key] = build_program(stop_after, noffn=noffn, dump=dump, ngrp=ngrp)
    nc, counts = _CACHE[key]
    in_maps = prep_inputs(**inputs)
    res = run_bass_kernel_spmd(nc, in_maps, core_ids=list(range(8)), trace=trace)
    out = np.empty((4, 4096, 1024), np.float32)
    for c in range(8):
        b, half = c // 2, c % 2
        out[b, half * 2048:(half + 1) * 2048] = res.results[c]["out"]
    return out, res


def kernel(**inputs):
    out, _ = run(inputs)
    return out
```
